# Optimizing a Trainium2 kernel written in Bass

```python
import math
import jax, jax.numpy as jnp
from jax import lax
import numpy as np

D_MODEL = 4096
BATCH = 4
SEQ = 4096
DEPTH = 1

GRID_W = 64
SSM_EXPAND = 2
SSM_D_INNER = SSM_EXPAND * D_MODEL
SSM_HEAD_DIM = 64
SSM_HEADS = SSM_D_INNER // SSM_HEAD_DIM
SSM_GROUPS = 8
SSM_STATE = 128
SSM_CONV_W = 3
SSM_CHUNK = 128
SSM_CONV_DIM = SSM_D_INNER + 2 * SSM_GROUPS * SSM_STATE
ATTN_HEAD_DIM = 128
ATTN_Q_HEADS = D_MODEL // ATTN_HEAD_DIM
ATTN_KV_HEADS = 8
ATTN_BLOCK_Q = 128
ROPE_THETA = 10000.0
FFN_DIM = 11008
FFN_CONV_W = 3
LN_EPS = 1e-5
RMS_EPS = 1e-6
DEEPNORM_ALPHA = (2.0 * DEPTH) ** 0.25
DEEPNORM_BETA = (8.0 * DEPTH) ** -0.25
ADA_INIT = 0.5

Z_COLS = SSM_D_INNER
XBC_COLS = SSM_CONV_DIM
DT_COLS = 2 * SSM_HEADS
Q_COLS = ATTN_Q_HEADS * ATTN_HEAD_DIM
KV_COLS = ATTN_KV_HEADS * ATTN_HEAD_DIM
IN_COLS = Z_COLS + XBC_COLS + DT_COLS + Q_COLS + 2 * KV_COLS
IN_SPLITS = [Z_COLS, Z_COLS + XBC_COLS, Z_COLS + XBC_COLS + DT_COLS,
             Z_COLS + XBC_COLS + DT_COLS + Q_COLS,
             Z_COLS + XBC_COLS + DT_COLS + Q_COLS + KV_COLS]

kernel_name = "hybrid_ssd_gqa_convffn_deepnorm_adaln"


def layer_norm(x, g=None, b=None):
    xf = x.astype(jnp.float32)
    mu = jnp.mean(xf, axis=-1, keepdims=True)
    var = jnp.mean(jnp.square(xf - mu), axis=-1, keepdims=True)
    y = (xf - mu) * lax.rsqrt(var + LN_EPS)
    if g is not None:
        y = y * g.astype(jnp.float32) + b.astype(jnp.float32)
    return y.astype(x.dtype)


def rms_norm(x, g):
    xf = x.astype(jnp.float32)
    y = xf * lax.rsqrt(jnp.mean(jnp.square(xf), axis=-1, keepdims=True) + RMS_EPS)
    return (y * g.astype(jnp.float32)).astype(x.dtype)


def dwconv_centered(x, w, b):
    K = w.shape[0]
    pad = K // 2
    S = x.shape[1]
    xp = jnp.pad(x, ((0, 0), (pad, pad), (0, 0)))
    y = xp[:, 0:S, :] * w[0]
    for j in range(1, K):
        y = y + xp[:, j:j + S, :] * w[j]
    return y + b


def axial_rope(x, row, col):
    half = x.shape[-1] // 2
    quarter = half // 2
    inv = 1.0 / (ROPE_THETA ** (jnp.arange(quarter, dtype=jnp.float32) / quarter))

    def rot(xh, pos):
        ang = pos.astype(jnp.float32)[:, None] * inv[None, :]
        cos = jnp.cos(ang)[None, :, None, :]
        sin = jnp.sin(ang)[None, :, None, :]
        x1 = xh[..., :quarter].astype(jnp.float32)
        x2 = xh[..., quarter:].astype(jnp.float32)
        return jnp.concatenate([x1 * cos - x2 * sin, x2 * cos + x1 * sin], axis=-1)

    out = jnp.concatenate([rot(x[..., :half], row), rot(x[..., half:], col)], axis=-1)
    return out.astype(x.dtype)


def block_attention(q, k, v):
    Bsz, S, Hq, Dh = q.shape
    Hkv = k.shape[2]
    G = Hq // Hkv
    nb = S // ATTN_BLOCK_Q
    qb = q.reshape(Bsz, nb, ATTN_BLOCK_Q, Hkv, G, Dh).transpose(1, 0, 2, 3, 4, 5)
    scale = Dh ** -0.5

    def one_block(qi):
        s = jnp.einsum('bqkgd,bskd->bkgqs', qi, k).astype(jnp.float32) * scale
        p = jax.nn.softmax(s, axis=-1).astype(v.dtype)
        return jnp.einsum('bkgqs,bskd->bqkgd', p, v)

    o = lax.map(one_block, qb)
    return o.transpose(1, 0, 2, 3, 4, 5).reshape(Bsz, S, Hq * Dh)


def ssd_scan(x, dt, A, Bm, Cm):
    Bsz, S, H, P = x.shape
    G, N = Bm.shape[2], Bm.shape[3]
    Hg = H // G
    L = SSM_CHUNK
    nc = S // L

    def chunk(t):
        return jnp.moveaxis(t.reshape((Bsz, nc, L) + t.shape[2:]), 1, 0)

    xc = chunk(x.reshape(Bsz, S, G, Hg, P))
    ac = chunk((dt * A).reshape(Bsz, S, G, Hg))
    dtc = chunk(dt.reshape(Bsz, S, G, Hg))
    Bc = chunk(Bm)
    Cc = chunk(Cm)
    mask = jnp.tril(jnp.ones((L, L), dtype=bool))[None, :, :, None, None]

    def step(h, inp):
        xi, ai, dti, Bi, Ci = inp
        cum = jnp.cumsum(ai, axis=1)
        seg = cum[:, :, None] - cum[:, None, :]
        decay = jnp.exp(jnp.where(mask, seg, -jnp.inf))
        cb = jnp.einsum('blgn,bsgn->blsg', Ci, Bi)
        y_intra = jnp.einsum('blsg,blsgh,bsgh,bsghp->blghp', cb, decay, dti, xi)
        y_inter = jnp.einsum('blgn,bghpn,blgh->blghp', Ci, h, jnp.exp(cum))
        to_end = jnp.exp(cum[:, -1:] - cum) * dti
        h_new = h * jnp.exp(cum[:, -1])[..., None, None] + \
            jnp.einsum('bsgn,bsgh,bsghp->bghpn', Bi, to_end, xi)
        return h_new, y_intra + y_inter

    h0 = jnp.zeros((Bsz, G, Hg, P, N), jnp.float32)
    _, y = lax.scan(step, h0, (xc, ac, dtc, Bc, Cc))
    return jnp.moveaxis(y, 0, 1).reshape(Bsz, S, H, P)


def ssd_branch(z, xbc, dt_raw, conv_w, conv_b, a_log, dt_bias, d_skip, norm_w):
    Bsz, S, _ = z.shape
    f32 = jnp.float32
    xbc = jax.nn.silu(dwconv_centered(xbc, conv_w, conv_b)).astype(f32)
    xs = xbc[..., :SSM_D_INNER].reshape(Bsz, S, SSM_HEADS, SSM_HEAD_DIM)
    Bm = xbc[..., SSM_D_INNER:SSM_D_INNER + SSM_GROUPS * SSM_STATE].reshape(Bsz, S, SSM_GROUPS, SSM_STATE)
    Cm = xbc[..., SSM_D_INNER + SSM_GROUPS * SSM_STATE:].reshape(Bsz, S, SSM_GROUPS, SSM_STATE)
    dt = jax.nn.softplus(dt_raw.astype(f32).reshape(Bsz, S, 2, SSM_HEADS) + dt_bias.astype(f32))
    A = -jnp.exp(a_log.astype(f32))
    flip = lambda t: jnp.flip(t, axis=1)
    y_f = ssd_scan(xs, dt[:, :, 0], A[0], Bm, Cm)
    y_b = flip(ssd_scan(flip(xs), flip(dt[:, :, 1]), A[1], flip(Bm), flip(Cm)))
    y = (y_f + y_b + d_skip.astype(f32)[:, None] * xs).reshape(Bsz, S, SSM_D_INNER)
    yg = (y * jax.nn.silu(z.astype(f32))).reshape(Bsz, S, SSM_GROUPS, SSM_D_INNER // SSM_GROUPS)
    yg = yg * lax.rsqrt(jnp.mean(jnp.square(yg), axis=-1, keepdims=True) + RMS_EPS)
    return (yg.reshape(Bsz, S, SSM_D_INNER) * norm_w.astype(f32)).astype(z.dtype)


def setup_inputs(seed: int = 0) -> dict:
    key = jax.random.key(seed)
    ks = jax.random.split(key, 26)
    f32 = jnp.float32
    nrm = lambda k, shape, s: jax.random.normal(k, shape, f32) * s
    dt0 = jnp.exp(jax.random.uniform(ks[7], (DEPTH, 2, SSM_HEADS), f32, math.log(1e-3), math.log(1e-1)))
    return {
        "x": nrm(ks[0], (BATCH, SEQ, D_MODEL), 1.0),
        "c": nrm(ks[1], (BATCH, D_MODEL), 1.0),
        "w_ada": nrm(ks[2], (DEPTH, D_MODEL, 6 * D_MODEL), ADA_INIT * D_MODEL ** -0.5),
        "b_ada": nrm(ks[3], (DEPTH, 6 * D_MODEL), 0.01),
        "w_in": nrm(ks[4], (DEPTH, D_MODEL, IN_COLS), D_MODEL ** -0.5),
        "ssm_conv_w": nrm(ks[5], (DEPTH, SSM_CONV_W, SSM_CONV_DIM), SSM_CONV_W ** -0.5),
        "ssm_conv_b": nrm(ks[6], (DEPTH, SSM_CONV_DIM), 0.01),
        "ssm_a_log": jnp.log(jax.random.uniform(ks[8], (DEPTH, 2, SSM_HEADS), f32, 1.0, 16.0)),
        "ssm_dt_bias": dt0 + jnp.log(-jnp.expm1(-dt0)),
        "ssm_d": 1.0 + nrm(ks[9], (DEPTH, SSM_HEADS), 0.1),
        "ssm_norm_w": 1.0 + nrm(ks[10], (DEPTH, SSM_D_INNER), 0.1),
        "q_norm_w": 1.0 + nrm(ks[11], (DEPTH, ATTN_HEAD_DIM), 0.1),
        "k_norm_w": 1.0 + nrm(ks[12], (DEPTH, ATTN_HEAD_DIM), 0.1),
        "w_ssm_proj": nrm(ks[13], (DEPTH, SSM_D_INNER, D_MODEL), SSM_D_INNER ** -0.5),
        "w_attn_proj": nrm(ks[14], (DEPTH, Q_COLS, D_MODEL), Q_COLS ** -0.5),
        "w_gate": nrm(ks[15], (DEPTH, D_MODEL, 2 * D_MODEL), D_MODEL ** -0.5),
        "b_gate": nrm(ks[16], (DEPTH, 2 * D_MODEL), 0.01),
        "w_out": nrm(ks[17], (DEPTH, D_MODEL, D_MODEL), DEEPNORM_BETA * D_MODEL ** -0.5),
        "ln1_g": 1.0 + nrm(ks[18], (DEPTH, D_MODEL), 0.1),
        "ln1_b": nrm(ks[19], (DEPTH, D_MODEL), 0.01),
        "w_up": nrm(ks[20], (DEPTH, D_MODEL, 2 * FFN_DIM), D_MODEL ** -0.5),
        "ffn_conv_w": nrm(ks[21], (DEPTH, FFN_CONV_W, 2 * FFN_DIM), FFN_CONV_W ** -0.5),
        "ffn_conv_b": nrm(ks[22], (DEPTH, 2 * FFN_DIM), 0.01),
        "w_down": nrm(ks[23], (DEPTH, FFN_DIM, D_MODEL), DEEPNORM_BETA * FFN_DIM ** -0.5),
        "ln2_g": 1.0 + nrm(ks[24], (DEPTH, D_MODEL), 0.1),
        "ln2_b": nrm(ks[25], (DEPTH, D_MODEL), 0.01),
    }


def reference(x, c, w_ada, b_ada, w_in, ssm_conv_w, ssm_conv_b, ssm_a_log, ssm_dt_bias,
              ssm_d, ssm_norm_w, q_norm_w, k_norm_w, w_ssm_proj, w_attn_proj, w_gate, b_gate,
              w_out, ln1_g, ln1_b, w_up, ffn_conv_w, ffn_conv_b, w_down, ln2_g, ln2_b):
    Bsz, S, D = x.shape
    rows = S // GRID_W
    row = jnp.repeat(jnp.arange(rows, dtype=jnp.int32), GRID_W)
    col = jnp.tile(jnp.arange(GRID_W, dtype=jnp.int32), rows)
    cond = jax.nn.silu(c)

    for layer in range(DEPTH):
        mod = (cond @ w_ada[layer] + b_ada[layer])[:, None, :]
        sh1, sc1, g1, sh2, sc2, g2 = jnp.split(mod, 6, axis=-1)

        h = layer_norm(x) * (1.0 + sc1) + sh1
        proj = h @ w_in[layer]
        z, xbc, dt_raw, q, k, v = jnp.split(proj, IN_SPLITS, axis=-1)

        y_ssm = ssd_branch(z, xbc, dt_raw, ssm_conv_w[layer], ssm_conv_b[layer],
                           ssm_a_log[layer], ssm_dt_bias[layer], ssm_d[layer],
                           ssm_norm_w[layer])

        q = rms_norm(q.reshape(Bsz, S, ATTN_Q_HEADS, ATTN_HEAD_DIM), q_norm_w[layer])
        k = rms_norm(k.reshape(Bsz, S, ATTN_KV_HEADS, ATTN_HEAD_DIM), k_norm_w[layer])
        v = v.reshape(Bsz, S, ATTN_KV_HEADS, ATTN_HEAD_DIM)
        q = axial_rope(q, row, col)
        k = axial_rope(k, row, col)
        y_attn = block_attention(q, k, v)

        o_ssm = y_ssm @ w_ssm_proj[layer]
        o_attn = y_attn @ w_attn_proj[layer]
        gates = jax.nn.sigmoid(h @ w_gate[layer] + b_gate[layer])
        ga, gb = jnp.split(gates, 2, axis=-1)
        mixed = (ga * o_ssm + gb * o_attn) @ w_out[layer]
        x = layer_norm(DEEPNORM_ALPHA * x + g1 * mixed, ln1_g[layer], ln1_b[layer])

        h2 = layer_norm(x) * (1.0 + sc2) + sh2
        u = dwconv_centered(h2 @ w_up[layer], ffn_conv_w[layer], ffn_conv_b[layer])
        ua, ub = jnp.split(u, 2, axis=-1)
        f = (jax.nn.silu(ua) * ub) @ w_down[layer]
        x = layer_norm(DEEPNORM_ALPHA * x + g2 * f, ln2_g[layer], ln2_b[layer])

    return x
```

```python
import math
from contextlib import ExitStack
import numpy as np
import concourse.bass as bass
import concourse.mybir as mybir
from concourse.bass_utils import run_bass_kernel_spmd

F32 = mybir.dt.float32
BF16 = mybir.dt.bfloat16
ALU = mybir.AluOpType
AF = mybir.ActivationFunctionType

D = 4096
S = 4096
NOWN = 2176
NOUT = 2048
KT = 32
DI = 8192
NCONV = 10240
FFN = 11008
IN_COLS = 24832
Z0, XBC0, DT0, Q0, K0, V0 = 0, 8192, 18432, 18688, 22784, 23808
ALPHA = 2.0 ** 0.25
LN_EPS = 1e-5
RMS_EPS = 1e-6


class Buf:
    __slots__ = ("w", "r")

    def __init__(self):
        self.w = None
        self.r = {}


class Sched:
    def __init__(self, nc, es, ndma=20):
        self.nc = nc
        self.engs = {}
        for name, h in (("pe", nc.tensor), ("act", nc.scalar), ("dve", nc.vector),
                        ("pool", nc.gpsimd), ("sp", nc.sync)):
            sem = es.enter_context(nc.semaphore("s_" + name))
            self.engs[name] = {"name": name, "h": h, "sem": sem, "cnt": 0, "seen": {}}
        self.dsems = [[es.enter_context(nc.semaphore("d%d" % i)), 0] for i in range(ndma)]
        self.di = 0

    def _wait(self, e, ev):
        sem, val, src = ev
        if src == "pe" and e["name"] == "pe":
            return
        k = id(sem)
        if e["seen"].get(k, 0) >= val:
            return
        e["h"].wait_ge(sem, val)
        e["seen"][k] = val

    def _deps(self, e, reads, writes):
        for b in reads:
            if b.w is not None:
                self._wait(e, b.w)
        for b in writes:
            if b.w is not None:
                self._wait(e, b.w)
            for ev in b.r.values():
                self._wait(e, ev)

    def _record(self, ev, reads, writes):
        for b in reads:
            b.r[ev[2]] = ev
        for b in writes:
            b.w = ev
            b.r = {}

    def op(self, en, fn, reads=(), writes=(), sig=True):
        e = self.engs[en]
        self._deps(e, reads, writes)
        ins = fn(e["h"])
        if sig:
            e["cnt"] += 1
            ins.then_inc(e["sem"], 1)
            ev = (e["sem"], e["cnt"], en)
        else:
            ev = (e["sem"], e["cnt"] + 1, en)
        self._record(ev, reads, writes)

    def dma(self, en, out, in_, reads=(), writes=(), **kw):
        e = self.engs[en]
        self._deps(e, reads, writes)
        slot = self.dsems[self.di]
        idx = self.di
        self.di = (self.di + 1) % len(self.dsems)
        if slot[1] > 0:
            self._wait(e, (slot[0], slot[1], "dq%d" % idx))
        ins = e["h"].dma_start(out=out, in_=in_, **kw)
        slot[1] += 16
        ins.then_inc(slot[0], 16)
        self._record((slot[0], slot[1], "dq%d" % idx), reads, writes)

    def barrier(self):
        for e in self.engs.values():
            for o in self.engs.values():
                if o is not e and o["cnt"] > 0:
                    self._wait(e, (o["sem"], o["cnt"], o["name"] + "_b"))
            for i, sl in enumerate(self.dsems):
                if sl[1] > 0:
                    self._wait(e, (sl[0], sl[1], "dq%d" % i))


def tok_blocks(n0, n, bs=512):
    out = []
    t = n0
    while t < n0 + n:
        m = min(bs, n0 + n - t)
        out.append((t, m))
        t += m
    return out


class K:
    def __init__(self, nc, dev_out):
        self.nc = nc
        self.dev_out = set(dev_out)
        self.flip = 0

    def dram(self, name, shape, dt, inp=False, out=False):
        if inp:
            return self.nc.dram_tensor(name, shape, dt, kind="ExternalInput").ap()
        if out or name in self.dev_out:
            return self.nc.dram_tensor(name, shape, dt, kind="ExternalOutput").ap()
        return self.nc.dram_tensor(name, shape, dt).ap()


def build_nc(dev_out=(), stop_after=99):
    nc = bass.Bass("TRN2", target_bir_lowering=False)
    kb = K(nc, dev_out)
    dram = kb.dram
    x_d = dram("x", [S, D], F32, inp=True)
    c_d = dram("c", [128, KT], F32, inp=True)
    w_ada = dram("w_ada", [D, 6 * D], F32, inp=True)
    b_ada = dram("b_ada", [1, 6 * D], F32, inp=True)
    w_in = dram("w_in", [D, IN_COLS], F32, inp=True)
    scw = dram("ssm_conv_w", [128, 80, 3], F32, inp=True)
    scb = dram("ssm_conv_b", [128, 80], F32, inp=True)
    alog = dram("ssm_a_log", [128, 2], F32, inp=True)
    dtb = dram("ssm_dt_bias", [128, 2], F32, inp=True)
    ssmd = dram("ssm_d", [1, 128], F32, inp=True)
    snw = dram("ssm_norm_w", [128, 64], F32, inp=True)
    qnw = dram("q_norm_w", [128, 1], F32, inp=True)
    knw = dram("k_norm_w", [128, 1], F32, inp=True)
    w_sp = dram("w_ssm_proj", [DI, D], F32, inp=True)
    w_ap = dram("w_attn_proj", [D, D], F32, inp=True)
    w_gate = dram("w_gate", [D, 2 * D], F32, inp=True)
    b_gate = dram("b_gate", [128, 64], F32, inp=True)
    w_out = dram("w_out", [D, D], F32, inp=True)
    ln1g = dram("ln1_g", [1, D], F32, inp=True)
    ln1b = dram("ln1_b", [1, D], F32, inp=True)
    w_up = dram("w_up", [D, 2 * FFN], F32, inp=True)
    fcw = dram("ffn_conv_w", [128, 172, 3], F32, inp=True)
    fcb = dram("ffn_conv_b", [128, 172], F32, inp=True)
    w_down = dram("w_down", [FFN, D], F32, inp=True)
    ln2g = dram("ln2_g", [1, D], F32, inp=True)
    ln2b = dram("ln2_b", [1, D], F32, inp=True)
    cos_d = dram("rope_cos", [128, S], F32, inp=True)
    sin_d = dram("rope_sin", [128, S], F32, inp=True)
    cst = dram("consts", [128, 8, 128], F32, inp=True)
    out_d = dram("out", [NOUT, D], F32, out=True)
    modD = dram("modD", [1, 6 * D], F32)
    hT = dram("hT", [D, S], BF16)
    zsT = dram("zsT", [DI, NOWN], F32)
    xbcpre = dram("xbcpre", [NCONV, S], F32)
    dtraw = dram("dtraw", [256, S], F32)
    qT = dram("qT", [D, NOWN], F32)
    kT = dram("kT", [1024, S], F32)
    vT = dram("vT", [1024, S], BF16)
    gatesT = dram("gatesT", [2 * D, NOWN], F32)
    xs_tok = dram("xs_tok", [S, DI], BF16)
    B_tok = dram("B_tok", [S, 1024], BF16)
    BTd = dram("BTd", [1024, S], BF16)
    CTd = dram("CTd", [1024, S], BF16)
    dt_tok = dram("dt_tok", [S, 256], F32)
    a_tok = dram("a_tok", [S, 256], F32)
    ysum = dram("ysum", [NOWN, DI], F32)
    yssmT = dram("yssmT", [DI, NOWN], BF16)
    qTn = dram("qTn", [D, NOWN], BF16)
    kTn = dram("kTn", [1024, S], BF16)
    v_tok = dram("v_tok", [S, 1024], BF16)
    yattnT = dram("yattnT", [D, NOWN], BF16)
    t1T = dram("t1T", [D, NOWN], F32)
    mixT = dram("mixT", [D, NOWN], BF16)
    mixed = dram("mixed", [NOWN, D], F32)
    x1d = dram("x1d", [NOWN, D], F32)
    h2T = dram("h2T", [D, NOWN], BF16)
    upre = dram("upre", [2 * FFN, NOWN], F32)
    actT = dram("actT", [FFN, NOUT], BF16)
    fd = dram("fd", [NOUT, D], F32)

    with ExitStack() as es0:
        S_ = Sched(nc, es0)
        uid = [0]

        def sb(es, name, shape, dt):
            uid[0] += 1
            return es.enter_context(nc.sbuf_tensor("%s_%d" % (name, uid[0]), shape, dt))

        def ps(es, name, shape, dt):
            uid[0] += 1
            return es.enter_context(nc.psum_tensor("%s_%d" % (name, uid[0]), shape, dt))
        C32 = sb(es0, "C32", [128, 8, 128], F32)
        C16 = sb(es0, "C16", [128, 8, 128], BF16)
        modP = sb(es0, "modP", [128, 192], F32)
        bC32, bC16, bmodP = Buf(), Buf(), Buf()
        S_.dma("sp", C32[:], cst[:, :, :], writes=[bC32])
        S_.dma("pool", C16[:], cst[:, :, :], writes=[bC16])
        IDf, ONEf, TRIf, SUf, TRIb, SUb, ROPf = [C32[:, i, :] for i in range(7)]
        IDb, ONEb = C16[:, 0, :], C16[:, 1, :]
        TRI = (TRIf, TRIb)
        SU = (SUf, SUb)

        with ExitStack() as es:
            ct = sb(es, "ct", [128, KT], F32)
            condT = sb(es, "condT", [128, KT], BF16)
            wb = [sb(es, "wada%d" % i, [128, KT, 512], BF16) for i in range(2)]
            brow = [sb(es, "brow%d" % i, [1, 512], F32) for i in range(2)]
            mrow = [sb(es, "mrow%d" % i, [1, 512], F32) for i in range(2)]
            pm = [ps(es, "pm%d" % i, [1, 512], F32) for i in range(2)]
            pcol = ps(es, "pcol", [128, 192], F32)
            b_ct, b_cond, b_pcol = Buf(), Buf(), Buf()
            b_brow, b_mrow = [Buf(), Buf()], [Buf(), Buf()]
            b_wb = [Buf(), Buf()]
            b_pm = [Buf(), Buf()]
            S_.dma("sp", ct[:], c_d[:, :], writes=[b_ct])
            S_.op("act", lambda h: h.activation(out=condT[:], in_=ct[:], func=AF.Silu), reads=[b_ct], writes=[b_cond])
            wv = w_ada.rearrange("(k p) c -> p k c", p=128)
            for cb in range(48):
                i = cb % 2
                S_.dma("pool", wb[i][:], wv[:, :, cb * 512:(cb + 1) * 512], writes=[b_wb[i]])
                S_.dma("sp", brow[i][:], b_ada[:, cb * 512:(cb + 1) * 512], writes=[b_brow[i]])
                for kt in range(KT):
                    S_.op("pe", lambda h, kt=kt, i=i: h.matmul(pm[i][:], lhsT=condT[:, kt:kt + 1], rhs=wb[i][:, kt, :],
                                                                 start=(kt == 0), stop=(kt == KT - 1)),
                          reads=[b_cond, b_wb[i]], writes=[b_pm[i]], sig=(kt == KT - 1))
                S_.op("dve", lambda h, i=i: h.tensor_tensor(out=mrow[i][:], in0=pm[i][:], in1=brow[i][:], op=ALU.add),
                      reads=[b_pm[i], b_brow[i]], writes=[b_mrow[i]])
                S_.dma("sp", modD[:, cb * 512:(cb + 1) * 512], mrow[i][:], reads=[b_mrow[i]])
                for q in range(4):
                    j = cb * 4 + q
                    S_.op("pe", lambda h, j=j, q=q, i=i: h.matmul(pcol[:, j:j + 1], lhsT=mrow[i][0:1, q * 128:(q + 1) * 128],
                                                                    rhs=ONEf[0:1, 0:1], start=True, stop=True),
                          reads=[b_mrow[i], bC32], writes=[b_pcol], sig=(q == 3))
            S_.op("dve", lambda h: h.tensor_copy(out=modP[:], in_=pcol[:]), reads=[b_pcol], writes=[bmodP])
            S_.op("dve", lambda h: h.tensor_scalar(out=modP[:, 32:64], in0=modP[:, 32:64], scalar1=1.0, scalar2=None, op0=ALU.add),
                  reads=[bmodP], writes=[bmodP])
            S_.op("dve", lambda h: h.tensor_scalar(out=modP[:, 128:160], in0=modP[:, 128:160], scalar1=1.0, scalar2=None, op0=ALU.add),
                  reads=[bmodP], writes=[bmodP])
            S_.barrier()
        if stop_after <= 0:
            return finish(nc, S_, out_d)

        def ln_stats(es_tiles, xt, b_xt, st, mv, rstd, b_st):
            for j in range(8):
                S_.op("dve", lambda h, j=j: h.bn_stats(out=st[:, j, :], in_=xt[:, j * 512:(j + 1) * 512]),
                      reads=[b_xt], writes=[b_st])
            S_.op("dve", lambda h: h.bn_aggr(out=mv[:], in_=st[:].rearrange("p a b -> p (a b)")), reads=[b_st], writes=[b_st])
            S_.op("dve", lambda h: h.tensor_scalar(out=rstd[:], in0=mv[:, 1:2], scalar1=LN_EPS, scalar2=None, op0=ALU.add),
                  reads=[b_st], writes=[b_st])
            S_.op("act", lambda h: h.activation(out=rstd[:], in_=rstd[:], func=AF.Sqrt), reads=[b_st], writes=[b_st])
            S_.op("dve", lambda h: h.reciprocal(out=rstd[:], in_=rstd[:]), reads=[b_st], writes=[b_st])

        def modT(xn, b_xn, hblk, b_hblk, sub, pT, b_pT, sc0, sh0):
            for g in range(8):
                i = g % 2
                for q in range(4):
                    kt = g * 4 + q
                    S_.op("pe", lambda h, kt=kt, q=q, i=i: h.transpose(out=pT[i][:, q * 128:(q + 1) * 128],
                                                                       in_=xn[:, kt * 128:(kt + 1) * 128], identity=IDf),
                          reads=[b_xn, bC32], writes=[b_pT[i]], sig=(q == 3))
                for q in range(4):
                    kt = g * 4 + q
                    S_.op("act", lambda h, kt=kt, q=q, i=i: h.activation(
                        out=hblk[:, kt, sub * 128:(sub + 1) * 128], in_=pT[i][:, q * 128:(q + 1) * 128], func=AF.Identity,
                        scale=modP[:, sc0 + kt:sc0 + kt + 1], bias=modP[:, sh0 + kt:sh0 + kt + 1]),
                        reads=[b_pT[i], bmodP], writes=[b_hblk])

        with ExitStack() as es:
            xt = [sb(es, "xt%d" % i, [128, D], F32) for i in range(2)]
            xn = [sb(es, "xn%d" % i, [128, D], F32) for i in range(2)]
            st = sb(es, "st", [128, 8, 6], F32)
            mv = sb(es, "mv", [128, 2], F32)
            rstd = sb(es, "rstd", [128, 1], F32)
            hblk = [sb(es, "hblk%d" % i, [128, KT, 512], BF16) for i in range(2)]
            pT = [ps(es, "pT%d" % i, [128, 512], F32) for i in range(2)]
            b_xt, b_xn, b_hblk, b_pT = [Buf(), Buf()], [Buf(), Buf()], [Buf(), Buf()], [Buf(), Buf()]
            b_st = Buf()
            hTv = hT.rearrange("(k p) t -> p k t", p=128)
            for blk in range(8):
                hb = blk % 2
                for sub in range(4):
                    tt = blk * 4 + sub
                    i = tt % 2
                    S_.dma("sp", xt[i][:], x_d[tt * 128:(tt + 1) * 128, :], writes=[b_xt[i]])
                    ln_stats(es, xt[i], b_xt[i], st, mv, rstd, b_st)
                    S_.op("dve", lambda h, i=i: h.tensor_scalar(out=xn[i][:], in0=xt[i][:], scalar1=mv[:, 0:1], scalar2=rstd[:, 0:1],
                                                                 op0=ALU.subtract, op1=ALU.mult),
                          reads=[b_xt[i], b_st], writes=[b_xn[i]])
                    modT(xn[i], b_xn[i], hblk[hb], b_hblk[hb], sub, pT, b_pT, 32, 0)
                S_.dma("sp", hTv[:, :, blk * 512:(blk + 1) * 512], hblk[hb][:], reads=[b_hblk[hb]])
            S_.barrier()
        if stop_after <= 1:
            return finish(nc, S_, out_d)

        def mm_A(name, act_d, kt_n, tok0, ntok, W_d, col0, ncols, mk_evac, cw=256):
            with ExitStack() as es:
                evac = mk_evac(es)
                A = sb(es, name + "_A", [128, kt_n, ntok], BF16)
                Wt = [sb(es, name + "_W%d" % i, [128, kt_n, cw], BF16) for i in range(2)]
                pp = [ps(es, name + "_p%d" % i, [128, 512], F32) for i in range(6)]
                b_A, b_W, b_pp = Buf(), [Buf(), Buf()], [Buf() for _ in range(6)]
                av = act_d.rearrange("(k p) t -> p k t", p=128)
                for k0 in range(0, kt_n, 8):
                    S_.dma("sp", A[:, k0:k0 + 8, :], av[:, k0:k0 + 8, tok0:tok0 + ntok], writes=[b_A])
                wv = W_d.rearrange("(k p) c -> p k c", p=128)
                tbs = tok_blocks(0, ntok)
                cnt = 0
                for wi, c0 in enumerate(range(col0, col0 + ncols, cw)):
                    i = wi % 2
                    S_.dma("pool", Wt[i][:], wv[:, :, c0:c0 + cw], writes=[b_W[i]])
                    for cc in range(cw // 128):
                        for (t0, n) in tbs:
                            pi = cnt % 6
                            cnt += 1
                            for kt in range(kt_n):
                                S_.op("pe", lambda h, kt=kt, i=i, cc=cc, t0=t0, n=n, pi=pi: h.matmul(
                                    pp[pi][:, 0:n], lhsT=Wt[i][:, kt, cc * 128:(cc + 1) * 128], rhs=A[:, kt, t0:t0 + n],
                                    start=(kt == 0), stop=(kt == kt_n - 1)),
                                    reads=[b_A, b_W[i]], writes=[b_pp[pi]], sig=(kt == kt_n - 1))
                            evac(es, pp[pi][:, 0:n], c0 + cc * 128, tok0 + t0, n, b_pp[pi], cnt)
                S_.barrier()

        class Stage:
            def __init__(self, es, name, dt, n=4, w=512):
                self.t = [sb(es, "%s_st%d" % (name, i), [128, w], dt) for i in range(n)]
                self.b = [Buf() for _ in range(n)]
                self.i = 0

            def nxt(self):
                j = self.i % len(self.t)
                self.i += 1
                return self.t[j], self.b[j]

        def p2(tok0, ntok, full):
            def mk(es_):
                stg = {"f": Stage(es_, "p2f", F32), "h": Stage(es_, "p2h", BF16)}

                def evac(es, p, c, t, n, b_p, idx):
                    return evac_(stg, es, p, c, t, n, b_p, idx)
                return evac

            def evac_(stg, es, p, c, t, n, b_p, idx):
                eng = "act" if idx % 2 == 0 else "dve"
                if c < XBC0:
                    tl, bt = stg["f"].nxt()
                    S_.op("act", lambda h: h.activation(out=tl[:, 0:n], in_=p, func=AF.Silu), reads=[b_p], writes=[bt])
                    S_.dma("sp", zsT[c:c + 128, t:t + n], tl[:, 0:n], reads=[bt])
                    return
                if c >= V0:
                    tl, bt = stg["h"].nxt()
                    dst = vT[c - V0:c - V0 + 128, t:t + n]
                else:
                    tl, bt = stg["f"].nxt()
                    if c < DT0:
                        dst = xbcpre[c - XBC0:c - XBC0 + 128, t:t + n]
                    elif c < Q0:
                        dst = dtraw[c - DT0:c - DT0 + 128, t:t + n]
                    elif c < K0:
                        dst = qT[c - Q0:c - Q0 + 128, t:t + n]
                    else:
                        dst = kT[c - K0:c - K0 + 128, t:t + n]
                if eng == "act":
                    S_.op("act", lambda h: h.activation(out=tl[:, 0:n], in_=p, func=AF.Copy), reads=[b_p], writes=[bt])
                else:
                    S_.op("dve", lambda h: h.tensor_copy(out=tl[:, 0:n], in_=p), reads=[b_p], writes=[bt])
                S_.dma("sp", dst, tl[:, 0:n], reads=[bt])
            if full:
                mm_A("p2a", hT, KT, tok0, ntok, w_in, 0, IN_COLS, mk)
            else:
                mm_A("p2b", hT, KT, tok0, ntok, w_in, XBC0, 8192 + 1024, mk)
                mm_A("p2c", hT, KT, tok0, ntok, w_in, DT0, 256, mk)
                mm_A("p2d", hT, KT, tok0, ntok, w_in, K0, 2048, mk)

        p2(0, NOWN, True)
        p2(NOWN, S - NOWN, False)

        def p2g():
            stg = {}
            with ExitStack() as esb:
                bg = sb(esb, "bgate", [128, 64], F32)
                b_bg = Buf()
                S_.dma("sp", bg[:], b_gate[:, :], writes=[b_bg])

                def mk(es_):
                    st_ = Stage(es_, "pgf", F32)

                    def evac(es, p, c, t, n, b_p, idx):
                        tl, bt = st_.nxt()
                        S_.op("act", lambda h: h.activation(out=tl[:, 0:n], in_=p, func=AF.Sigmoid, bias=bg[:, c // 128:c // 128 + 1]),
                              reads=[b_p, b_bg], writes=[bt])
                        S_.dma("sp", gatesT[c:c + 128, t:t + n], tl[:, 0:n], reads=[bt])
                    return evac
                mm_A("pg", hT, KT, 0, NOWN, w_gate, 0, 2 * D, mk)
        p2g()
        if stop_after <= 2:
            return finish(nc, S_, out_d)

        def p3():
            with ExitStack() as es:
                xin = [sb(es, "cxin%d" % i, [128, S + 2], F32) for i in range(2)]
                acc = sb(es, "cacc", [128, S], F32)
                ob = [sb(es, "cob%d" % i, [128, S], BF16) for i in range(2)]
                cw = sb(es, "ccw", [128, 80, 3], F32)
                cbs = sb(es, "ccb", [128, 80], F32)
                stT = [sb(es, "cstT%d" % i, [128, 32, 128], BF16) for i in range(2)]
                pT = [ps(es, "cpT%d" % i, [128, 1024], BF16) for i in range(2)]
                pF = [ps(es, "cpF%d" % i, [128, 512], F32) for i in range(2)]
                tt = sb(es, "ctt", [128, S], F32)
                dtv = sb(es, "cdtv", [128, S], F32)
                stF = sb(es, "cstF", [128, 32, 128], F32)
                al = sb(es, "cal", [128, 2], F32)
                db = sb(es, "cdb", [128, 2], F32)
                b_xin, b_ob, b_stT, b_pT, b_pF = [Buf(), Buf()], [Buf(), Buf()], [Buf(), Buf()], [Buf(), Buf()], [Buf(), Buf()]
                b_acc, b_cw, b_tt, b_dtv, b_stF, b_al = Buf(), Buf(), Buf(), Buf(), Buf(), Buf()
                S_.dma("sp", cw[:], scw[:, :, :], writes=[b_cw])
                S_.dma("sp", cbs[:], scb[:, :], writes=[b_cw])
                S_.dma("sp", al[:], alog[:, :], writes=[b_al])
                S_.dma("sp", db[:], dtb[:, :], writes=[b_al])
                for i in range(2):
                    S_.op("dve", lambda h, i=i: h.memset(xin[i][:, 0:1], 0.0), writes=[b_xin[i]])
                    S_.op("dve", lambda h, i=i: h.memset(xin[i][:, S + 1:S + 2], 0.0), writes=[b_xin[i]])
                xsv = xs_tok.rearrange("(t p) c -> p t c", p=128)
                btv = B_tok.rearrange("(t p) c -> p t c", p=128)
                for blk in range(80):
                    i = blk % 2
                    nld = S if blk < 72 else NOWN
                    S_.dma("sp", xin[i][:, 1:nld + 1], xbcpre[blk * 128:(blk + 1) * 128, 0:nld], writes=[b_xin[i]])
                    S_.op("act", lambda h, i=i, blk=blk: h.activation(out=acc[:], in_=xin[i][:, 1:S + 1], func=AF.Identity,
                                                                        scale=cw[:, blk, 1:2], bias=cbs[:, blk:blk + 1]),
                          reads=[b_xin[i], b_cw], writes=[b_acc])
                    S_.op("dve", lambda h, i=i, blk=blk: h.scalar_tensor_tensor(out=acc[:], in0=xin[i][:, 0:S], scalar=cw[:, blk, 0:1],
                                                                                  in1=acc[:], op0=ALU.mult, op1=ALU.add),
                          reads=[b_xin[i], b_cw], writes=[b_acc])
                    S_.op("dve", lambda h, i=i, blk=blk: h.scalar_tensor_tensor(out=acc[:], in0=xin[i][:, 2:S + 2], scalar=cw[:, blk, 2:3],
                                                                                  in1=acc[:], op0=ALU.mult, op1=ALU.add),
                          reads=[b_xin[i], b_cw], writes=[b_acc])
                    S_.op("act", lambda h, i=i: h.activation(out=ob[i][:], in_=acc[:], func=AF.Silu), reads=[b_acc], writes=[b_ob[i]])
                    if blk < 72:
                        for g in range(4):
                            j = g % 2
                            for q in range(8):
                                tl = g * 8 + q
                                S_.op("pe", lambda h, tl=tl, q=q, j=j, i=i: h.transpose(out=pT[j][:, q * 128:(q + 1) * 128],
                                                                                         in_=ob[i][:, tl * 128:(tl + 1) * 128], identity=IDb),
                                      reads=[b_ob[i], bC16], writes=[b_pT[j]], sig=(q == 7))
                            if g % 2 == 0:
                                S_.op("act", lambda h, g=g, j=j, i=i: h.activation(out=stT[i][:, g * 8:(g + 1) * 8, :].rearrange("p a b -> p (a b)"),
                                                                                     in_=pT[j][:], func=AF.Copy),
                                      reads=[b_pT[j]], writes=[b_stT[i]])
                            else:
                                S_.op("dve", lambda h, g=g, j=j, i=i: h.tensor_copy(out=stT[i][:, g * 8:(g + 1) * 8, :].rearrange("p a b -> p (a b)"),
                                                                                      in_=pT[j][:]),
                                      reads=[b_pT[j]], writes=[b_stT[i]])
                        if blk < 64:
                            S_.dma("sp", xsv[:, :, blk * 128:(blk + 1) * 128], stT[i][:], reads=[b_stT[i]])
                        else:
                            S_.dma("sp", btv[:, :, (blk - 64) * 128:(blk - 63) * 128], stT[i][:], reads=[b_stT[i]])
                            S_.dma("sp", BTd[(blk - 64) * 128:(blk - 63) * 128, :], ob[i][:], reads=[b_ob[i]])
                    else:
                        S_.dma("sp", CTd[(blk - 72) * 128:(blk - 71) * 128, 0:NOWN], ob[i][:, 0:NOWN], reads=[b_ob[i]])
                S_.op("act", lambda h: h.activation(out=al[:], in_=al[:], func=AF.Exp), reads=[b_al], writes=[b_al])
                S_.op("dve", lambda h: h.tensor_scalar(out=al[:], in0=al[:], scalar1=-1.0, scalar2=None, op0=ALU.mult), reads=[b_al], writes=[b_al])
                dtv_ = dt_tok.rearrange("(t p) c -> p t c", p=128)
                atv_ = a_tok.rearrange("(t p) c -> p t c", p=128)
                for d in range(2):
                    i = d
                    S_.dma("sp", xin[i][:, 1:S + 1], dtraw[d * 128:(d + 1) * 128, :], writes=[b_xin[i]])
                    S_.op("act", lambda h, i=i, d=d: h.activation(out=acc[:], in_=xin[i][:, 1:S + 1], func=AF.Identity, bias=db[:, d:d + 1]),
                          reads=[b_xin[i], b_al], writes=[b_acc])
                    S_.op("act", lambda h: h.activation(out=tt[:], in_=acc[:], func=AF.Abs), reads=[b_acc], writes=[b_tt])
                    S_.op("act", lambda h: h.activation(out=tt[:], in_=tt[:], func=AF.Exp, scale=-1.0), reads=[b_tt], writes=[b_tt])
                    S_.op("act", lambda h: h.activation(out=tt[:], in_=tt[:], func=AF.Ln, bias=1.0), reads=[b_tt], writes=[b_tt])
                    S_.op("dve", lambda h: h.scalar_tensor_tensor(out=dtv[:], in0=acc[:], scalar=0.0, in1=tt[:], op0=ALU.max, op1=ALU.add),
                          reads=[b_acc, b_tt], writes=[b_dtv])
                    S_.op("dve", lambda h, d=d: h.tensor_scalar(out=tt[:], in0=dtv[:], scalar1=al[:, d:d + 1], scalar2=None, op0=ALU.mult),
                          reads=[b_dtv, b_al], writes=[b_tt])
                    for (src, b_src, dstv) in ((dtv, b_dtv, dtv_), (tt, b_tt, atv_)):
                        for g in range(8):
                            j = g % 2
                            for q in range(4):
                                tl = g * 4 + q
                                S_.op("pe", lambda h, tl=tl, q=q, j=j, src=src: h.transpose(out=pF[j][:, q * 128:(q + 1) * 128],
                                                                                             in_=src[:, tl * 128:(tl + 1) * 128], identity=IDf),
                                      reads=[b_src, bC32], writes=[b_pF[j]], sig=(q == 3))
                            S_.op("dve", lambda h, g=g, j=j: h.tensor_copy(out=stF[:, g * 4:(g + 1) * 4, :].rearrange("p a b -> p (a b)"), in_=pF[j][:]),
                                  reads=[b_pF[j]], writes=[b_stF])
                        S_.dma("sp", dstv[:, :, d * 128:(d + 1) * 128], stF[:], reads=[b_stF])
                S_.barrier()
        p3()
        if stop_after <= 3:
            return finish(nc, S_, out_d)

        def p4():
            with ExitStack() as es:
                H = sb(es, "sH", [128, 8, 1024], F32)
                Hb = sb(es, "sHb", [128, 8, 1024], BF16)
                xs = [sb(es, "sxs%d" % i, [128, DI], BF16) for i in range(2)]
                xdt = sb(es, "sxdt", [128, DI], BF16)
                xw = sb(es, "sxw", [128, DI], BF16)
                bt = [sb(es, "sbt%d" % i, [128, 1024], BF16) for i in range(2)]
                BTc = [sb(es, "sBT%d" % i, [128, 8, 128], BF16) for i in range(2)]
                CTc = [sb(es, "sCT%d" % i, [128, 8, 128], BF16) for i in range(2)]
                dts = [sb(es, "sdt%d" % i, [128, 128], F32) for i in range(2)]
                as_ = [sb(es, "sa%d" % i, [128, 128], F32) for i in range(2)]
                toend = sb(es, "stoend", [128, 128], F32)
                eL = sb(es, "seL", [128, 128], F32)
                ecum = sb(es, "secum", [128, 128], F32)
                wts = sb(es, "swts", [128, 128], F32)
                Dbc = sb(es, "sDbc", [128, 128], F32)
                segr = sb(es, "ssegr", [128, 16, 128], F32)
                E = sb(es, "sE", [128, 16, 128], F32)
                MT = sb(es, "sMT", [128, 16, 128], BF16)
                CBm = sb(es, "sCBm", [128, 128], F32)
                tmp = sb(es, "stmp", [128, 1024], F32)
                yst = [sb(es, "syst%d" % i, [128, 1024], F32) for i in range(2)]
                yld = [sb(es, "syld%d" % i, [128, 1024], F32) for i in range(2)]
                pseg = ps(es, "spseg", [128, 2048], F32)
                pyi = ps(es, "spyi", [128, 1024], F32)
                pyo = ps(es, "spyo", [128, 1024], F32)
                b_xs, b_bt, b_BC, b_da, b_yst, b_yld = ([Buf(), Buf()] for _ in range(6))
                b_xdt, b_xw, b_sm, b_D, b_segr, b_E, b_MT, b_CBm, b_tmp, b_pseg, b_pyi, b_pyo = (Buf() for _ in range(12))
                b_H = [Buf() for _ in range(8)]
                b_Hb = [Buf() for _ in range(8)]
                S_.dma("sp", Dbc[:], ssmd[0:1, :].partition_broadcast(128).rearrange("p a b -> p (a b)"), writes=[b_D])
                btv = BTd.rearrange("(g n) t -> n g t", n=128)
                ctv = CTd.rearrange("(g n) t -> n g t", n=128)
                v3 = lambda ap: ap.rearrange("p (h q) -> p h q", q=64)
                it = 0
                yk = 0
                for d in range(2):
                    for g in range(8):
                        S_.op("dve", lambda h, g=g: h.memset(H[:, g, :], 0.0), writes=[b_H[g]])
                        S_.op("dve", lambda h, g=g: h.memset(Hb[:, g, :], 0.0), writes=[b_Hb[g]])
                    chunks = list(range(17)) if d == 0 else list(range(31, -1, -1))
                    for c in chunks:
                        full = c <= 16
                        last = (c == chunks[-1])
                        i = it % 2
                        it += 1
                        r0 = c * 128
                        S_.dma("sp", xs[i][:], xs_tok[r0:r0 + 128, :], writes=[b_xs[i]])
                        S_.dma("sp", bt[i][:], B_tok[r0:r0 + 128, :], writes=[b_bt[i]])
                        S_.dma("sp", dts[i][:], dt_tok[r0:r0 + 128, d * 128:(d + 1) * 128], writes=[b_da[i]])
                        S_.dma("sp", as_[i][:], a_tok[r0:r0 + 128, d * 128:(d + 1) * 128], writes=[b_da[i]])
                        if full:
                            S_.dma("sp", BTc[i][:], btv[:, :, r0:r0 + 128], writes=[b_BC[i]])
                            S_.dma("sp", CTc[i][:], ctv[:, :, r0:r0 + 128], writes=[b_BC[i]])
                        for q, cm in enumerate((SU[d], ONEf, TRI[d])):
                            S_.op("pe", lambda h, q=q, cm=cm, i=i: h.matmul(pyo[:, q * 128:(q + 1) * 128], lhsT=cm, rhs=as_[i][:], start=True, stop=True),
                                  reads=[b_da[i], bC32], writes=[b_pyo], sig=(q == 2))
                        for q, dst in enumerate((toend, eL, ecum)):
                            S_.op("act", lambda h, q=q, dst=dst: h.activation(out=dst[:], in_=pyo[:, q * 128:(q + 1) * 128], func=AF.Exp),
                                  reads=[b_pyo], writes=[b_sm])
                        S_.op("dve", lambda h, i=i: h.tensor_tensor(out=wts[:], in0=toend[:], in1=dts[i][:], op=ALU.mult),
                              reads=[b_sm, b_da[i]], writes=[b_sm])
                        if full:
                            S_.op("dve", lambda h, i=i: h.tensor_tensor(out=v3(xdt[:]), in0=v3(xs[i][:]),
                                                                        in1=dts[i][:].unsqueeze(2).broadcast_to([128, 128, 64]), op=ALU.mult),
                                  reads=[b_xs[i], b_da[i]], writes=[b_xdt])
                        if not last:
                            S_.op("dve", lambda h, i=i: h.tensor_tensor(out=v3(xw[:]), in0=v3(xs[i][:]),
                                                                        in1=wts[:].unsqueeze(2).broadcast_to([128, 128, 64]), op=ALU.mult),
                                  reads=[b_xs[i], b_sm], writes=[b_xw])
                        for g in range(8):
                            if full:
                                S_.op("pe", lambda h, g=g, i=i: h.matmul(pyi[:, 0:128], lhsT=BTc[i][:, g, :], rhs=CTc[i][:, g, :], start=True, stop=True),
                                      reads=[b_BC[i]], writes=[b_pyi])
                                S_.op("dve", lambda h: h.tensor_tensor(out=CBm[:], in0=pyi[:, 0:128], in1=TRI[d], op=ALU.mult),
                                      reads=[b_pyi, bC32], writes=[b_CBm])
                                S_.op("dve", lambda h, g=g, i=i: h.tensor_tensor(
                                    out=segr[:], in0=TRI[d].unsqueeze(1).broadcast_to([128, 16, 128]),
                                    in1=as_[i][:, g * 16:(g + 1) * 16].unsqueeze(2).broadcast_to([128, 16, 128]), op=ALU.mult),
                                    reads=[b_da[i], bC32], writes=[b_segr])
                                sflat = segr[:].rearrange("p a b -> p (a b)")
                                for q in range(4):
                                    S_.op("pe", lambda h, q=q: h.matmul(pseg[:, q * 512:(q + 1) * 512], lhsT=SU[d], rhs=sflat[:, q * 512:(q + 1) * 512],
                                                                        start=True, stop=True),
                                          reads=[b_segr, bC32], writes=[b_pseg], sig=(q == 3))
                                eflat = E[:].rearrange("p a b -> p (a b)")
                                for q in range(4):
                                    S_.op("act", lambda h, q=q: h.activation(out=eflat[:, q * 512:(q + 1) * 512], in_=pseg[:, q * 512:(q + 1) * 512], func=AF.Exp),
                                          reads=[b_pseg], writes=[b_E])
                                S_.op("dve", lambda h: h.tensor_tensor(out=MT[:], in0=E[:], in1=CBm[:].unsqueeze(1).broadcast_to([128, 16, 128]), op=ALU.mult),
                                      reads=[b_E, b_CBm], writes=[b_MT])
                                for j in range(16):
                                    hh = g * 16 + j
                                    S_.op("pe", lambda h, j=j, hh=hh: h.matmul(pyi[:, j * 64:(j + 1) * 64], lhsT=MT[:, j, :], rhs=xdt[:, hh * 64:(hh + 1) * 64],
                                                                               start=True, stop=True),
                                          reads=[b_MT, b_xdt], writes=[b_pyi], sig=(j == 15))
                                for q in range(2):
                                    S_.op("pe", lambda h, q=q, g=g, i=i: h.matmul(pyo[:, q * 512:(q + 1) * 512], lhsT=CTc[i][:, g, :], rhs=Hb[:, g, q * 512:(q + 1) * 512],
                                                                                 start=True, stop=True),
                                          reads=[b_BC[i], b_Hb[g]], writes=[b_pyo], sig=(q == 1))
                                k = yk % 2
                                yk += 1
                                if d == 1:
                                    S_.dma("sp", yld[k][:], ysum[r0:r0 + 128, g * 1024:(g + 1) * 1024], writes=[b_yld[k]])
                                S_.op("dve", lambda h, g=g: h.tensor_tensor(out=v3(tmp[:]), in0=v3(pyo[:]),
                                                                            in1=ecum[:, g * 16:(g + 1) * 16].unsqueeze(2).broadcast_to([128, 16, 64]), op=ALU.mult),
                                      reads=[b_pyo, b_sm], writes=[b_tmp])
                                S_.op("dve", lambda h, k=k: h.tensor_tensor(out=yst[k][:], in0=tmp[:], in1=pyi[:], op=ALU.add),
                                      reads=[b_tmp, b_pyi], writes=[b_yst[k]])
                                if d == 0:
                                    S_.op("dve", lambda h, g=g, i=i: h.tensor_tensor(out=v3(tmp[:]), in0=v3(xs[i][:, g * 1024:(g + 1) * 1024]),
                                                                                    in1=Dbc[:, g * 16:(g + 1) * 16].unsqueeze(2).broadcast_to([128, 16, 64]), op=ALU.mult),
                                          reads=[b_xs[i], b_D], writes=[b_tmp])
                                    S_.op("dve", lambda h, k=k: h.tensor_tensor(out=yst[k][:], in0=yst[k][:], in1=tmp[:], op=ALU.add),
                                          reads=[b_tmp], writes=[b_yst[k]])
                                else:
                                    S_.op("dve", lambda h, k=k: h.tensor_tensor(out=yst[k][:], in0=yst[k][:], in1=yld[k][:], op=ALU.add),
                                          reads=[b_yld[k]], writes=[b_yst[k]])
                                S_.dma("sp", ysum[r0:r0 + 128, g * 1024:(g + 1) * 1024], yst[k][:], reads=[b_yst[k]], writes=[b_yld[k]])
                            if not last:
                                for q in range(2):
                                    S_.op("pe", lambda h, q=q, g=g, i=i: h.matmul(pseg[:, q * 512:(q + 1) * 512], lhsT=bt[i][:, g * 128:(g + 1) * 128],
                                                                                 rhs=xw[:, g * 1024 + q * 512:g * 1024 + (q + 1) * 512], start=True, stop=True),
                                          reads=[b_bt[i], b_xw], writes=[b_pseg], sig=(q == 1))
                                S_.op("dve", lambda h, g=g: h.tensor_tensor(out=v3(tmp[:]), in0=v3(H[:, g, :]),
                                                                            in1=eL[:, g * 16:(g + 1) * 16].unsqueeze(2).broadcast_to([128, 16, 64]), op=ALU.mult),
                                      reads=[b_H[g], b_sm], writes=[b_tmp])
                                S_.op("dve", lambda h, g=g: h.tensor_tensor(out=H[:, g, :], in0=tmp[:], in1=pseg[:, 0:1024], op=ALU.add),
                                      reads=[b_tmp, b_pseg], writes=[b_H[g]])
                                S_.op("act", lambda h, g=g: h.activation(out=Hb[:, g, :], in_=H[:, g, :], func=AF.Copy),
                                      reads=[b_H[g]], writes=[b_Hb[g]])
                    S_.barrier()
        p4()
        if stop_after <= 4:
            return finish(nc, S_, out_d)

        def p5():
            with ExitStack() as es:
                nw = sb(es, "g5nw", [128, 64], F32)
                ys = [sb(es, "g5ys%d" % i, [128, 4, 1024], F32) for i in range(2)]
                zs = [sb(es, "g5zs%d" % i, [128, 8, 512], F32) for i in range(2)]
                G = sb(es, "g5G", [128, 8, 512], F32)
                sq = [sb(es, "g5sq%d" % i, [128, 512], F32) for i in range(2)]
                rr = sb(es, "g5rr", [128, 512], F32)
                ob = [sb(es, "g5o%d" % i, [128, 512], BF16) for i in range(2)]
                pY = [ps(es, "g5pY%d" % i, [128, 512], F32) for i in range(2)]
                pSS = ps(es, "g5pSS", [128, 512], F32)
                b_nw, b_G, b_rr, b_pSS = Buf(), Buf(), Buf(), Buf()
                b_ys, b_zs, b_sq, b_ob, b_pY = ([Buf(), Buf()] for _ in range(5))
                S_.dma("sp", nw[:], snw[:, :], writes=[b_nw])
                ysv = ysum.rearrange("(t p) c -> p t c", p=128)
                zsv = zsT.rearrange("(c p) t -> p c t", p=128)
                it = 0
                for (t0, n) in tok_blocks(0, NOWN):
                    nt = n // 128
                    for g in range(8):
                        i = it % 2
                        it += 1
                        S_.dma("sp", ys[i][:, 0:nt, :], ysv[:, t0 // 128:t0 // 128 + nt, g * 1024:(g + 1) * 1024], writes=[b_ys[i]])
                        S_.dma("sp", zs[i][:, :, 0:n], zsv[:, g * 8:(g + 1) * 8, t0:t0 + n], writes=[b_zs[i]])
                        for ct in range(8):
                            j = ct % 2
                            for q in range(nt):
                                S_.op("pe", lambda h, q=q, ct=ct, j=j, i=i: h.transpose(out=pY[j][:, q * 128:(q + 1) * 128],
                                                                                         in_=ys[i][:, q, ct * 128:(ct + 1) * 128], identity=IDf),
                                      reads=[b_ys[i], bC32], writes=[b_pY[j]], sig=(q == nt - 1))
                            S_.op("dve", lambda h, ct=ct, j=j, i=i, n=n: h.tensor_tensor(out=G[:, ct, 0:n], in0=pY[j][:, 0:n], in1=zs[i][:, ct, 0:n], op=ALU.mult),
                                  reads=[b_pY[j], b_zs[i]], writes=[b_G])
                            S_.op("act", lambda h, ct=ct, j=j, n=n: h.activation(out=sq[j][:, 0:n], in_=G[:, ct, 0:n], func=AF.Square),
                                  reads=[b_G], writes=[b_sq[j]])
                            S_.op("pe", lambda h, ct=ct, j=j, n=n: h.matmul(pSS[:, 0:n], lhsT=ONEf, rhs=sq[j][:, 0:n], start=(ct == 0), stop=(ct == 7)),
                                  reads=[b_sq[j], bC32], writes=[b_pSS], sig=True)
                        S_.op("dve", lambda h, n=n: h.tensor_scalar(out=rr[:, 0:n], in0=pSS[:, 0:n], scalar1=1.0 / 1024.0, scalar2=RMS_EPS,
                                                                     op0=ALU.mult, op1=ALU.add), reads=[b_pSS], writes=[b_rr])
                        S_.op("act", lambda h, n=n: h.activation(out=rr[:, 0:n], in_=rr[:, 0:n], func=AF.Sqrt), reads=[b_rr], writes=[b_rr])
                        S_.op("dve", lambda h, n=n: h.reciprocal(out=rr[:, 0:n], in_=rr[:, 0:n]), reads=[b_rr], writes=[b_rr])
                        for ct in range(8):
                            j = ct % 2
                            cg = g * 8 + ct
                            S_.op("dve", lambda h, ct=ct, j=j, cg=cg, n=n: h.scalar_tensor_tensor(out=ob[j][:, 0:n], in0=G[:, ct, 0:n], scalar=nw[:, cg:cg + 1],
                                                                                                 in1=rr[:, 0:n], op0=ALU.mult, op1=ALU.mult),
                                  reads=[b_G, b_rr, b_nw], writes=[b_ob[j]])
                            S_.dma("sp", yssmT[cg * 128:(cg + 1) * 128, t0:t0 + n], ob[j][:, 0:n], reads=[b_ob[j]])
                S_.barrier()
        p5()
        if stop_after <= 5:
            return finish(nc, S_, out_d)

        def p6():
            with ExitStack() as es:
                gq = sb(es, "gq", [128, 1], F32)
                gk = sb(es, "gk", [128, 1], F32)
                xq = [sb(es, "a6x%d" % i, [128, 512], F32) for i in range(2)]
                cs = [sb(es, "a6c%d" % i, [128, 512], F32) for i in range(2)]
                sn = [sb(es, "a6s%d" % i, [128, 512], F32) for i in range(2)]
                sq = sb(es, "a6sq", [128, 512], F32)
                rr = sb(es, "a6r", [128, 512], F32)
                xn = sb(es, "a6xn", [128, 512], F32)
                t1 = sb(es, "a6t1", [128, 512], F32)
                t2 = sb(es, "a6t2", [128, 512], F32)
                ob = [sb(es, "a6o%d" % i, [128, 512], BF16) for i in range(2)]
                pS = ps(es, "a6pS", [128, 512], F32)
                pR = ps(es, "a6pR", [128, 512], F32)
                vrow = [sb(es, "a6v%d" % i, [128, S], BF16) for i in range(2)]
                vst = [sb(es, "a6vs%d" % i, [128, 32, 128], BF16) for i in range(2)]
                pT = [ps(es, "a6pT%d" % i, [128, 1024], BF16) for i in range(2)]
                b_g, b_sq, b_rr, b_xn, b_t1, b_t2, b_pS, b_pR = Buf(), Buf(), Buf(), Buf(), Buf(), Buf(), Buf(), Buf()
                b_xq, b_cs, b_ob, b_vrow, b_vst, b_pT = ([Buf(), Buf()] for _ in range(6))
                S_.dma("sp", gq[:], qnw[:, :], writes=[b_g])
                S_.dma("sp", gk[:], knw[:, :], writes=[b_g])
                it = 0
                for (src, dst, nh, ntok, g) in ((kT, kTn, 8, S, gk), (qT, qTn, 32, NOWN, gq)):
                    for hd in range(nh):
                        for (t0, n) in tok_blocks(0, ntok):
                            i = it % 2
                            it += 1
                            S_.dma("sp", xq[i][:, 0:n], src[hd * 128:(hd + 1) * 128, t0:t0 + n], writes=[b_xq[i]])
                            S_.dma("sp", cs[i][:, 0:n], cos_d[:, t0:t0 + n], writes=[b_cs[i]])
                            S_.dma("sp", sn[i][:, 0:n], sin_d[:, t0:t0 + n], writes=[b_cs[i]])
                            S_.op("act", lambda h, i=i, n=n: h.activation(out=sq[:, 0:n], in_=xq[i][:, 0:n], func=AF.Square),
                                  reads=[b_xq[i]], writes=[b_sq])
                            S_.op("pe", lambda h, n=n: h.matmul(pS[:, 0:n], lhsT=ONEf, rhs=sq[:, 0:n], start=True, stop=True),
                                  reads=[b_sq, bC32], writes=[b_pS])
                            S_.op("dve", lambda h, n=n: h.tensor_scalar(out=rr[:, 0:n], in0=pS[:, 0:n], scalar1=1.0 / 128.0, scalar2=RMS_EPS,
                                                                         op0=ALU.mult, op1=ALU.add), reads=[b_pS], writes=[b_rr])
                            S_.op("act", lambda h, n=n: h.activation(out=rr[:, 0:n], in_=rr[:, 0:n], func=AF.Sqrt), reads=[b_rr], writes=[b_rr])
                            S_.op("dve", lambda h, n=n: h.reciprocal(out=rr[:, 0:n], in_=rr[:, 0:n]), reads=[b_rr], writes=[b_rr])
                            S_.op("dve", lambda h, i=i, n=n, g=g: h.scalar_tensor_tensor(out=xn[:, 0:n], in0=xq[i][:, 0:n], scalar=g[:, 0:1],
                                                                                         in1=rr[:, 0:n], op0=ALU.mult, op1=ALU.mult),
                                  reads=[b_xq[i], b_rr, b_g], writes=[b_xn])
                            S_.op("pe", lambda h, n=n: h.matmul(pR[:, 0:n], lhsT=ROPf, rhs=xn[:, 0:n], start=True, stop=True),
                                  reads=[b_xn, bC32], writes=[b_pR])
                            S_.op("dve", lambda h, i=i, n=n: h.tensor_tensor(out=t1[:, 0:n], in0=xn[:, 0:n], in1=cs[i][:, 0:n], op=ALU.mult),
                                  reads=[b_xn, b_cs[i]], writes=[b_t1])
                            S_.op("dve", lambda h, i=i, n=n: h.tensor_tensor(out=t2[:, 0:n], in0=pR[:, 0:n], in1=sn[i][:, 0:n], op=ALU.mult),
                                  reads=[b_pR, b_cs[i]], writes=[b_t2])
                            S_.op("dve", lambda h, i=i, n=n: h.tensor_tensor(out=ob[i][:, 0:n], in0=t1[:, 0:n], in1=t2[:, 0:n], op=ALU.add),
                                  reads=[b_t1, b_t2], writes=[b_ob[i]])
                            S_.dma("sp", dst[hd * 128:(hd + 1) * 128, t0:t0 + n], ob[i][:, 0:n], reads=[b_ob[i]])
                vtv = v_tok.rearrange("(t p) c -> p t c", p=128)
                for hd in range(8):
                    i = hd % 2
                    S_.dma("sp", vrow[i][:], vT[hd * 128:(hd + 1) * 128, :], writes=[b_vrow[i]])
                    for g in range(4):
                        j = g % 2
                        for q in range(8):
                            tl = g * 8 + q
                            S_.op("pe", lambda h, tl=tl, q=q, j=j, i=i: h.transpose(out=pT[j][:, q * 128:(q + 1) * 128],
                                                                                     in_=vrow[i][:, tl * 128:(tl + 1) * 128], identity=IDb),
                                  reads=[b_vrow[i], bC16], writes=[b_pT[j]], sig=(q == 7))
                        S_.op("dve", lambda h, g=g, j=j, i=i: h.tensor_copy(out=vst[i][:, g * 8:(g + 1) * 8, :].rearrange("p a b -> p (a b)"), in_=pT[j][:]),
                              reads=[b_pT[j]], writes=[b_vst[i]])
                    S_.dma("sp", vtv[:, :, hd * 128:(hd + 1) * 128], vst[i][:], reads=[b_vst[i]])
                S_.barrier()
        p6()
        if stop_after <= 6:
            return finish(nc, S_, out_d)

        def p7():
            scale = 128.0 ** -0.5
            with ExitStack() as es:
                ksb = [sb(es, "a7k%d" % i, [128, S], BF16) for i in range(2)]
                vsb = [sb(es, "a7v%d" % i, [128, 32, 128], BF16) for i in range(2)]
                qsb = [sb(es, "a7q%d" % i, [128, 512], BF16) for i in range(2)]
                pt = [sb(es, "a7p%d" % i, [128, 512], BF16) for i in range(4)]
                rec = sb(es, "a7rec", [128, 512], F32)
                osb = [sb(es, "a7o%d" % i, [128, 512], BF16) for i in range(2)]
                pS = [ps(es, "a7pS%d" % i, [128, 512], F32) for i in range(4)]
                pO = [ps(es, "a7pO%d" % i, [128, 512], F32) for i in range(2)]
                pL = [ps(es, "a7pL%d" % i, [128, 512], F32) for i in range(2)]
                b_k, b_v, b_q, b_o, b_pO, b_pL = ([Buf(), Buf()] for _ in range(6))
                b_pt = [Buf() for _ in range(4)]
                b_pS = [Buf() for _ in range(4)]
                b_rec = Buf()
                vtv = v_tok.rearrange("(t p) c -> p t c", p=128)
                it = 0
                sc = 0
                for kvh in range(8):
                    ki = kvh % 2
                    S_.dma("sp", ksb[ki][:], kTn[kvh * 128:(kvh + 1) * 128, :], writes=[b_k[ki]])
                    S_.dma("sp", vsb[ki][:], vtv[:, :, kvh * 128:(kvh + 1) * 128], writes=[b_v[ki]])
                    for qh in range(4):
                        hd = kvh * 4 + qh
                        for (t0, n) in tok_blocks(0, NOWN):
                            i = it % 2
                            it += 1
                            S_.dma("sp", qsb[i][:, 0:n], qTn[hd * 128:(hd + 1) * 128, t0:t0 + n], writes=[b_q[i]])

                            def smm(kt, i=i, n=n, ki=ki):
                                j = (sc + kt) % 4
                                S_.op("pe", lambda h: h.matmul(pS[j][:, 0:n], lhsT=ksb[ki][:, kt * 128:(kt + 1) * 128], rhs=qsb[i][:, 0:n],
                                                               start=True, stop=True), reads=[b_k[ki], b_q[i]], writes=[b_pS[j]])
                                S_.op("act", lambda h: h.activation(out=pt[j][:, 0:n], in_=pS[j][:, 0:n], func=AF.Exp, scale=scale),
                                      reads=[b_pS[j]], writes=[b_pt[j]])

                            def pv(kt, i=i, n=n, ki=ki):
                                j = (sc + kt) % 4
                                S_.op("pe", lambda h: h.matmul(pO[i][:, 0:n], lhsT=vsb[ki][:, kt, :], rhs=pt[j][:, 0:n],
                                                               start=(kt == 0), stop=(kt == 31)), reads=[b_v[ki], b_pt[j]], writes=[b_pO[i]], sig=(kt == 31))
                                S_.op("pe", lambda h: h.matmul(pL[i][:, 0:n], lhsT=ONEb, rhs=pt[j][:, 0:n],
                                                               start=(kt == 0), stop=(kt == 31)), reads=[bC16, b_pt[j]], writes=[b_pL[i]], sig=True)
                            smm(0)
                            smm(1)
                            for kt in range(32):
                                pv(kt)
                                if kt + 2 < 32:
                                    smm(kt + 2)
                            sc += 32
                            S_.op("dve", lambda h, i=i, n=n: h.reciprocal(out=rec[:, 0:n], in_=pL[i][:, 0:n]), reads=[b_pL[i]], writes=[b_rec])
                            S_.op("dve", lambda h, i=i, n=n: h.tensor_tensor(out=osb[i][:, 0:n], in0=pO[i][:, 0:n], in1=rec[:, 0:n], op=ALU.mult),
                                  reads=[b_pO[i], b_rec], writes=[b_o[i]])
                            S_.dma("sp", yattnT[hd * 128:(hd + 1) * 128, t0:t0 + n], osb[i][:, 0:n], reads=[b_o[i]])
                S_.barrier()
        p7()
        if stop_after <= 7:
            return finish(nc, S_, out_d)

        def p8():
            def mk1(es_):
                gs, ts = Stage(es_, "p8g", F32), Stage(es_, "p8t", F32)

                def evac(es, p, c, t, n, b_p, idx):
                    gl, bg_ = gs.nxt()
                    tl, bt_ = ts.nxt()
                    S_.dma("sp", gl[:, 0:n], gatesT[c:c + 128, t:t + n], writes=[bg_])
                    S_.op("dve", lambda h: h.tensor_tensor(out=tl[:, 0:n], in0=p, in1=gl[:, 0:n], op=ALU.mult), reads=[b_p, bg_], writes=[bt_])
                    S_.dma("sp", t1T[c:c + 128, t:t + n], tl[:, 0:n], reads=[bt_])
                return evac
            mm_A("p8a", yssmT, 64, 0, 1088, w_sp, 0, D, mk1, cw=128)
            mm_A("p8b", yssmT, 64, 1088, 1088, w_sp, 0, D, mk1, cw=128)

            def mk2(es_):
                gs, ts, ms, os_ = Stage(es_, "p8g2", F32), Stage(es_, "p8t2", F32), Stage(es_, "p8m", F32), Stage(es_, "p8o", BF16)

                def evac(es, p, c, t, n, b_p, idx):
                    gl, bg_ = gs.nxt()
                    tl, bt_ = ts.nxt()
                    ml, bm_ = ms.nxt()
                    ol, bo_ = os_.nxt()
                    S_.dma("sp", gl[:, 0:n], gatesT[D + c:D + c + 128, t:t + n], writes=[bg_])
                    S_.dma("sp", tl[:, 0:n], t1T[c:c + 128, t:t + n], writes=[bt_])
                    S_.op("dve", lambda h: h.tensor_tensor(out=ml[:, 0:n], in0=p, in1=gl[:, 0:n], op=ALU.mult), reads=[b_p, bg_], writes=[bm_])
                    S_.op("dve", lambda h: h.tensor_tensor(out=ol[:, 0:n], in0=ml[:, 0:n], in1=tl[:, 0:n], op=ALU.add), reads=[bm_, bt_], writes=[bo_])
                    S_.dma("sp", mixT[c:c + 128, t:t + n], ol[:, 0:n], reads=[bo_])
                return evac
            mm_A("p8c", yattnT, KT, 0, NOWN, w_ap, 0, D, mk2)
        p8()
        if stop_after <= 8:
            return finish(nc, S_, out_d)

        def mm_B(name, act_d, kt_n, tok0, ntok, W_d, ncols, dst_d, dst_t0, cw=256):
            with ExitStack() as es:
                A = sb(es, name + "_A", [128, kt_n, ntok], BF16)
                Wt = [sb(es, name + "_W%d" % i, [128, kt_n, cw], BF16) for i in range(2)]
                pp = [ps(es, name + "_p%d" % i, [128, 512], F32) for i in range(6)]
                stg = Stage(es, name + "_s", F32, n=4, w=cw)
                b_A, b_W, b_pp = Buf(), [Buf(), Buf()], [Buf() for _ in range(6)]
                av = act_d.rearrange("(k p) t -> p k t", p=128)
                step = 8 if kt_n % 8 == 0 else 2
                for k0 in range(0, kt_n, step):
                    S_.dma("sp", A[:, k0:k0 + step, :], av[:, k0:k0 + step, tok0:tok0 + ntok], writes=[b_A])
                wv = W_d.rearrange("(k p) c -> p k c", p=128)
                cnt = 0
                for wi, c0 in enumerate(range(0, ncols, cw)):
                    i = wi % 2
                    S_.dma("pool", Wt[i][:], wv[:, :, c0:c0 + cw], writes=[b_W[i]])
                    for tt in range(ntok // 128):
                        pi = cnt % 6
                        cnt += 1
                        for kt in range(kt_n):
                            S_.op("pe", lambda h, kt=kt, i=i, tt=tt, pi=pi: h.matmul(
                                pp[pi][:, 0:cw], lhsT=A[:, kt, tt * 128:(tt + 1) * 128], rhs=Wt[i][:, kt, :],
                                start=(kt == 0), stop=(kt == kt_n - 1)),
                                reads=[b_A, b_W[i]], writes=[b_pp[pi]], sig=(kt == kt_n - 1))
                        tl, bt_ = stg.nxt()
                        if cnt % 2 == 0:
                            S_.op("act", lambda h, pi=pi, tl=tl: h.activation(out=tl[:, 0:cw], in_=pp[pi][:, 0:cw], func=AF.Copy), reads=[b_pp[pi]], writes=[bt_])
                        else:
                            S_.op("dve", lambda h, pi=pi, tl=tl: h.tensor_copy(out=tl[:, 0:cw], in_=pp[pi][:, 0:cw]), reads=[b_pp[pi]], writes=[bt_])
                        r0 = dst_t0 + tt * 128
                        S_.dma("sp", dst_d[r0:r0 + 128, c0:c0 + cw], tl[:, 0:cw], reads=[bt_])
                S_.barrier()

        mm_B("p9", mixT, KT, 0, NOWN, w_out, D, mixed, 0)

        def ln_affine_pass(name, a_d, res_d, gcol0, lng_d, lnb_d, ntiles, dst_d, do_h2):
            with ExitStack() as es:
                gbc = sb(es, name + "gbc", [128, D], F32)
                lg = sb(es, name + "lg", [128, D], F32)
                lb = sb(es, name + "lb", [128, D], F32)
                at = sb(es, name + "at", [128, D], F32)
                rt = [sb(es, name + "rt%d" % i, [128, D], F32) for i in range(2)]
                st = sb(es, name + "st", [128, 8, 6], F32)
                mv = sb(es, name + "mv", [128, 2], F32)
                rstd = sb(es, name + "rstd", [128, 1], F32)
                b_bc, b_at, b_st = Buf(), Buf(), Buf()
                b_rt = [Buf(), Buf()]
                bc = lambda ap: ap.partition_broadcast(128).rearrange("p a b -> p (a b)")
                S_.dma("sp", gbc[:], bc(modD[0:1, gcol0:gcol0 + D]), writes=[b_bc])
                S_.dma("sp", lg[:], bc(lng_d[0:1, :]), writes=[b_bc])
                S_.dma("sp", lb[:], bc(lnb_d[0:1, :]), writes=[b_bc])
                if do_h2:
                    xn = sb(es, name + "xn", [128, D], F32)
                    hblk = sb(es, name + "hblk", [128, KT, 512], BF16)
                    pT = [ps(es, name + "pT%d" % i, [128, 512], F32) for i in range(2)]
                    b_xn, b_hblk = Buf(), Buf()
                    b_pT = [Buf(), Buf()]
                    h2v = h2T.rearrange("(k p) t -> p k t", p=128)
                for tt in range(ntiles):
                    i = tt % 2
                    r0 = tt * 128
                    S_.dma("sp", at[:], a_d[r0:r0 + 128, :], writes=[b_at])
                    S_.dma("sp", rt[i][:], res_d[r0:r0 + 128, :], writes=[b_rt[i]])
                    S_.op("dve", lambda h: h.tensor_tensor(out=at[:], in0=at[:], in1=gbc[:], op=ALU.mult), reads=[b_bc], writes=[b_at])
                    S_.op("dve", lambda h, i=i: h.scalar_tensor_tensor(out=rt[i][:], in0=rt[i][:], scalar=float(ALPHA), in1=at[:], op0=ALU.mult, op1=ALU.add),
                          reads=[b_at], writes=[b_rt[i]])
                    ln_stats(es, rt[i], b_rt[i], st, mv, rstd, b_st)
                    S_.op("dve", lambda h, i=i: h.tensor_scalar(out=rt[i][:], in0=rt[i][:], scalar1=mv[:, 0:1], scalar2=rstd[:, 0:1],
                                                                 op0=ALU.subtract, op1=ALU.mult), reads=[b_st], writes=[b_rt[i]])
                    S_.op("dve", lambda h, i=i: h.tensor_tensor(out=rt[i][:], in0=rt[i][:], in1=lg[:], op=ALU.mult), reads=[b_bc], writes=[b_rt[i]])
                    S_.op("dve", lambda h, i=i: h.tensor_tensor(out=rt[i][:], in0=rt[i][:], in1=lb[:], op=ALU.add), reads=[b_bc], writes=[b_rt[i]])
                    S_.dma("sp", dst_d[r0:r0 + 128, :], rt[i][:], reads=[b_rt[i]])
                    if do_h2:
                        ln_stats(es, rt[i], b_rt[i], st, mv, rstd, b_st)
                        S_.op("dve", lambda h, i=i: h.tensor_scalar(out=xn[:], in0=rt[i][:], scalar1=mv[:, 0:1], scalar2=rstd[:, 0:1],
                                                                     op0=ALU.subtract, op1=ALU.mult), reads=[b_rt[i], b_st], writes=[b_xn])
                        sub = tt % 4
                        modT(xn, b_xn, hblk, b_hblk, sub, pT, b_pT, 128, 96)
                        if sub == 3 or tt == ntiles - 1:
                            t0 = (tt // 4) * 512
                            w = (sub + 1) * 128
                            S_.dma("sp", h2v[:, :, t0:t0 + w], hblk[:, :, 0:w], reads=[b_hblk])
                S_.barrier()
        ln_affine_pass("l1", mixed, x_d, 2 * D, ln1g, ln1b, NOWN // 128, x1d, True)
        if stop_after <= 9:
            return finish(nc, S_, out_d)

        def p10():
            def mk(es_):
                st_ = Stage(es_, "p10s", F32)

                def evac(es, p, c, t, n, b_p, idx):
                    tl, bt_ = st_.nxt()
                    if idx % 2 == 0:
                        S_.op("act", lambda h: h.activation(out=tl[:, 0:n], in_=p, func=AF.Copy), reads=[b_p], writes=[bt_])
                    else:
                        S_.op("dve", lambda h: h.tensor_copy(out=tl[:, 0:n], in_=p), reads=[b_p], writes=[bt_])
                    S_.dma("sp", upre[c:c + 128, t:t + n], tl[:, 0:n], reads=[bt_])
                return evac
            mm_A("p10", h2T, KT, 0, NOWN, w_up, 0, 2 * FFN, mk)
        p10()

        def p11():
            NW = NOUT + 2
            with ExitStack() as es:
                xin = [sb(es, "fxin%d" % i, [128, NW], F32) for i in range(4)]
                acc = [sb(es, "facc%d" % i, [128, NOUT], F32) for i in range(2)]
                sa = sb(es, "fsa", [128, NOUT], F32)
                ob = [sb(es, "fob%d" % i, [128, NOUT], BF16) for i in range(2)]
                cw = sb(es, "fcw", [128, 172, 3], F32)
                cbs = sb(es, "fcb", [128, 172], F32)
                b_xin = [Buf() for _ in range(4)]
                b_acc, b_ob = [Buf(), Buf()], [Buf(), Buf()]
                b_cw, b_sa = Buf(), Buf()
                S_.dma("sp", cw[:], fcw[:, :, :], writes=[b_cw])
                S_.dma("sp", cbs[:], fcb[:, :], writes=[b_cw])
                for i in range(4):
                    S_.op("dve", lambda h, i=i: h.memset(xin[i][:, 0:1], 0.0), writes=[b_xin[i]])
                for blk in range(86):
                    for half in range(2):
                        ci = half * 86 + blk
                        i = (blk % 2) * 2 + half
                        S_.dma("sp", xin[i][:, 1:NW], upre[ci * 128:(ci + 1) * 128, 0:NOUT + 1], writes=[b_xin[i]])
                        S_.op("act", lambda h, i=i, ci=ci, half=half: h.activation(out=acc[half][:], in_=xin[i][:, 1:NOUT + 1], func=AF.Identity,
                                                                                scale=cw[:, ci, 1:2], bias=cbs[:, ci:ci + 1]),
                              reads=[b_xin[i], b_cw], writes=[b_acc[half]])
                        S_.op("dve", lambda h, i=i, ci=ci, half=half: h.scalar_tensor_tensor(out=acc[half][:], in0=xin[i][:, 0:NOUT], scalar=cw[:, ci, 0:1],
                                                                                          in1=acc[half][:], op0=ALU.mult, op1=ALU.add),
                              reads=[b_xin[i], b_cw], writes=[b_acc[half]])
                        S_.op("dve", lambda h, i=i, ci=ci, half=half: h.scalar_tensor_tensor(out=acc[half][:], in0=xin[i][:, 2:NOUT + 2], scalar=cw[:, ci, 2:3],
                                                                                          in1=acc[half][:], op0=ALU.mult, op1=ALU.add),
                              reads=[b_xin[i], b_cw], writes=[b_acc[half]])
                    j = blk % 2
                    S_.op("act", lambda h: h.activation(out=sa[:], in_=acc[0][:], func=AF.Silu), reads=[b_acc[0]], writes=[b_sa])
                    S_.op("dve", lambda h, j=j: h.tensor_tensor(out=ob[j][:], in0=sa[:], in1=acc[1][:], op=ALU.mult), reads=[b_sa, b_acc[1]], writes=[b_ob[j]])
                    S_.dma("sp", actT[blk * 128:(blk + 1) * 128, :], ob[j][:], reads=[b_ob[j]])
                S_.barrier()
        p11()
        if stop_after <= 11:
            return finish(nc, S_, out_d)

        for sbk in range(4):
            mm_B("p12_%d" % sbk, actT, 86, sbk * 512, 512, w_down, D, fd, sbk * 512)
        ln_affine_pass("l2", fd, x1d, 5 * D, ln2g, ln2b, NOUT // 128, out_d, False)

        return finish(nc, S_, out_d)


def finish(nc, S_, out_d):
    S_.barrier()
    return nc


def _consts():
    c = np.zeros((128, 8, 128), np.float32)
    i = np.arange(128)
    c[:, 0, :] = np.eye(128, dtype=np.float32)
    c[:, 1, :] = 1.0
    c[:, 2, :] = (i[:, None] <= i[None, :])
    c[:, 3, :] = (i[:, None] > i[None, :])
    c[:, 4, :] = (i[:, None] >= i[None, :])
    c[:, 5, :] = (i[:, None] < i[None, :])
    P = np.zeros((128, 128), np.float32)
    for base in (0, 64):
        for j in range(32):
            P[base + j, base + 32 + j] = -1.0
            P[base + 32 + j, base + j] = 1.0
    c[:, 6, :] = P.T
    return c


def _rope_tables(flip):
    t = np.arange(S)
    if flip:
        t = t[::-1]
    row = (t // 64).astype(np.float32)
    colp = (t % 64).astype(np.float32)
    inv = (1.0 / (np.float32(10000.0) ** (np.arange(32, dtype=np.float32) / np.float32(32)))).astype(np.float32)
    cos = np.zeros((128, S), np.float32)
    sin = np.zeros((128, S), np.float32)
    for d in range(128):
        pos = row if d < 64 else colp
        ang = (pos * inv[d % 32]).astype(np.float32)
        cos[d] = np.cos(ang)
        sin[d] = np.sin(ang)
    return cos, sin


def _pp(v, nb):
    return np.ascontiguousarray(np.asarray(v, np.float32).reshape(nb, 128).T)


def make_in_maps(inputs, cores):
    f = lambda k: np.asarray(inputs[k], np.float32)
    w_in0 = np.ascontiguousarray(f("w_in")[0])
    w_in1 = w_in0.copy()
    w_in1[:, DT0:DT0 + 128] = w_in0[:, DT0 + 128:DT0 + 256]
    w_in1[:, DT0 + 128:DT0 + 256] = w_in0[:, DT0:DT0 + 128]
    consts = _consts()
    ropes = [_rope_tables(0), _rope_tables(1)]
    maps = []
    for core in cores:
        b, hf = core // 2, core % 2
        xl = f("x")[b]
        if hf:
            xl = xl[::-1]
        taps = [2, 1, 0] if hf else [0, 1, 2]
        dirs = [1, 0] if hf else [0, 1]
        scw = f("ssm_conv_w")[0][taps]
        fcw = f("ffn_conv_w")[0][taps]
        m = {
            "x": np.ascontiguousarray(xl),
            "c": _pp(f("c")[b], 32),
            "w_ada": f("w_ada")[0], "b_ada": f("b_ada")[0].reshape(1, -1),
            "w_in": w_in1 if hf else w_in0,
            "ssm_conv_w": np.ascontiguousarray(scw.reshape(3, 80, 128).transpose(2, 1, 0)),
            "ssm_conv_b": _pp(f("ssm_conv_b")[0], 80),
            "ssm_a_log": np.ascontiguousarray(f("ssm_a_log")[0][dirs].T),
            "ssm_dt_bias": np.ascontiguousarray(f("ssm_dt_bias")[0][dirs].T),
            "ssm_d": f("ssm_d")[0].reshape(1, 128),
            "ssm_norm_w": _pp(f("ssm_norm_w")[0], 64),
            "q_norm_w": f("q_norm_w")[0].reshape(128, 1), "k_norm_w": f("k_norm_w")[0].reshape(128, 1),
            "w_ssm_proj": f("w_ssm_proj")[0], "w_attn_proj": f("w_attn_proj")[0],
            "w_gate": f("w_gate")[0], "b_gate": _pp(f("b_gate")[0], 64),
            "w_out": f("w_out")[0], "ln1_g": f("ln1_g")[0].reshape(1, -1), "ln1_b": f("ln1_b")[0].reshape(1, -1),
            "w_up": f("w_up")[0],
            "ffn_conv_w": np.ascontiguousarray(fcw.reshape(3, 172, 128).transpose(2, 1, 0)),
            "ffn_conv_b": _pp(f("ffn_conv_b")[0], 172),
            "w_down": f("w_down")[0], "ln2_g": f("ln2_g")[0].reshape(1, -1), "ln2_b": f("ln2_b")[0].reshape(1, -1),
            "rope_cos": ropes[hf][0], "rope_sin": ropes[hf][1],
            "consts": consts,
        }
        maps.append(m)
    return maps


def kernel(**inputs):
    nc = build_nc()
    cores = list(range(8))
    maps = make_in_maps(inputs, cores)
    res = run_bass_kernel_spmd(nc, maps, core_ids=cores)
    out = np.zeros((4, S, D), np.float32)
    for core in cores:
        b, hf = core // 2, core % 2
        o = np.asarray(res.results[core]["out"])
        if hf:
            out[b, NOUT:] = o[::-1]
        else:
            out[b, :NOUT] = o
    return out
```

```python
import math
from contextlib import ExitStack
import numpy as np
import concourse.bass as bass
import concourse.mybir as mybir
from concourse.bass_utils import run_bass_kernel_spmd

F32 = mybir.dt.float32
BF16 = mybir.dt.bfloat16
ALU = mybir.AluOpType
AF = mybir.ActivationFunctionType

D = 4096
S = 4096
NOWN = 2176
NOUT = 2048
KT = 32
DI = 8192
NCONV = 10240
FFN = 11008
IN_COLS = 24832
Z0, XBC0, DT0, Q0, K0, V0 = 0, 8192, 18432, 18688, 22784, 23808
ALPHA = 2.0 ** 0.25
LN_EPS = 1e-5
RMS_EPS = 1e-6


class Buf:
    __slots__ = ("w", "r")

    def __init__(self):
        self.w = None
        self.r = {}


class Sched:
    def __init__(self, nc, es, ndma=20):
        self.nc = nc
        self.engs = {}
        for name, h in (("pe", nc.tensor), ("act", nc.scalar), ("dve", nc.vector),
                        ("pool", nc.gpsimd), ("sp", nc.sync)):
            sem = es.enter_context(nc.semaphore("s_" + name))
            self.engs[name] = {"name": name, "h": h, "sem": sem, "cnt": 0, "seen": {}}
        self.dsems = {q: [[es.enter_context(nc.semaphore("d%s%d" % (q, i)), ), 0] for i in range(n)]
                      for q, n in (("sp", 12), ("pool", 10), ("act", 6))}
        self.di = {"sp": 0, "pool": 0, "act": 0}
        self.store_q = "pool"

    def _wait(self, e, ev):
        sem, val, src = ev
        if src == "pe" and e["name"] == "pe":
            return
        k = id(sem)
        if e["seen"].get(k, 0) >= val:
            return
        e["h"].wait_ge(sem, val)
        e["seen"][k] = val

    def _deps(self, e, reads, writes):
        for b in reads:
            if b.w is not None:
                self._wait(e, b.w)
        for b in writes:
            if b.w is not None:
                self._wait(e, b.w)
            for ev in b.r.values():
                self._wait(e, ev)

    def _record(self, ev, reads, writes):
        for b in reads:
            b.r[ev[2]] = ev
        for b in writes:
            b.w = ev
            b.r = {}

    def op(self, en, fn, reads=(), writes=(), sig=True):
        e = self.engs[en]
        self._deps(e, reads, writes)
        ins = fn(e["h"])
        if sig:
            e["cnt"] += 1
            ins.then_inc(e["sem"], 1)
            ev = (e["sem"], e["cnt"], en)
        else:
            ev = (e["sem"], e["cnt"] + 1, en)
        self._record(ev, reads, writes)

    def dma(self, en, out, in_, reads=(), writes=(), **kw):
        if en == "sp" and str(out.space) == "DRAM":
            en = self.store_q
        e = self.engs[en]
        self._deps(e, reads, writes)
        pool_ = self.dsems[en]
        idx = self.di[en]
        slot = pool_[idx]
        self.di[en] = (idx + 1) % len(pool_)
        key = "dq%s%d" % (en, idx)
        if slot[1] > 0:
            self._wait(e, (slot[0], slot[1], key))
        ins = e["h"].dma_start(out=out, in_=in_, **kw)
        slot[1] += 16
        ins.then_inc(slot[0], 16)
        self._record((slot[0], slot[1], key), reads, writes)

    def barrier(self):
        for e in self.engs.values():
            for o in self.engs.values():
                if o is not e and o["cnt"] > 0:
                    self._wait(e, (o["sem"], o["cnt"], o["name"] + "_b"))
            for q, pool_ in self.dsems.items():
                for i, sl in enumerate(pool_):
                    if sl[1] > 0:
                        self._wait(e, (sl[0], sl[1], "dq%s%d" % (q, i)))


def tok_blocks(n0, n, bs=512):
    out = []
    t = n0
    while t < n0 + n:
        m = min(bs, n0 + n - t)
        out.append((t, m))
        t += m
    return out


class K:
    def __init__(self, nc, dev_out):
        self.nc = nc
        self.dev_out = set(dev_out)
        self.flip = 0

    def dram(self, name, shape, dt, inp=False, out=False):
        if inp:
            return self.nc.dram_tensor(name, shape, dt, kind="ExternalInput").ap()
        if out or name in self.dev_out:
            return self.nc.dram_tensor(name, shape, dt, kind="ExternalOutput").ap()
        return self.nc.dram_tensor(name, shape, dt).ap()


def build_nc(dev_out=(), stop_after=99):
    nc = bass.Bass("TRN2", target_bir_lowering=False)
    kb = K(nc, dev_out)
    dram = kb.dram
    x_d = dram("x", [S, D], F32, inp=True)
    c_d = dram("c", [128, KT], F32, inp=True)
    w_ada = dram("w_ada", [D, 6 * D], F32, inp=True)
    b_ada = dram("b_ada", [1, 6 * D], F32, inp=True)
    w_in = dram("w_in", [D, IN_COLS], F32, inp=True)
    scw = dram("ssm_conv_w", [128, 80, 3], F32, inp=True)
    scb = dram("ssm_conv_b", [128, 80], F32, inp=True)
    alog = dram("ssm_a_log", [128, 2], F32, inp=True)
    dtb = dram("ssm_dt_bias", [128, 2], F32, inp=True)
    ssmd = dram("ssm_d", [1, 128], F32, inp=True)
    snw = dram("ssm_norm_w", [128, 64], F32, inp=True)
    qnw = dram("q_norm_w", [128, 1], F32, inp=True)
    knw = dram("k_norm_w", [128, 1], F32, inp=True)
    w_sp = dram("w_ssm_proj", [DI, D], F32, inp=True)
    w_ap = dram("w_attn_proj", [D, D], F32, inp=True)
    w_gate = dram("w_gate", [D, 2 * D], F32, inp=True)
    b_gate = dram("b_gate", [128, 64], F32, inp=True)
    w_out = dram("w_out", [D, D], F32, inp=True)
    ln1g = dram("ln1_g", [1, D], F32, inp=True)
    ln1b = dram("ln1_b", [1, D], F32, inp=True)
    w_up = dram("w_up", [D, 2 * FFN], F32, inp=True)
    fcw = dram("ffn_conv_w", [128, 172, 3], F32, inp=True)
    fcb = dram("ffn_conv_b", [128, 172], F32, inp=True)
    w_down = dram("w_down", [FFN, D], F32, inp=True)
    ln2g = dram("ln2_g", [1, D], F32, inp=True)
    ln2b = dram("ln2_b", [1, D], F32, inp=True)
    cos_d = dram("rope_cos", [128, S], F32, inp=True)
    sin_d = dram("rope_sin", [128, S], F32, inp=True)
    cst = dram("consts", [128, 8, 128], F32, inp=True)
    out_d = dram("out", [NOUT, D], F32, out=True)
    modD = dram("modD", [1, 6 * D], F32)
    hT = dram("hT", [D, S], BF16)
    zsT = dram("zsT", [DI, NOWN], F32)
    xbcpre = dram("xbcpre", [NCONV, S], F32)
    dtraw = dram("dtraw", [256, S], F32)
    qT = dram("qT", [D, NOWN], F32)
    kT = dram("kT", [1024, S], F32)
    vT = dram("vT", [1024, S], BF16)
    gatesT = dram("gatesT", [2 * D, NOWN], F32)
    xs_tok = dram("xs_tok", [S, DI], BF16)
    B_tok = dram("B_tok", [S, 1024], BF16)
    BTd = dram("BTd", [1024, S], BF16)
    CTd = dram("CTd", [1024, S], BF16)
    dt_tok = dram("dt_tok", [S, 256], F32)
    a_tok = dram("a_tok", [S, 256], F32)
    ysum = dram("ysum", [NOWN, DI], F32)
    yssmT = dram("yssmT", [DI, NOWN], BF16)
    qTn = dram("qTn", [D, NOWN], BF16)
    kTn = dram("kTn", [1024, S], BF16)
    v_tok = dram("v_tok", [S, 1024], BF16)
    yattnT = dram("yattnT", [D, NOWN], BF16)
    t1T = dram("t1T", [D, NOWN], F32)
    mixT = dram("mixT", [D, NOWN], BF16)
    mixed = dram("mixed", [NOWN, D], F32)
    x1d = dram("x1d", [NOWN, D], F32)
    h2T = dram("h2T", [D, NOWN], BF16)
    upre = dram("upre", [2 * FFN, NOWN], F32)
    actT = dram("actT", [FFN, NOUT], BF16)
    fd = dram("fd", [NOUT, D], F32)

    with ExitStack() as es0:
        S_ = Sched(nc, es0)
        uid = [0]

        def sb(es, name, shape, dt):
            uid[0] += 1
            return es.enter_context(nc.sbuf_tensor("%s_%d" % (name, uid[0]), shape, dt))

        def ps(es, name, shape, dt):
            uid[0] += 1
            return es.enter_context(nc.psum_tensor("%s_%d" % (name, uid[0]), shape, dt))
        C32 = sb(es0, "C32", [128, 8, 128], F32)
        C16 = sb(es0, "C16", [128, 8, 128], BF16)
        modP = sb(es0, "modP", [128, 192], F32)
        bC32, bC16, bmodP = Buf(), Buf(), Buf()
        S_.dma("sp", C32[:], cst[:, :, :], writes=[bC32])
        S_.dma("pool", C16[:], cst[:, :, :], writes=[bC16])
        IDf, ONEf, TRIf, SUf, TRIb, SUb, ROPf = [C32[:, i, :] for i in range(7)]
        IDb, ONEb = C16[:, 0, :], C16[:, 1, :]
        TRI = (TRIf, TRIb)
        SU = (SUf, SUb)

        with ExitStack() as es:
            ct = sb(es, "ct", [128, KT], F32)
            condT = sb(es, "condT", [128, KT], BF16)
            wb = [sb(es, "wada%d" % i, [128, KT, 512], BF16) for i in range(2)]
            brow = [sb(es, "brow%d" % i, [1, 512], F32) for i in range(2)]
            mrow = [sb(es, "mrow%d" % i, [1, 512], F32) for i in range(2)]
            pm = [ps(es, "pm%d" % i, [1, 512], F32) for i in range(2)]
            pcol = ps(es, "pcol", [128, 192], F32)
            b_ct, b_cond, b_pcol = Buf(), Buf(), Buf()
            b_brow, b_mrow = [Buf(), Buf()], [Buf(), Buf()]
            b_wb = [Buf(), Buf()]
            b_pm = [Buf(), Buf()]
            S_.dma("sp", ct[:], c_d[:, :], writes=[b_ct])
            S_.op("act", lambda h: h.activation(out=condT[:], in_=ct[:], func=AF.Silu), reads=[b_ct], writes=[b_cond])
            wv = w_ada.rearrange("(k p) c -> p k c", p=128)
            for cb in range(48):
                i = cb % 2
                S_.dma("pool", wb[i][:], wv[:, :, cb * 512:(cb + 1) * 512], writes=[b_wb[i]])
                S_.dma("sp", brow[i][:], b_ada[:, cb * 512:(cb + 1) * 512], writes=[b_brow[i]])
                for kt in range(KT):
                    S_.op("pe", lambda h, kt=kt, i=i: h.matmul(pm[i][:], lhsT=condT[:, kt:kt + 1], rhs=wb[i][:, kt, :],
                                                                 start=(kt == 0), stop=(kt == KT - 1)),
                          reads=[b_cond, b_wb[i]], writes=[b_pm[i]], sig=(kt == KT - 1))
                S_.op("dve", lambda h, i=i: h.tensor_tensor(out=mrow[i][:], in0=pm[i][:], in1=brow[i][:], op=ALU.add),
                      reads=[b_pm[i], b_brow[i]], writes=[b_mrow[i]])
                S_.dma("sp", modD[:, cb * 512:(cb + 1) * 512], mrow[i][:], reads=[b_mrow[i]])
                for q in range(4):
                    j = cb * 4 + q
                    S_.op("pe", lambda h, j=j, q=q, i=i: h.matmul(pcol[:, j:j + 1], lhsT=mrow[i][0:1, q * 128:(q + 1) * 128],
                                                                    rhs=ONEf[0:1, 0:1], start=True, stop=True),
                          reads=[b_mrow[i], bC32], writes=[b_pcol], sig=(q == 3))
            S_.op("dve", lambda h: h.tensor_copy(out=modP[:], in_=pcol[:]), reads=[b_pcol], writes=[bmodP])
            S_.op("dve", lambda h: h.tensor_scalar(out=modP[:, 32:64], in0=modP[:, 32:64], scalar1=1.0, scalar2=None, op0=ALU.add),
                  reads=[bmodP], writes=[bmodP])
            S_.op("dve", lambda h: h.tensor_scalar(out=modP[:, 128:160], in0=modP[:, 128:160], scalar1=1.0, scalar2=None, op0=ALU.add),
                  reads=[bmodP], writes=[bmodP])
            S_.barrier()
        if stop_after <= 0:
            return finish(nc, S_, out_d)

        def ln_stats(es_tiles, xt, b_xt, st, mv, rstd, b_st):
            for j in range(8):
                S_.op("dve", lambda h, j=j: h.bn_stats(out=st[:, j, :], in_=xt[:, j * 512:(j + 1) * 512]),
                      reads=[b_xt], writes=[b_st])
            S_.op("dve", lambda h: h.bn_aggr(out=mv[:], in_=st[:].rearrange("p a b -> p (a b)")), reads=[b_st], writes=[b_st])
            S_.op("dve", lambda h: h.tensor_scalar(out=rstd[:], in0=mv[:, 1:2], scalar1=LN_EPS, scalar2=None, op0=ALU.add),
                  reads=[b_st], writes=[b_st])
            S_.op("act", lambda h: h.activation(out=rstd[:], in_=rstd[:], func=AF.Sqrt), reads=[b_st], writes=[b_st])
            S_.op("dve", lambda h: h.reciprocal(out=rstd[:], in_=rstd[:]), reads=[b_st], writes=[b_st])

        def modT(xn, b_xn, hblk, b_hblk, sub, pT, b_pT, sc0, sh0):
            for g in range(8):
                i = g % 2
                for q in range(4):
                    kt = g * 4 + q
                    S_.op("pe", lambda h, kt=kt, q=q, i=i: h.transpose(out=pT[i][:, q * 128:(q + 1) * 128],
                                                                       in_=xn[:, kt * 128:(kt + 1) * 128], identity=IDf),
                          reads=[b_xn, bC32], writes=[b_pT[i]], sig=(q == 3))
                for q in range(4):
                    kt = g * 4 + q
                    S_.op("act", lambda h, kt=kt, q=q, i=i: h.activation(
                        out=hblk[:, kt, sub * 128:(sub + 1) * 128], in_=pT[i][:, q * 128:(q + 1) * 128], func=AF.Identity,
                        scale=modP[:, sc0 + kt:sc0 + kt + 1], bias=modP[:, sh0 + kt:sh0 + kt + 1]),
                        reads=[b_pT[i], bmodP], writes=[b_hblk])

        with ExitStack() as es:
            xt = [sb(es, "xt%d" % i, [128, D], F32) for i in range(2)]
            xn = [sb(es, "xn%d" % i, [128, D], F32) for i in range(2)]
            st = sb(es, "st", [128, 8, 6], F32)
            mv = sb(es, "mv", [128, 2], F32)
            rstd = sb(es, "rstd", [128, 1], F32)
            hblk = [sb(es, "hblk%d" % i, [128, KT, 512], BF16) for i in range(2)]
            pT = [ps(es, "pT%d" % i, [128, 512], F32) for i in range(2)]
            b_xt, b_xn, b_hblk, b_pT = [Buf(), Buf()], [Buf(), Buf()], [Buf(), Buf()], [Buf(), Buf()]
            b_st = Buf()
            hTv = hT.rearrange("(k p) t -> p k t", p=128)
            for blk in range(8):
                hb = blk % 2
                for sub in range(4):
                    tt = blk * 4 + sub
                    i = tt % 2
                    S_.dma("sp", xt[i][:], x_d[tt * 128:(tt + 1) * 128, :], writes=[b_xt[i]])
                    ln_stats(es, xt[i], b_xt[i], st, mv, rstd, b_st)
                    S_.op("dve", lambda h, i=i: h.tensor_scalar(out=xn[i][:], in0=xt[i][:], scalar1=mv[:, 0:1], scalar2=rstd[:, 0:1],
                                                                 op0=ALU.subtract, op1=ALU.mult),
                          reads=[b_xt[i], b_st], writes=[b_xn[i]])
                    modT(xn[i], b_xn[i], hblk[hb], b_hblk[hb], sub, pT, b_pT, 32, 0)
                S_.dma("sp", hTv[:, :, blk * 512:(blk + 1) * 512], hblk[hb][:], reads=[b_hblk[hb]])
            S_.barrier()
        if stop_after <= 1:
            return finish(nc, S_, out_d)

        def mm_A(name, act_d, kt_n, tok0, ntok, W_d, col0, ncols, mk_evac, cw=256):
            with ExitStack() as es:
                S_.store_q = "sp"
                evac = mk_evac(es)
                A = sb(es, name + "_A", [128, kt_n, ntok], BF16)
                Wt = [sb(es, name + "_W%d" % i, [128, kt_n, cw], BF16) for i in range(2)]
                pp = [ps(es, name + "_p%d" % i, [128, 512], F32) for i in range(6)]
                b_A, b_W, b_pp = Buf(), [Buf(), Buf()], [Buf() for _ in range(6)]
                av = act_d.rearrange("(k p) t -> p k t", p=128)
                for k0 in range(0, kt_n, 8):
                    S_.dma("sp", A[:, k0:k0 + 8, :], av[:, k0:k0 + 8, tok0:tok0 + ntok], writes=[b_A])
                wv = W_d.rearrange("(k p) c -> p k c", p=128)
                tbs = tok_blocks(0, ntok)
                cnt = 0
                for wi, c0 in enumerate(range(col0, col0 + ncols, cw)):
                    i = wi % 2
                    S_.dma("pool", Wt[i][:], wv[:, :, c0:c0 + cw], writes=[b_W[i]])
                    for cc in range(cw // 128):
                        for (t0, n) in tbs:
                            pi = cnt % 6
                            cnt += 1
                            for kt in range(kt_n):
                                S_.op("pe", lambda h, kt=kt, i=i, cc=cc, t0=t0, n=n, pi=pi: h.matmul(
                                    pp[pi][:, 0:n], lhsT=Wt[i][:, kt, cc * 128:(cc + 1) * 128], rhs=A[:, kt, t0:t0 + n],
                                    start=(kt == 0), stop=(kt == kt_n - 1)),
                                    reads=[b_A, b_W[i]], writes=[b_pp[pi]], sig=(kt == kt_n - 1))
                            evac(es, pp[pi][:, 0:n], c0 + cc * 128, tok0 + t0, n, b_pp[pi], cnt)
                S_.barrier()
                S_.store_q = "pool"

        class Stage:
            def __init__(self, es, name, dt, n=4, w=512):
                self.t = [sb(es, "%s_st%d" % (name, i), [128, w], dt) for i in range(n)]
                self.b = [Buf() for _ in range(n)]
                self.i = 0

            def nxt(self):
                j = self.i % len(self.t)
                self.i += 1
                return self.t[j], self.b[j]

        def p2(tok0, ntok, full):
            def mk(es_):
                stg = {"f": Stage(es_, "p2f", F32), "h": Stage(es_, "p2h", BF16)}

                def evac(es, p, c, t, n, b_p, idx):
                    return evac_(stg, es, p, c, t, n, b_p, idx)
                return evac

            def evac_(stg, es, p, c, t, n, b_p, idx):
                eng = "act" if idx % 2 == 0 else "dve"
                if c < XBC0:
                    tl, bt = stg["f"].nxt()
                    S_.op("act", lambda h: h.activation(out=tl[:, 0:n], in_=p, func=AF.Silu), reads=[b_p], writes=[bt])
                    S_.dma("sp", zsT[c:c + 128, t:t + n], tl[:, 0:n], reads=[bt])
                    return
                if c >= V0:
                    tl, bt = stg["h"].nxt()
                    dst = vT[c - V0:c - V0 + 128, t:t + n]
                else:
                    tl, bt = stg["f"].nxt()
                    if c < DT0:
                        dst = xbcpre[c - XBC0:c - XBC0 + 128, t:t + n]
                    elif c < Q0:
                        dst = dtraw[c - DT0:c - DT0 + 128, t:t + n]
                    elif c < K0:
                        dst = qT[c - Q0:c - Q0 + 128, t:t + n]
                    else:
                        dst = kT[c - K0:c - K0 + 128, t:t + n]
                if eng == "act":
                    S_.op("act", lambda h: h.activation(out=tl[:, 0:n], in_=p, func=AF.Copy), reads=[b_p], writes=[bt])
                else:
                    S_.op("dve", lambda h: h.tensor_copy(out=tl[:, 0:n], in_=p), reads=[b_p], writes=[bt])
                S_.dma("sp", dst, tl[:, 0:n], reads=[bt])
            if full:
                mm_A("p2a", hT, KT, tok0, ntok, w_in, 0, IN_COLS, mk)
            else:
                mm_A("p2b", hT, KT, tok0, ntok, w_in, XBC0, 8192 + 1024, mk)
                mm_A("p2c", hT, KT, tok0, ntok, w_in, DT0, 256, mk)
                mm_A("p2d", hT, KT, tok0, ntok, w_in, K0, 2048, mk)

        p2(0, NOWN, True)
        p2(NOWN, S - NOWN, False)

        def p2g():
            stg = {}
            with ExitStack() as esb:
                bg = sb(esb, "bgate", [128, 64], F32)
                b_bg = Buf()
                S_.dma("sp", bg[:], b_gate[:, :], writes=[b_bg])

                def mk(es_):
                    st_ = Stage(es_, "pgf", F32)

                    def evac(es, p, c, t, n, b_p, idx):
                        tl, bt = st_.nxt()
                        S_.op("act", lambda h: h.activation(out=tl[:, 0:n], in_=p, func=AF.Sigmoid, bias=bg[:, c // 128:c // 128 + 1]),
                              reads=[b_p, b_bg], writes=[bt])
                        S_.dma("sp", gatesT[c:c + 128, t:t + n], tl[:, 0:n], reads=[bt])
                    return evac
                mm_A("pg", hT, KT, 0, NOWN, w_gate, 0, 2 * D, mk)
        p2g()
        if stop_after <= 2:
            return finish(nc, S_, out_d)

        def p3():
            with ExitStack() as es:
                xin = [sb(es, "cxin%d" % i, [128, S + 2], F32) for i in range(2)]
                acc = [sb(es, "cacc%d" % i, [128, S], F32) for i in range(2)]
                ob = [sb(es, "cob%d" % i, [128, S], BF16) for i in range(2)]
                cw = sb(es, "ccw", [128, 80, 3], F32)
                cbs = sb(es, "ccb", [128, 80], F32)
                stT = [sb(es, "cstT%d" % i, [128, 32, 128], BF16) for i in range(2)]
                pT = [ps(es, "cpT%d" % i, [128, 1024], BF16) for i in range(2)]
                pF = [ps(es, "cpF%d" % i, [128, 512], F32) for i in range(2)]
                tt = sb(es, "ctt", [128, S], F32)
                dtv = sb(es, "cdtv", [128, S], F32)
                stF = sb(es, "cstF", [128, 32, 128], F32)
                al = sb(es, "cal", [128, 2], F32)
                db = sb(es, "cdb", [128, 2], F32)
                b_xin, b_ob, b_stT, b_pT, b_pF = [Buf(), Buf()], [Buf(), Buf()], [Buf(), Buf()], [Buf(), Buf()], [Buf(), Buf()]
                b_cw, b_tt, b_dtv, b_stF, b_al = Buf(), Buf(), Buf(), Buf(), Buf()
                b_acc = [Buf(), Buf()]
                S_.dma("sp", cw[:], scw[:, :, :], writes=[b_cw])
                S_.dma("sp", cbs[:], scb[:, :], writes=[b_cw])
                S_.dma("sp", al[:], alog[:, :], writes=[b_al])
                S_.dma("sp", db[:], dtb[:, :], writes=[b_al])
                for i in range(2):
                    S_.op("dve", lambda h, i=i: h.memset(xin[i][:, 0:1], 0.0), writes=[b_xin[i]])
                    S_.op("dve", lambda h, i=i: h.memset(xin[i][:, S + 1:S + 2], 0.0), writes=[b_xin[i]])
                xsv = xs_tok.rearrange("(t p) c -> p t c", p=128)
                btv = B_tok.rearrange("(t p) c -> p t c", p=128)
                for blk in range(80):
                    i = blk % 2
                    nld = S if blk < 72 else NOWN
                    S_.dma("sp", xin[i][:, 1:nld + 1], xbcpre[blk * 128:(blk + 1) * 128, 0:nld], writes=[b_xin[i]])
                    S_.op("act", lambda h, i=i, blk=blk: h.activation(out=acc[i][:], in_=xin[i][:, 1:S + 1], func=AF.Identity,
                                                                        scale=cw[:, blk, 1:2], bias=cbs[:, blk:blk + 1]),
                          reads=[b_xin[i], b_cw], writes=[b_acc[i]])
                    S_.op("dve", lambda h, i=i, blk=blk: h.scalar_tensor_tensor(out=acc[i][:], in0=xin[i][:, 0:S], scalar=cw[:, blk, 0:1],
                                                                                  in1=acc[i][:], op0=ALU.mult, op1=ALU.add),
                          reads=[b_xin[i], b_cw], writes=[b_acc[i]])
                    S_.op("dve", lambda h, i=i, blk=blk: h.scalar_tensor_tensor(out=acc[i][:], in0=xin[i][:, 2:S + 2], scalar=cw[:, blk, 2:3],
                                                                                  in1=acc[i][:], op0=ALU.mult, op1=ALU.add),
                          reads=[b_xin[i], b_cw], writes=[b_acc[i]])
                    S_.op("act", lambda h, i=i: h.activation(out=ob[i][:], in_=acc[i][:], func=AF.Silu), reads=[b_acc[i]], writes=[b_ob[i]])
                    if blk < 72:
                        for g in range(4):
                            j = g % 2
                            for q in range(8):
                                tl = g * 8 + q
                                S_.op("pe", lambda h, tl=tl, q=q, j=j, i=i: h.transpose(out=pT[j][:, q * 128:(q + 1) * 128],
                                                                                         in_=ob[i][:, tl * 128:(tl + 1) * 128], identity=IDb),
                                      reads=[b_ob[i], bC16], writes=[b_pT[j]], sig=(q == 7))
                            if g % 2 == 0:
                                S_.op("act", lambda h, g=g, j=j, i=i: h.activation(out=stT[i][:, g * 8:(g + 1) * 8, :].rearrange("p a b -> p (a b)"),
                                                                                     in_=pT[j][:], func=AF.Copy),
                                      reads=[b_pT[j]], writes=[b_stT[i]])
                            else:
                                S_.op("dve", lambda h, g=g, j=j, i=i: h.tensor_copy(out=stT[i][:, g * 8:(g + 1) * 8, :].rearrange("p a b -> p (a b)"),
                                                                                      in_=pT[j][:]),
                                      reads=[b_pT[j]], writes=[b_stT[i]])
                        if blk < 64:
                            S_.dma("sp", xsv[:, :, blk * 128:(blk + 1) * 128], stT[i][:], reads=[b_stT[i]])
                        else:
                            S_.dma("sp", btv[:, :, (blk - 64) * 128:(blk - 63) * 128], stT[i][:], reads=[b_stT[i]])
                            S_.dma("sp", BTd[(blk - 64) * 128:(blk - 63) * 128, :], ob[i][:], reads=[b_ob[i]])
                    else:
                        S_.dma("sp", CTd[(blk - 72) * 128:(blk - 71) * 128, 0:NOWN], ob[i][:, 0:NOWN], reads=[b_ob[i]])
                S_.op("act", lambda h: h.activation(out=al[:], in_=al[:], func=AF.Exp), reads=[b_al], writes=[b_al])
                S_.op("dve", lambda h: h.tensor_scalar(out=al[:], in0=al[:], scalar1=-1.0, scalar2=None, op0=ALU.mult), reads=[b_al], writes=[b_al])
                dtv_ = dt_tok.rearrange("(t p) c -> p t c", p=128)
                atv_ = a_tok.rearrange("(t p) c -> p t c", p=128)
                for d in range(2):
                    i = d
                    S_.dma("sp", xin[i][:, 1:S + 1], dtraw[d * 128:(d + 1) * 128, :], writes=[b_xin[i]])
                    S_.op("act", lambda h, i=i, d=d: h.activation(out=acc[i][:], in_=xin[i][:, 1:S + 1], func=AF.Identity, bias=db[:, d:d + 1]),
                          reads=[b_xin[i], b_al], writes=[b_acc[i]])
                    S_.op("act", lambda h, i=i: h.activation(out=tt[:], in_=acc[i][:], func=AF.Abs), reads=[b_acc[i]], writes=[b_tt])
                    S_.op("act", lambda h: h.activation(out=tt[:], in_=tt[:], func=AF.Exp, scale=-1.0), reads=[b_tt], writes=[b_tt])
                    S_.op("act", lambda h: h.activation(out=tt[:], in_=tt[:], func=AF.Ln, bias=1.0), reads=[b_tt], writes=[b_tt])
                    S_.op("dve", lambda h, i=i: h.scalar_tensor_tensor(out=dtv[:], in0=acc[i][:], scalar=0.0, in1=tt[:], op0=ALU.max, op1=ALU.add),
                          reads=[b_acc[i], b_tt], writes=[b_dtv])
                    S_.op("dve", lambda h, d=d: h.tensor_scalar(out=tt[:], in0=dtv[:], scalar1=al[:, d:d + 1], scalar2=None, op0=ALU.mult),
                          reads=[b_dtv, b_al], writes=[b_tt])
                    for (src, b_src, dstv) in ((dtv, b_dtv, dtv_), (tt, b_tt, atv_)):
                        for g in range(8):
                            j = g % 2
                            for q in range(4):
                                tl = g * 4 + q
                                S_.op("pe", lambda h, tl=tl, q=q, j=j, src=src: h.transpose(out=pF[j][:, q * 128:(q + 1) * 128],
                                                                                             in_=src[:, tl * 128:(tl + 1) * 128], identity=IDf),
                                      reads=[b_src, bC32], writes=[b_pF[j]], sig=(q == 3))
                            S_.op("dve", lambda h, g=g, j=j: h.tensor_copy(out=stF[:, g * 4:(g + 1) * 4, :].rearrange("p a b -> p (a b)"), in_=pF[j][:]),
                                  reads=[b_pF[j]], writes=[b_stF])
                        S_.dma("sp", dstv[:, :, d * 128:(d + 1) * 128], stF[:], reads=[b_stF])
                S_.barrier()
        p3()
        if stop_after <= 3:
            return finish(nc, S_, out_d)

        def p4():
            with ExitStack() as es:
                H = sb(es, "sH", [128, 8, 1024], F32)
                Hb = sb(es, "sHb", [128, 8, 1024], BF16)
                xs = [sb(es, "sxs%d" % i, [128, DI], BF16) for i in range(2)]
                xdt = [sb(es, "sxdt%d" % i, [128, DI], BF16) for i in range(2)]
                xw = [sb(es, "sxw%d" % i, [128, DI], BF16) for i in range(2)]
                bt = [sb(es, "sbt%d" % i, [128, 1024], BF16) for i in range(2)]
                BTc = [sb(es, "sBT%d" % i, [128, 8, 128], BF16) for i in range(2)]
                CTc = [sb(es, "sCT%d" % i, [128, 8, 128], BF16) for i in range(2)]
                dts = [sb(es, "sdt%d" % i, [128, 128], F32) for i in range(2)]
                as_ = [sb(es, "sa%d" % i, [128, 128], F32) for i in range(2)]
                toend = [sb(es, "stoend%d" % i, [128, 128], F32) for i in range(2)]
                eL = [sb(es, "seL%d" % i, [128, 128], F32) for i in range(2)]
                ecum = [sb(es, "secum%d" % i, [128, 128], F32) for i in range(2)]
                wts = [sb(es, "swts%d" % i, [128, 128], F32) for i in range(2)]
                Dbc = sb(es, "sDbc", [128, 128], F32)
                segr = [sb(es, "ssegr%d" % i, [128, 8, 128], F32) for i in range(3)]
                E = [sb(es, "sE%d" % i, [128, 8, 128], BF16) for i in range(2)]
                MT = [sb(es, "sMT%d" % i, [128, 8, 128], BF16) for i in range(2)]
                CBm = [sb(es, "sCBm%d" % i, [128, 128], F32) for i in range(2)]
                tmpy = [sb(es, "stmpy%d" % i, [128, 512], F32) for i in range(1)] * 2
                tmph = [sb(es, "stmph%d" % i, [128, 512], F32) for i in range(1)] * 2
                xD = [sb(es, "sxD%d" % i, [128, 512], BF16) for i in range(3)]
                yst = [sb(es, "syst%d" % i, [128, 512], F32) for i in range(2)]
                yld = [sb(es, "syld%d" % i, [128, 512], F32) for i in range(3)]
                pseg = [ps(es, "spseg%d" % i, [128, 1024], F32) for i in range(2)]
                pyi = [ps(es, "spyi%d" % i, [128, 512], F32) for i in range(2)]
                pyo = ps(es, "spyo", [128, 512], F32)
                psu = ps(es, "spsu", [128, 512], F32)
                (b_xs, b_xdt, b_xw, b_bt, b_BC, b_da, b_sm, b_segr, b_E, b_MT, b_CBm, b_tmpy, b_tmph, b_xD,
                 b_yst, b_yld, b_pseg, b_pyi) = ([Buf(), Buf(), Buf()] for _ in range(18))
                b_D, b_pyo, b_psu = Buf(), Buf(), Buf()
                b_tmpy = [b_tmpy[0]] * 2
                b_tmph = [b_tmph[0]] * 2
                b_H = [[Buf(), Buf()] for _ in range(8)]
                b_Hb = [[Buf(), Buf()] for _ in range(8)]
                S_.store_q = "pool"
                S_.dma("sp", Dbc[:], ssmd[0:1, :].partition_broadcast(128).rearrange("p a b -> p (a b)"), writes=[b_D])
                btv = BTd.rearrange("(g n) t -> n g t", n=128)
                ctv = CTd.rearrange("(g n) t -> n g t", n=128)
                v3 = lambda ap: ap.rearrange("p (h q) -> p h q", q=64)
                citer = [0]

                def prologue(d, c, i, full, last):
                    r0 = c * 128
                    S_.dma("sp", xs[i][:], xs_tok[r0:r0 + 128, :], writes=[b_xs[i]])
                    S_.dma("sp", bt[i][:], B_tok[r0:r0 + 128, :], writes=[b_bt[i]])
                    S_.dma("sp", dts[i][:], dt_tok[r0:r0 + 128, d * 128:(d + 1) * 128], writes=[b_da[i]])
                    S_.dma("sp", as_[i][:], a_tok[r0:r0 + 128, d * 128:(d + 1) * 128], writes=[b_da[i]])
                    if full:
                        S_.dma("sp", BTc[i][:], btv[:, :, r0:r0 + 128], writes=[b_BC[i]])
                        S_.dma("sp", CTc[i][:], ctv[:, :, r0:r0 + 128], writes=[b_BC[i]])
                    for q, cm in enumerate((SU[d], ONEf, TRI[d])):
                        S_.op("pe", lambda h, q=q, cm=cm: h.matmul(psu[:, 128 + q * 128:256 + q * 128], lhsT=cm, rhs=as_[i][:], start=True, stop=True),
                              reads=[b_da[i], bC32], writes=[b_psu], sig=(q == 2))
                    for q, dst in enumerate((toend[i], eL[i], ecum[i])):
                        S_.op("act", lambda h, q=q, dst=dst: h.activation(out=dst[:], in_=psu[:, 128 + q * 128:256 + q * 128], func=AF.Exp),
                              reads=[b_psu], writes=[b_sm[i]])
                    S_.op("dve", lambda h: h.tensor_tensor(out=wts[i][:], in0=toend[i][:], in1=dts[i][:], op=ALU.mult),
                          reads=[b_sm[i], b_da[i]], writes=[b_sm[i]])
                    if full:
                        S_.op("dve", lambda h: h.tensor_tensor(out=v3(xdt[i][:]), in0=v3(xs[i][:]),
                                                                in1=dts[i][:].unsqueeze(2).broadcast_to([128, 128, 64]), op=ALU.mult),
                              reads=[b_xs[i], b_da[i]], writes=[b_xdt[i]])
                    if not last:
                        S_.op("dve", lambda h: h.tensor_tensor(out=v3(xw[i][:]), in0=v3(xs[i][:]),
                                                                in1=wts[i][:].unsqueeze(2).broadcast_to([128, 128, 64]), op=ALU.mult),
                              reads=[b_xs[i], b_sm[i]], writes=[b_xw[i]])

                def stage_a0(u, un):
                    d, c, i, g, hg, k, full, last, first = u
                    if not full:
                        return
                    k3 = un % 3
                    if hg == 0:
                        S_.op("pe", lambda h: h.matmul(psu[:, 0:128], lhsT=BTc[i][:, g, :], rhs=CTc[i][:, g, :], start=True, stop=True),
                              reads=[b_BC[i]], writes=[b_psu])
                        S_.op("dve", lambda h: h.tensor_tensor(out=CBm[g % 2][:], in0=psu[:, 0:128], in1=TRI[d], op=ALU.mult),
                              reads=[b_psu, bC32], writes=[b_CBm[g % 2]])
                    h0 = g * 16 + hg * 8
                    if d == 1:
                        S_.dma("sp", yld[k3][:], ysum[c * 128:(c + 1) * 128, g * 1024 + hg * 512:g * 1024 + (hg + 1) * 512], writes=[b_yld[k3]])
                    else:
                        S_.op("dve", lambda h: h.tensor_tensor(out=v3(xD[k3][:]), in0=v3(xs[i][:, g * 1024 + hg * 512:g * 1024 + (hg + 1) * 512]),
                                                               in1=Dbc[:, h0:h0 + 8].unsqueeze(2).broadcast_to([128, 8, 64]), op=ALU.mult),
                              reads=[b_xs[i], b_D], writes=[b_xD[k3]])
                    for j in range(8):
                        S_.op("act", lambda h, j=j: h.activation(out=segr[k3][:, j, :], in_=TRI[d], func=AF.Copy, scale=as_[i][:, h0 + j:h0 + j + 1]),
                              reads=[b_da[i], bC32], writes=[b_segr[k3]])

                def stage_a1(u, un):
                    d, c, i, g, hg, k, full, last, first = u
                    if not full:
                        return
                    k3 = un % 3
                    sflat = segr[k3][:].rearrange("p a b -> p (a b)")
                    eflat = E[k][:].rearrange("p a b -> p (a b)")
                    for q in range(2):
                        S_.op("pe", lambda h, q=q: h.matmul(pseg[k][:, q * 512:(q + 1) * 512], lhsT=SU[d], rhs=sflat[:, q * 512:(q + 1) * 512],
                                                            start=True, stop=True),
                              reads=[b_segr[k3], bC32], writes=[b_pseg[k]], sig=(q == 1))
                    for q in range(2):
                        S_.op("act", lambda h, q=q: h.activation(out=eflat[:, q * 512:(q + 1) * 512], in_=pseg[k][:, q * 512:(q + 1) * 512], func=AF.Exp),
                              reads=[b_pseg[k]], writes=[b_E[k]])

                def stage_b(u, un):
                    d, c, i, g, hg, k, full, last, first = u
                    k3 = un % 3
                    r0 = c * 128
                    h0 = g * 16 + hg * 8
                    c0 = g * 1024 + hg * 512
                    if full:
                        S_.op("dve", lambda h: h.tensor_tensor(out=MT[k][:], in0=E[k][:], in1=CBm[g % 2][:].unsqueeze(1).broadcast_to([128, 8, 128]), op=ALU.mult),
                              reads=[b_E[k], b_CBm[g % 2]], writes=[b_MT[k]])
                        if d == 0:
                            S_.op("pe", lambda h: h.matmul(pyi[k][:, :], lhsT=IDb, rhs=xD[k3][:], start=True, stop=False),
                                  reads=[b_xD[k3], bC16], writes=[b_pyi[k]], sig=False)
                        else:
                            S_.op("pe", lambda h: h.matmul(pyi[k][:, :], lhsT=IDf, rhs=yld[k3][:], start=True, stop=False),
                                  reads=[b_yld[k3], bC32], writes=[b_pyi[k]], sig=False)
                        for j in range(8):
                            S_.op("pe", lambda h, j=j: h.matmul(pyi[k][:, j * 64:(j + 1) * 64], lhsT=MT[k][:, j, :], rhs=xdt[i][:, (h0 + j) * 64:(h0 + j + 1) * 64],
                                                                start=False, stop=(j == 7)),
                                  reads=[b_MT[k], b_xdt[i]], writes=[b_pyi[k]], sig=(j == 7))
                        S_.op("pe", lambda h: h.matmul(pyo[:, :], lhsT=CTc[i][:, g, :], rhs=Hb[:, g, hg * 512:(hg + 1) * 512], start=True, stop=True),
                              reads=[b_BC[i], b_Hb[g][hg]], writes=[b_pyo])
                        S_.op("dve", lambda h: h.tensor_tensor(out=v3(tmpy[k][:]), in0=v3(pyo[:, :]),
                                                               in1=ecum[i][:, h0:h0 + 8].unsqueeze(2).broadcast_to([128, 8, 64]), op=ALU.mult),
                              reads=[b_pyo, b_sm[i]], writes=[b_tmpy[k]])
                        S_.op("dve", lambda h: h.tensor_tensor(out=yst[k][:], in0=tmpy[k][:], in1=pyi[k][:], op=ALU.add),
                              reads=[b_tmpy[k], b_pyi[k]], writes=[b_yst[k]])
                        S_.dma("sp", ysum[r0:r0 + 128, c0:c0 + 512], yst[k][:], reads=[b_yst[k]])
                    if not last:
                        S_.op("pe", lambda h: h.matmul(psu[:, :], lhsT=bt[i][:, g * 128:(g + 1) * 128], rhs=xw[i][:, c0:c0 + 512], start=True, stop=True),
                              reads=[b_bt[i], b_xw[i]], writes=[b_psu])
                        S_.op("dve", lambda h: h.tensor_tensor(out=v3(tmph[k][:]), in0=v3(H[:, g, hg * 512:(hg + 1) * 512]),
                                                               in1=eL[i][:, h0:h0 + 8].unsqueeze(2).broadcast_to([128, 8, 64]), op=ALU.mult),
                              reads=[b_H[g][hg], b_sm[i]], writes=[b_tmph[k]])
                        S_.op("dve", lambda h: h.tensor_tensor(out=H[:, g, hg * 512:(hg + 1) * 512], in0=tmph[k][:], in1=psu[:, :], op=ALU.add),
                              reads=[b_tmph[k], b_psu], writes=[b_H[g][hg]])
                        S_.op("act", lambda h: h.activation(out=Hb[:, g, hg * 512:(hg + 1) * 512], in_=H[:, g, hg * 512:(hg + 1) * 512], func=AF.Copy),
                              reads=[b_H[g][hg]], writes=[b_Hb[g][hg]])

                for d in range(2):
                    for g in range(8):
                        for hg in range(2):
                            S_.op("dve", lambda h, g=g, hg=hg: h.memset(H[:, g, hg * 512:(hg + 1) * 512], 0.0), writes=[b_H[g][hg]])
                            S_.op("dve", lambda h, g=g, hg=hg: h.memset(Hb[:, g, hg * 512:(hg + 1) * 512], 0.0), writes=[b_Hb[g][hg]])
                    chunks = list(range(17)) if d == 0 else list(range(31, -1, -1))
                    units = []
                    cinfo = []
                    for c in chunks:
                        i = citer[0] % 2
                        citer[0] += 1
                        cinfo.append((d, c, i, c <= 16, c == chunks[-1]))
                        for g in range(8):
                            for hg in range(2):
                                units.append((d, c, i, g, hg, len(units) % 2, c <= 16, c == chunks[-1], g == 0 and hg == 0))
                    prologue(*cinfo[0])
                    prologue(*cinfo[1])
                    stage_a0(units[0], 0)
                    stage_a0(units[1], 1)
                    stage_a1(units[0], 0)
                    for ui, u in enumerate(units):
                        if ui + 2 < len(units):
                            stage_a0(units[ui + 2], ui + 2)
                        if ui + 1 < len(units):
                            stage_a1(units[ui + 1], ui + 1)
                        stage_b(u, ui)
                        if ui % 16 == 15:
                            ci = ui // 16
                            if ci + 2 < len(cinfo):
                                prologue(*cinfo[ci + 2])
                    S_.barrier()
                S_.store_q = "pool"
        p4()
        if stop_after <= 4:
            return finish(nc, S_, out_d)

        def p5():
            with ExitStack() as es:
                nw = sb(es, "g5nw", [128, 64], F32)
                ys = [sb(es, "g5ys%d" % i, [128, 4, 1024], F32) for i in range(2)]
                zs = [sb(es, "g5zs%d" % i, [128, 8, 512], F32) for i in range(2)]
                G = sb(es, "g5G", [128, 8, 512], F32)
                sq = [sb(es, "g5sq%d" % i, [128, 512], F32) for i in range(2)]
                rr = sb(es, "g5rr", [128, 512], F32)
                ob = [sb(es, "g5o%d" % i, [128, 512], BF16) for i in range(2)]
                pY = [ps(es, "g5pY%d" % i, [128, 512], F32) for i in range(2)]
                pSS = ps(es, "g5pSS", [128, 512], F32)
                b_nw, b_G, b_rr, b_pSS = Buf(), Buf(), Buf(), Buf()
                b_ys, b_zs, b_sq, b_ob, b_pY = ([Buf(), Buf()] for _ in range(5))
                S_.dma("sp", nw[:], snw[:, :], writes=[b_nw])
                ysv = ysum.rearrange("(t p) c -> p t c", p=128)
                zsv = zsT.rearrange("(c p) t -> p c t", p=128)
                it = 0
                for (t0, n) in tok_blocks(0, NOWN):
                    nt = n // 128
                    for g in range(8):
                        i = it % 2
                        it += 1
                        S_.dma("sp", ys[i][:, 0:nt, :], ysv[:, t0 // 128:t0 // 128 + nt, g * 1024:(g + 1) * 1024], writes=[b_ys[i]])
                        S_.dma("sp", zs[i][:, :, 0:n], zsv[:, g * 8:(g + 1) * 8, t0:t0 + n], writes=[b_zs[i]])
                        for ct in range(8):
                            j = ct % 2
                            for q in range(nt):
                                S_.op("pe", lambda h, q=q, ct=ct, j=j, i=i: h.transpose(out=pY[j][:, q * 128:(q + 1) * 128],
                                                                                         in_=ys[i][:, q, ct * 128:(ct + 1) * 128], identity=IDf),
                                      reads=[b_ys[i], bC32], writes=[b_pY[j]], sig=(q == nt - 1))
                            S_.op("dve", lambda h, ct=ct, j=j, i=i, n=n: h.tensor_tensor(out=G[:, ct, 0:n], in0=pY[j][:, 0:n], in1=zs[i][:, ct, 0:n], op=ALU.mult),
                                  reads=[b_pY[j], b_zs[i]], writes=[b_G])
                            S_.op("act", lambda h, ct=ct, j=j, n=n: h.activation(out=sq[j][:, 0:n], in_=G[:, ct, 0:n], func=AF.Square),
                                  reads=[b_G], writes=[b_sq[j]])
                            S_.op("pe", lambda h, ct=ct, j=j, n=n: h.matmul(pSS[:, 0:n], lhsT=ONEf, rhs=sq[j][:, 0:n], start=(ct == 0), stop=(ct == 7)),
                                  reads=[b_sq[j], bC32], writes=[b_pSS], sig=True)
                        S_.op("dve", lambda h, n=n: h.tensor_scalar(out=rr[:, 0:n], in0=pSS[:, 0:n], scalar1=1.0 / 1024.0, scalar2=RMS_EPS,
                                                                     op0=ALU.mult, op1=ALU.add), reads=[b_pSS], writes=[b_rr])
                        S_.op("act", lambda h, n=n: h.activation(out=rr[:, 0:n], in_=rr[:, 0:n], func=AF.Sqrt), reads=[b_rr], writes=[b_rr])
                        S_.op("dve", lambda h, n=n: h.reciprocal(out=rr[:, 0:n], in_=rr[:, 0:n]), reads=[b_rr], writes=[b_rr])
                        for ct in range(8):
                            j = ct % 2
                            cg = g * 8 + ct
                            S_.op("dve", lambda h, ct=ct, j=j, cg=cg, n=n: h.scalar_tensor_tensor(out=ob[j][:, 0:n], in0=G[:, ct, 0:n], scalar=nw[:, cg:cg + 1],
                                                                                                 in1=rr[:, 0:n], op0=ALU.mult, op1=ALU.mult),
                                  reads=[b_G, b_rr, b_nw], writes=[b_ob[j]])
                            S_.dma("sp", yssmT[cg * 128:(cg + 1) * 128, t0:t0 + n], ob[j][:, 0:n], reads=[b_ob[j]])
                S_.barrier()
        p5()
        if stop_after <= 5:
            return finish(nc, S_, out_d)

        def p6():
            with ExitStack() as es:
                gq = sb(es, "gq", [128, 1], F32)
                gk = sb(es, "gk", [128, 1], F32)
                xq = [sb(es, "a6x%d" % i, [128, 512], F32) for i in range(2)]
                cs = [sb(es, "a6c%d" % i, [128, 512], F32) for i in range(2)]
                sn = [sb(es, "a6s%d" % i, [128, 512], F32) for i in range(2)]
                sq = [sb(es, "a6sq%d" % i, [128, 512], F32) for i in range(2)]
                rr = [sb(es, "a6r%d" % i, [128, 512], F32) for i in range(2)]
                xn = [sb(es, "a6xn%d" % i, [128, 512], F32) for i in range(2)]
                t1 = [sb(es, "a6t1%d" % i, [128, 512], F32) for i in range(2)]
                t2 = [sb(es, "a6t2%d" % i, [128, 512], F32) for i in range(2)]
                ob = [sb(es, "a6o%d" % i, [128, 512], BF16) for i in range(2)]
                pS = [ps(es, "a6pS%d" % i, [128, 512], F32) for i in range(2)]
                pR = [ps(es, "a6pR%d" % i, [128, 512], F32) for i in range(2)]
                vrow = [sb(es, "a6v%d" % i, [128, S], BF16) for i in range(2)]
                vst = [sb(es, "a6vs%d" % i, [128, 32, 128], BF16) for i in range(2)]
                pT = [ps(es, "a6pT%d" % i, [128, 1024], BF16) for i in range(2)]
                b_g = Buf()
                (b_xq, b_cs, b_ob, b_vrow, b_vst, b_pT, b_sq, b_rr, b_xn, b_t1, b_t2, b_pS, b_pR) = ([Buf(), Buf()] for _ in range(13))
                S_.dma("sp", gq[:], qnw[:, :], writes=[b_g])
                S_.dma("sp", gk[:], knw[:, :], writes=[b_g])
                it = 0
                for (src, dst, nh, ntok, g) in ((kT, kTn, 8, S, gk), (qT, qTn, 32, NOWN, gq)):
                    for hd in range(nh):
                        for (t0, n) in tok_blocks(0, ntok):
                            i = it % 2
                            it += 1
                            S_.dma("sp", xq[i][:, 0:n], src[hd * 128:(hd + 1) * 128, t0:t0 + n], writes=[b_xq[i]])
                            S_.dma("sp", cs[i][:, 0:n], cos_d[:, t0:t0 + n], writes=[b_cs[i]])
                            S_.dma("sp", sn[i][:, 0:n], sin_d[:, t0:t0 + n], writes=[b_cs[i]])
                            S_.op("act", lambda h, i=i, n=n: h.activation(out=sq[i][:, 0:n], in_=xq[i][:, 0:n], func=AF.Square),
                                  reads=[b_xq[i]], writes=[b_sq[i]])
                            S_.op("pe", lambda h, i=i, n=n: h.matmul(pS[i][:, 0:n], lhsT=ONEf, rhs=sq[i][:, 0:n], start=True, stop=True),
                                  reads=[b_sq[i], bC32], writes=[b_pS[i]])
                            S_.op("dve", lambda h, i=i, n=n: h.tensor_scalar(out=rr[i][:, 0:n], in0=pS[i][:, 0:n], scalar1=1.0 / 128.0, scalar2=RMS_EPS,
                                                                              op0=ALU.mult, op1=ALU.add), reads=[b_pS[i]], writes=[b_rr[i]])
                            S_.op("act", lambda h, i=i, n=n: h.activation(out=rr[i][:, 0:n], in_=rr[i][:, 0:n], func=AF.Sqrt), reads=[b_rr[i]], writes=[b_rr[i]])
                            S_.op("dve", lambda h, i=i, n=n: h.reciprocal(out=rr[i][:, 0:n], in_=rr[i][:, 0:n]), reads=[b_rr[i]], writes=[b_rr[i]])
                            S_.op("dve", lambda h, i=i, n=n, g=g: h.scalar_tensor_tensor(out=xn[i][:, 0:n], in0=xq[i][:, 0:n], scalar=g[:, 0:1],
                                                                                         in1=rr[i][:, 0:n], op0=ALU.mult, op1=ALU.mult),
                                  reads=[b_xq[i], b_rr[i], b_g], writes=[b_xn[i]])
                            S_.op("pe", lambda h, i=i, n=n: h.matmul(pR[i][:, 0:n], lhsT=ROPf, rhs=xn[i][:, 0:n], start=True, stop=True),
                                  reads=[b_xn[i], bC32], writes=[b_pR[i]])
                            S_.op("dve", lambda h, i=i, n=n: h.tensor_tensor(out=t1[i][:, 0:n], in0=xn[i][:, 0:n], in1=cs[i][:, 0:n], op=ALU.mult),
                                  reads=[b_xn[i], b_cs[i]], writes=[b_t1[i]])
                            S_.op("dve", lambda h, i=i, n=n: h.tensor_tensor(out=t2[i][:, 0:n], in0=pR[i][:, 0:n], in1=sn[i][:, 0:n], op=ALU.mult),
                                  reads=[b_pR[i], b_cs[i]], writes=[b_t2[i]])
                            S_.op("dve", lambda h, i=i, n=n: h.tensor_tensor(out=ob[i][:, 0:n], in0=t1[i][:, 0:n], in1=t2[i][:, 0:n], op=ALU.add),
                                  reads=[b_t1[i], b_t2[i]], writes=[b_ob[i]])
                            S_.dma("sp", dst[hd * 128:(hd + 1) * 128, t0:t0 + n], ob[i][:, 0:n], reads=[b_ob[i]])
                vtv = v_tok.rearrange("(t p) c -> p t c", p=128)
                for hd in range(8):
                    i = hd % 2
                    S_.dma("sp", vrow[i][:], vT[hd * 128:(hd + 1) * 128, :], writes=[b_vrow[i]])
                    for g in range(4):
                        j = g % 2
                        for q in range(8):
                            tl = g * 8 + q
                            S_.op("pe", lambda h, tl=tl, q=q, j=j, i=i: h.transpose(out=pT[j][:, q * 128:(q + 1) * 128],
                                                                                     in_=vrow[i][:, tl * 128:(tl + 1) * 128], identity=IDb),
                                  reads=[b_vrow[i], bC16], writes=[b_pT[j]], sig=(q == 7))
                        S_.op("dve", lambda h, g=g, j=j, i=i: h.tensor_copy(out=vst[i][:, g * 8:(g + 1) * 8, :].rearrange("p a b -> p (a b)"), in_=pT[j][:]),
                              reads=[b_pT[j]], writes=[b_vst[i]])
                    S_.dma("sp", vtv[:, :, hd * 128:(hd + 1) * 128], vst[i][:], reads=[b_vst[i]])
                S_.barrier()
        p6()
        if stop_after <= 6:
            return finish(nc, S_, out_d)

        def p7():
            scale = 128.0 ** -0.5
            with ExitStack() as es:
                ksb = [sb(es, "a7k%d" % i, [128, S], BF16) for i in range(2)]
                vsb = [sb(es, "a7v%d" % i, [128, 32, 128], BF16) for i in range(2)]
                qsb = [sb(es, "a7q%d" % i, [128, 512], BF16) for i in range(2)]
                pt = [sb(es, "a7p%d" % i, [128, 2, 512], BF16) for i in range(3)]
                rec = sb(es, "a7rec", [128, 512], F32)
                lacc = sb(es, "a7lacc", [128, 512], F32)
                osb = [sb(es, "a7o%d" % i, [128, 512], BF16) for i in range(2)]
                pS = [ps(es, "a7pS%d" % i, [128, 2, 512], F32) for i in range(2)]
                pO = [ps(es, "a7pO%d" % i, [128, 512], F32) for i in range(2)]
                pL = [ps(es, "a7pL%d" % i, [128, 512], F32) for i in range(2)]
                b_k, b_v, b_q, b_o, b_pO, b_pL, b_pS = ([Buf(), Buf()] for _ in range(7))
                b_pt = [Buf() for _ in range(3)]
                b_rec, b_lacc = Buf(), Buf()
                vtv = v_tok.rearrange("(t p) c -> p t c", p=128)
                it = 0
                sc = 0
                for kvh in range(8):
                    ki = kvh % 2
                    S_.dma("sp", ksb[ki][:], kTn[kvh * 128:(kvh + 1) * 128, :], writes=[b_k[ki]])
                    S_.dma("sp", vsb[ki][:], vtv[:, :, kvh * 128:(kvh + 1) * 128], writes=[b_v[ki]])
                    for qh in range(4):
                        hd = kvh * 4 + qh
                        for (t0, n) in tok_blocks(0, NOWN):
                            i = it % 2
                            it += 1
                            S_.dma("sp", qsb[i][:, 0:n], qTn[hd * 128:(hd + 1) * 128, t0:t0 + n], writes=[b_q[i]])

                            def smm(pi, i=i, n=n, ki=ki):
                                js, jp = (sc + pi) % 2, (sc + pi) % 3
                                for e in range(2):
                                    kt = 2 * pi + e
                                    S_.op("pe", lambda h, e=e, kt=kt: h.matmul(pS[js][:, e, 0:n], lhsT=ksb[ki][:, kt * 128:(kt + 1) * 128], rhs=qsb[i][:, 0:n],
                                                                               start=True, stop=True), reads=[b_k[ki], b_q[i]], writes=[b_pS[js]], sig=(e == 1))
                                S_.op("act", lambda h: h.activation(out=pt[jp][:, :, 0:n], in_=pS[js][:, :, 0:n], func=AF.Exp, scale=scale),
                                      reads=[b_pS[js]], writes=[b_pt[jp]])

                            def pv(pi, i=i, n=n, ki=ki):
                                jp = (sc + pi) % 3
                                for e in range(2):
                                    kt = 2 * pi + e
                                    S_.op("pe", lambda h, e=e, kt=kt: h.matmul(pO[i][:, 0:n], lhsT=vsb[ki][:, kt, :], rhs=pt[jp][:, e, 0:n],
                                                                               start=(kt == 0), stop=(kt == 31)), reads=[b_v[ki], b_pt[jp]], writes=[b_pO[i]], sig=(kt == 31))
                                S_.op("pe", lambda h: h.matmul(pL[i][:, 0:n], lhsT=ONEb, rhs=pt[jp][:, 0, 0:n], start=(pi == 0), stop=False),
                                      reads=[bC16, b_pt[jp]], writes=[b_pL[i]], sig=True)
                                if pi == 0:
                                    S_.op("dve", lambda h: h.tensor_copy(out=lacc[:, 0:n], in_=pt[jp][:, 1, 0:n]), reads=[b_pt[jp]], writes=[b_lacc])
                                else:
                                    S_.op("dve", lambda h: h.tensor_tensor(out=lacc[:, 0:n], in0=lacc[:, 0:n], in1=pt[jp][:, 1, 0:n], op=ALU.add),
                                          reads=[b_pt[jp]], writes=[b_lacc])
                            smm(0)
                            smm(1)
                            for pi in range(16):
                                pv(pi)
                                if pi + 2 < 16:
                                    smm(pi + 2)
                            sc += 16
                            S_.op("pe", lambda h, i=i, n=n: h.matmul(pL[i][:, 0:n], lhsT=ONEf, rhs=lacc[:, 0:n], start=False, stop=True),
                                  reads=[bC32, b_lacc], writes=[b_pL[i]])
                            S_.op("dve", lambda h, i=i, n=n: h.reciprocal(out=rec[:, 0:n], in_=pL[i][:, 0:n]), reads=[b_pL[i]], writes=[b_rec])
                            S_.op("dve", lambda h, i=i, n=n: h.tensor_tensor(out=osb[i][:, 0:n], in0=pO[i][:, 0:n], in1=rec[:, 0:n], op=ALU.mult),
                                  reads=[b_pO[i], b_rec], writes=[b_o[i]])
                            S_.dma("sp", yattnT[hd * 128:(hd + 1) * 128, t0:t0 + n], osb[i][:, 0:n], reads=[b_o[i]])
                S_.barrier()
        p7()
        if stop_after <= 7:
            return finish(nc, S_, out_d)

        def p8():
            def mk1(es_):
                gs, ts = Stage(es_, "p8g", F32), Stage(es_, "p8t", F32)

                def evac(es, p, c, t, n, b_p, idx):
                    gl, bg_ = gs.nxt()
                    tl, bt_ = ts.nxt()
                    S_.dma("sp", gl[:, 0:n], gatesT[c:c + 128, t:t + n], writes=[bg_])
                    S_.op("dve", lambda h: h.tensor_tensor(out=tl[:, 0:n], in0=p, in1=gl[:, 0:n], op=ALU.mult), reads=[b_p, bg_], writes=[bt_])
                    S_.dma("sp", t1T[c:c + 128, t:t + n], tl[:, 0:n], reads=[bt_])
                return evac
            mm_A("p8a", yssmT, 64, 0, 1088, w_sp, 0, D, mk1, cw=128)
            mm_A("p8b", yssmT, 64, 1088, 1088, w_sp, 0, D, mk1, cw=128)

            def mk2(es_):
                gs, ts, ms, os_ = Stage(es_, "p8g2", F32), Stage(es_, "p8t2", F32), Stage(es_, "p8m", F32), Stage(es_, "p8o", BF16)

                def evac(es, p, c, t, n, b_p, idx):
                    gl, bg_ = gs.nxt()
                    tl, bt_ = ts.nxt()
                    ml, bm_ = ms.nxt()
                    ol, bo_ = os_.nxt()
                    S_.dma("sp", gl[:, 0:n], gatesT[D + c:D + c + 128, t:t + n], writes=[bg_])
                    S_.dma("sp", tl[:, 0:n], t1T[c:c + 128, t:t + n], writes=[bt_])
                    S_.op("dve", lambda h: h.tensor_tensor(out=ml[:, 0:n], in0=p, in1=gl[:, 0:n], op=ALU.mult), reads=[b_p, bg_], writes=[bm_])
                    S_.op("dve", lambda h: h.tensor_tensor(out=ol[:, 0:n], in0=ml[:, 0:n], in1=tl[:, 0:n], op=ALU.add), reads=[bm_, bt_], writes=[bo_])
                    S_.dma("sp", mixT[c:c + 128, t:t + n], ol[:, 0:n], reads=[bo_])
                return evac
            mm_A("p8c", yattnT, KT, 0, NOWN, w_ap, 0, D, mk2)
        p8()
        if stop_after <= 8:
            return finish(nc, S_, out_d)

        def mm_B(name, act_d, kt_n, tok0, ntok, W_d, ncols, dst_d, dst_t0, cw=256):
            with ExitStack() as es:
                S_.store_q = "sp"
                A = sb(es, name + "_A", [128, kt_n, ntok], BF16)
                Wt = [sb(es, name + "_W%d" % i, [128, kt_n, cw], BF16) for i in range(2)]
                pp = [ps(es, name + "_p%d" % i, [128, 512], F32) for i in range(6)]
                stg = Stage(es, name + "_s", F32, n=4, w=cw)
                b_A, b_W, b_pp = Buf(), [Buf(), Buf()], [Buf() for _ in range(6)]
                av = act_d.rearrange("(k p) t -> p k t", p=128)
                step = 8 if kt_n % 8 == 0 else 2
                for k0 in range(0, kt_n, step):
                    S_.dma("sp", A[:, k0:k0 + step, :], av[:, k0:k0 + step, tok0:tok0 + ntok], writes=[b_A])
                wv = W_d.rearrange("(k p) c -> p k c", p=128)
                cnt = 0
                for wi, c0 in enumerate(range(0, ncols, cw)):
                    i = wi % 2
                    S_.dma("pool", Wt[i][:], wv[:, :, c0:c0 + cw], writes=[b_W[i]])
                    for tt in range(ntok // 128):
                        pi = cnt % 6
                        cnt += 1
                        for kt in range(kt_n):
                            S_.op("pe", lambda h, kt=kt, i=i, tt=tt, pi=pi: h.matmul(
                                pp[pi][:, 0:cw], lhsT=A[:, kt, tt * 128:(tt + 1) * 128], rhs=Wt[i][:, kt, :],
                                start=(kt == 0), stop=(kt == kt_n - 1)),
                                reads=[b_A, b_W[i]], writes=[b_pp[pi]], sig=(kt == kt_n - 1))
                        tl, bt_ = stg.nxt()
                        if cnt % 2 == 0:
                            S_.op("act", lambda h, pi=pi, tl=tl: h.activation(out=tl[:, 0:cw], in_=pp[pi][:, 0:cw], func=AF.Copy), reads=[b_pp[pi]], writes=[bt_])
                        else:
                            S_.op("dve", lambda h, pi=pi, tl=tl: h.tensor_copy(out=tl[:, 0:cw], in_=pp[pi][:, 0:cw]), reads=[b_pp[pi]], writes=[bt_])
                        r0 = dst_t0 + tt * 128
                        S_.dma("sp", dst_d[r0:r0 + 128, c0:c0 + cw], tl[:, 0:cw], reads=[bt_])
                S_.barrier()
                S_.store_q = "pool"

        mm_B("p9", mixT, KT, 0, NOWN, w_out, D, mixed, 0)

        def ln_affine_pass(name, a_d, res_d, gcol0, lng_d, lnb_d, ntiles, dst_d, do_h2):
            with ExitStack() as es:
                gbc = sb(es, name + "gbc", [128, D], F32)
                lg = sb(es, name + "lg", [128, D], F32)
                lb = sb(es, name + "lb", [128, D], F32)
                at = sb(es, name + "at", [128, D], F32)
                rt = [sb(es, name + "rt%d" % i, [128, D], F32) for i in range(2)]
                st = sb(es, name + "st", [128, 8, 6], F32)
                mv = sb(es, name + "mv", [128, 2], F32)
                rstd = sb(es, name + "rstd", [128, 1], F32)
                b_bc, b_at, b_st = Buf(), Buf(), Buf()
                b_rt = [Buf(), Buf()]
                bc = lambda ap: ap.partition_broadcast(128).rearrange("p a b -> p (a b)")
                S_.dma("sp", gbc[:], bc(modD[0:1, gcol0:gcol0 + D]), writes=[b_bc])
                S_.dma("sp", lg[:], bc(lng_d[0:1, :]), writes=[b_bc])
                S_.dma("sp", lb[:], bc(lnb_d[0:1, :]), writes=[b_bc])
                if do_h2:
                    xn = sb(es, name + "xn", [128, D], F32)
                    hblk = sb(es, name + "hblk", [128, KT, 512], BF16)
                    pT = [ps(es, name + "pT%d" % i, [128, 512], F32) for i in range(2)]
                    b_xn, b_hblk = Buf(), Buf()
                    b_pT = [Buf(), Buf()]
                    h2v = h2T.rearrange("(k p) t -> p k t", p=128)
                for tt in range(ntiles):
                    i = tt % 2
                    r0 = tt * 128
                    S_.dma("sp", at[:], a_d[r0:r0 + 128, :], writes=[b_at])
                    S_.dma("sp", rt[i][:], res_d[r0:r0 + 128, :], writes=[b_rt[i]])
                    S_.op("dve", lambda h: h.tensor_tensor(out=at[:], in0=at[:], in1=gbc[:], op=ALU.mult), reads=[b_bc], writes=[b_at])
                    S_.op("dve", lambda h, i=i: h.scalar_tensor_tensor(out=rt[i][:], in0=rt[i][:], scalar=float(ALPHA), in1=at[:], op0=ALU.mult, op1=ALU.add),
                          reads=[b_at], writes=[b_rt[i]])
                    ln_stats(es, rt[i], b_rt[i], st, mv, rstd, b_st)
                    S_.op("dve", lambda h, i=i: h.tensor_scalar(out=rt[i][:], in0=rt[i][:], scalar1=mv[:, 0:1], scalar2=rstd[:, 0:1],
                                                                 op0=ALU.subtract, op1=ALU.mult), reads=[b_st], writes=[b_rt[i]])
                    S_.op("dve", lambda h, i=i: h.tensor_tensor(out=rt[i][:], in0=rt[i][:], in1=lg[:], op=ALU.mult), reads=[b_bc], writes=[b_rt[i]])
                    S_.op("dve", lambda h, i=i: h.tensor_tensor(out=rt[i][:], in0=rt[i][:], in1=lb[:], op=ALU.add), reads=[b_bc], writes=[b_rt[i]])
                    S_.dma("sp", dst_d[r0:r0 + 128, :], rt[i][:], reads=[b_rt[i]])
                    if do_h2:
                        ln_stats(es, rt[i], b_rt[i], st, mv, rstd, b_st)
                        S_.op("dve", lambda h, i=i: h.tensor_scalar(out=xn[:], in0=rt[i][:], scalar1=mv[:, 0:1], scalar2=rstd[:, 0:1],
                                                                     op0=ALU.subtract, op1=ALU.mult), reads=[b_rt[i], b_st], writes=[b_xn])
                        sub = tt % 4
                        modT(xn, b_xn, hblk, b_hblk, sub, pT, b_pT, 128, 96)
                        if sub == 3 or tt == ntiles - 1:
                            t0 = (tt // 4) * 512
                            w = (sub + 1) * 128
                            S_.dma("sp", h2v[:, :, t0:t0 + w], hblk[:, :, 0:w], reads=[b_hblk])
                S_.barrier()
        ln_affine_pass("l1", mixed, x_d, 2 * D, ln1g, ln1b, NOWN // 128, x1d, True)
        if stop_after <= 9:
            return finish(nc, S_, out_d)

        def p10():
            def mk(es_):
                st_ = Stage(es_, "p10s", F32)

                def evac(es, p, c, t, n, b_p, idx):
                    tl, bt_ = st_.nxt()
                    if idx % 2 == 0:
                        S_.op("act", lambda h: h.activation(out=tl[:, 0:n], in_=p, func=AF.Copy), reads=[b_p], writes=[bt_])
                    else:
                        S_.op("dve", lambda h: h.tensor_copy(out=tl[:, 0:n], in_=p), reads=[b_p], writes=[bt_])
                    S_.dma("sp", upre[c:c + 128, t:t + n], tl[:, 0:n], reads=[bt_])
                return evac
            mm_A("p10", h2T, KT, 0, NOWN, w_up, 0, 2 * FFN, mk)
        p10()

        def p11():
            NW = NOUT + 2
            with ExitStack() as es:
                xin = [sb(es, "fxin%d" % i, [128, NW], F32) for i in range(4)]
                acc = [sb(es, "facc%d" % i, [128, NOUT], F32) for i in range(2)]
                sa = sb(es, "fsa", [128, NOUT], F32)
                ob = [sb(es, "fob%d" % i, [128, NOUT], BF16) for i in range(2)]
                cw = sb(es, "fcw", [128, 172, 3], F32)
                cbs = sb(es, "fcb", [128, 172], F32)
                b_xin = [Buf() for _ in range(4)]
                b_acc, b_ob = [Buf(), Buf()], [Buf(), Buf()]
                b_cw, b_sa = Buf(), Buf()
                S_.dma("sp", cw[:], fcw[:, :, :], writes=[b_cw])
                S_.dma("sp", cbs[:], fcb[:, :], writes=[b_cw])
                for i in range(4):
                    S_.op("dve", lambda h, i=i: h.memset(xin[i][:, 0:1], 0.0), writes=[b_xin[i]])
                for blk in range(86):
                    for half in range(2):
                        ci = half * 86 + blk
                        i = (blk % 2) * 2 + half
                        S_.dma("sp", xin[i][:, 1:NW], upre[ci * 128:(ci + 1) * 128, 0:NOUT + 1], writes=[b_xin[i]])
                        S_.op("act", lambda h, i=i, ci=ci, half=half: h.activation(out=acc[half][:], in_=xin[i][:, 1:NOUT + 1], func=AF.Identity,
                                                                                scale=cw[:, ci, 1:2], bias=cbs[:, ci:ci + 1]),
                              reads=[b_xin[i], b_cw], writes=[b_acc[half]])
                        S_.op("dve", lambda h, i=i, ci=ci, half=half: h.scalar_tensor_tensor(out=acc[half][:], in0=xin[i][:, 0:NOUT], scalar=cw[:, ci, 0:1],
                                                                                          in1=acc[half][:], op0=ALU.mult, op1=ALU.add),
                              reads=[b_xin[i], b_cw], writes=[b_acc[half]])
                        S_.op("dve", lambda h, i=i, ci=ci, half=half: h.scalar_tensor_tensor(out=acc[half][:], in0=xin[i][:, 2:NOUT + 2], scalar=cw[:, ci, 2:3],
                                                                                          in1=acc[half][:], op0=ALU.mult, op1=ALU.add),
                              reads=[b_xin[i], b_cw], writes=[b_acc[half]])
                    j = blk % 2
                    S_.op("act", lambda h: h.activation(out=sa[:], in_=acc[0][:], func=AF.Silu), reads=[b_acc[0]], writes=[b_sa])
                    S_.op("dve", lambda h, j=j: h.tensor_tensor(out=ob[j][:], in0=sa[:], in1=acc[1][:], op=ALU.mult), reads=[b_sa, b_acc[1]], writes=[b_ob[j]])
                    S_.dma("sp", actT[blk * 128:(blk + 1) * 128, :], ob[j][:], reads=[b_ob[j]])
                S_.barrier()
        p11()
        if stop_after <= 11:
            return finish(nc, S_, out_d)

        for sbk in range(4):
            mm_B("p12_%d" % sbk, actT, 86, sbk * 512, 512, w_down, D, fd, sbk * 512)
        ln_affine_pass("l2", fd, x1d, 5 * D, ln2g, ln2b, NOUT // 128, out_d, False)

        return finish(nc, S_, out_d)


def finish(nc, S_, out_d):
    S_.barrier()
    return nc


def _consts():
    c = np.zeros((128, 8, 128), np.float32)
    i = np.arange(128)
    c[:, 0, :] = np.eye(128, dtype=np.float32)
    c[:, 1, :] = 1.0
    c[:, 2, :] = (i[:, None] <= i[None, :])
    c[:, 3, :] = (i[:, None] > i[None, :])
    c[:, 4, :] = (i[:, None] >= i[None, :])
    c[:, 5, :] = (i[:, None] < i[None, :])
    P = np.zeros((128, 128), np.float32)
    for base in (0, 64):
        for j in range(32):
            P[base + j, base + 32 + j] = -1.0
            P[base + 32 + j, base + j] = 1.0
    c[:, 6, :] = P.T
    return c


def _rope_tables(flip):
    t = np.arange(S)
    if flip:
        t = t[::-1]
    row = (t // 64).astype(np.float32)
    colp = (t % 64).astype(np.float32)
    inv = (1.0 / (np.float32(10000.0) ** (np.arange(32, dtype=np.float32) / np.float32(32)))).astype(np.float32)
    cos = np.zeros((128, S), np.float32)
    sin = np.zeros((128, S), np.float32)
    for d in range(128):
        pos = row if d < 64 else colp
        ang = (pos * inv[d % 32]).astype(np.float32)
        cos[d] = np.cos(ang)
        sin[d] = np.sin(ang)
    return cos, sin


def _pp(v, nb):
    return np.ascontiguousarray(np.asarray(v, np.float32).reshape(nb, 128).T)


def make_in_maps(inputs, cores):
    f = lambda k: np.asarray(inputs[k], np.float32)
    w_in0 = np.ascontiguousarray(f("w_in")[0])
    w_in1 = w_in0.copy()
    w_in1[:, DT0:DT0 + 128] = w_in0[:, DT0 + 128:DT0 + 256]
    w_in1[:, DT0 + 128:DT0 + 256] = w_in0[:, DT0:DT0 + 128]
    consts = _consts()
    ropes = [_rope_tables(0), _rope_tables(1)]
    maps = []
    for core in cores:
        b, hf = core // 2, core % 2
        xl = f("x")[b]
        if hf:
            xl = xl[::-1]
        taps = [2, 1, 0] if hf else [0, 1, 2]
        dirs = [1, 0] if hf else [0, 1]
        scw = f("ssm_conv_w")[0][taps]
        fcw = f("ffn_conv_w")[0][taps]
        m = {
            "x": np.ascontiguousarray(xl),
            "c": _pp(f("c")[b], 32),
            "w_ada": f("w_ada")[0], "b_ada": f("b_ada")[0].reshape(1, -1),
            "w_in": w_in1 if hf else w_in0,
            "ssm_conv_w": np.ascontiguousarray(scw.reshape(3, 80, 128).transpose(2, 1, 0)),
            "ssm_conv_b": _pp(f("ssm_conv_b")[0], 80),
            "ssm_a_log": np.ascontiguousarray(f("ssm_a_log")[0][dirs].T),
            "ssm_dt_bias": np.ascontiguousarray(f("ssm_dt_bias")[0][dirs].T),
            "ssm_d": f("ssm_d")[0].reshape(1, 128),
            "ssm_norm_w": _pp(f("ssm_norm_w")[0], 64),
            "q_norm_w": f("q_norm_w")[0].reshape(128, 1), "k_norm_w": f("k_norm_w")[0].reshape(128, 1),
            "w_ssm_proj": f("w_ssm_proj")[0], "w_attn_proj": f("w_attn_proj")[0],
            "w_gate": f("w_gate")[0], "b_gate": _pp(f("b_gate")[0], 64),
            "w_out": f("w_out")[0], "ln1_g": f("ln1_g")[0].reshape(1, -1), "ln1_b": f("ln1_b")[0].reshape(1, -1),
            "w_up": f("w_up")[0],
            "ffn_conv_w": np.ascontiguousarray(fcw.reshape(3, 172, 128).transpose(2, 1, 0)),
            "ffn_conv_b": _pp(f("ffn_conv_b")[0], 172),
            "w_down": f("w_down")[0], "ln2_g": f("ln2_g")[0].reshape(1, -1), "ln2_b": f("ln2_b")[0].reshape(1, -1),
            "rope_cos": ropes[hf][0], "rope_sin": ropes[hf][1],
            "consts": consts,
        }
        maps.append(m)
    return maps


def kernel(**inputs):
    nc = build_nc()
    cores = list(range(8))
    maps = make_in_maps(inputs, cores)
    res = run_bass_kernel_spmd(nc, maps, core_ids=cores)
    out = np.zeros((4, S, D), np.float32)
    for core in cores:
        b, hf = core // 2, core % 2
        o = np.asarray(res.results[core]["out"])
        if hf:
            out[b, NOUT:] = o[::-1]
        else:
            out[b, :NOUT] = o
    return out
```

```python
import math
from contextlib import ExitStack
import numpy as np
import concourse.bass as bass
import concourse.mybir as mybir
from concourse.bass_utils import run_bass_kernel_spmd

F32 = mybir.dt.float32
BF16 = mybir.dt.bfloat16
ALU = mybir.AluOpType
AF = mybir.ActivationFunctionType

D = 4096
S = 4096
NOWN = 2176
NOUT = 2048
KT = 32
DI = 8192
NCONV = 10240
FFN = 11008
IN_COLS = 24832
Z0, XBC0, DT0, Q0, K0, V0 = 0, 8192, 18432, 18688, 22784, 23808
ALPHA = 2.0 ** 0.25
LN_EPS = 1e-5
RMS_EPS = 1e-6


class Buf:
    __slots__ = ("w", "r")

    def __init__(self):
        self.w = None
        self.r = {}


class Sched:
    def __init__(self, nc, es, ndma=20):
        self.nc = nc
        self.engs = {}
        for name, h in (("pe", nc.tensor), ("act", nc.scalar), ("dve", nc.vector),
                        ("pool", nc.gpsimd), ("sp", nc.sync)):
            sem = es.enter_context(nc.semaphore("s_" + name))
            self.engs[name] = {"name": name, "h": h, "sem": sem, "cnt": 0, "seen": {}}
        self.dsems = {q: [[es.enter_context(nc.semaphore("d%s%d" % (q, i)), ), 0] for i in range(n)]
                      for q, n in (("sp", 12), ("pool", 10), ("act", 6))}
        self.di = {"sp": 0, "pool": 0, "act": 0}
        self.store_q = "pool"

    def _wait(self, e, ev):
        sem, val, src = ev
        if src == "pe" and e["name"] == "pe":
            return
        k = id(sem)
        if e["seen"].get(k, 0) >= val:
            return
        e["h"].wait_ge(sem, val)
        e["seen"][k] = val

    def _deps(self, e, reads, writes):
        for b in reads:
            if b.w is not None:
                self._wait(e, b.w)
        for b in writes:
            if b.w is not None:
                self._wait(e, b.w)
            for ev in b.r.values():
                self._wait(e, ev)

    def _record(self, ev, reads, writes):
        for b in reads:
            b.r[ev[2]] = ev
        for b in writes:
            b.w = ev
            b.r = {}

    def op(self, en, fn, reads=(), writes=(), sig=True):
        e = self.engs[en]
        self._deps(e, reads, writes)
        ins = fn(e["h"])
        if sig:
            e["cnt"] += 1
            ins.then_inc(e["sem"], 1)
            ev = (e["sem"], e["cnt"], en)
        else:
            ev = (e["sem"], e["cnt"] + 1, en)
        self._record(ev, reads, writes)

    def dma(self, en, out, in_, reads=(), writes=(), **kw):
        if en == "sp" and str(out.space) == "DRAM":
            en = self.store_q
        e = self.engs[en]
        self._deps(e, reads, writes)
        pool_ = self.dsems[en]
        idx = self.di[en]
        slot = pool_[idx]
        self.di[en] = (idx + 1) % len(pool_)
        key = "dq%s%d" % (en, idx)
        if slot[1] > 0:
            self._wait(e, (slot[0], slot[1], key))
        ins = e["h"].dma_start(out=out, in_=in_, **kw)
        slot[1] += 16
        ins.then_inc(slot[0], 16)
        self._record((slot[0], slot[1], key), reads, writes)

    def barrier(self):
        for e in self.engs.values():
            for o in self.engs.values():
                if o is not e and o["cnt"] > 0:
                    self._wait(e, (o["sem"], o["cnt"], o["name"] + "_b"))
            for q, pool_ in self.dsems.items():
                for i, sl in enumerate(pool_):
                    if sl[1] > 0:
                        self._wait(e, (sl[0], sl[1], "dq%s%d" % (q, i)))


def tok_blocks(n0, n, bs=512):
    out = []
    t = n0
    while t < n0 + n:
        m = min(bs, n0 + n - t)
        out.append((t, m))
        t += m
    return out


class K:
    def __init__(self, nc, dev_out):
        self.nc = nc
        self.dev_out = set(dev_out)
        self.flip = 0

    def dram(self, name, shape, dt, inp=False, out=False):
        if inp:
            return self.nc.dram_tensor(name, shape, dt, kind="ExternalInput").ap()
        if out or name in self.dev_out:
            return self.nc.dram_tensor(name, shape, dt, kind="ExternalOutput").ap()
        return self.nc.dram_tensor(name, shape, dt).ap()


def build_nc(dev_out=(), stop_after=99):
    nc = bass.Bass("TRN2", target_bir_lowering=False)
    kb = K(nc, dev_out)
    dram = kb.dram
    x_d = dram("x", [S, D], F32, inp=True)
    c_d = dram("c", [128, KT], F32, inp=True)
    w_ada = dram("w_ada", [D, 6 * D], F32, inp=True)
    b_ada = dram("b_ada", [1, 6 * D], F32, inp=True)
    w_in = dram("w_in", [D, IN_COLS], F32, inp=True)
    scw = dram("ssm_conv_w", [128, 80, 3], F32, inp=True)
    scb = dram("ssm_conv_b", [128, 80], F32, inp=True)
    alog = dram("ssm_a_log", [128, 2], F32, inp=True)
    dtb = dram("ssm_dt_bias", [128, 2], F32, inp=True)
    ssmd = dram("ssm_d", [1, 128], F32, inp=True)
    snw = dram("ssm_norm_w", [128, 64], F32, inp=True)
    qnw = dram("q_norm_w", [128, 1], F32, inp=True)
    knw = dram("k_norm_w", [128, 1], F32, inp=True)
    w_sp = dram("w_ssm_proj", [DI, D], F32, inp=True)
    w_ap = dram("w_attn_proj", [D, D], F32, inp=True)
    w_gate = dram("w_gate", [D, 2 * D], F32, inp=True)
    b_gate = dram("b_gate", [128, 64], F32, inp=True)
    w_out = dram("w_out", [D, D], F32, inp=True)
    ln1g = dram("ln1_g", [1, D], F32, inp=True)
    ln1b = dram("ln1_b", [1, D], F32, inp=True)
    w_up = dram("w_up", [D, 2 * FFN], F32, inp=True)
    fcw = dram("ffn_conv_w", [128, 172, 3], F32, inp=True)
    fcb = dram("ffn_conv_b", [128, 172], F32, inp=True)
    w_down = dram("w_down", [FFN, D], F32, inp=True)
    ln2g = dram("ln2_g", [1, D], F32, inp=True)
    ln2b = dram("ln2_b", [1, D], F32, inp=True)
    cos_d = dram("rope_cos", [128, S], F32, inp=True)
    sin_d = dram("rope_sin", [128, S], F32, inp=True)
    cst = dram("consts", [128, 8, 128], F32, inp=True)
    out_d = dram("out", [NOUT, D], F32, out=True)
    modD = dram("modD", [1, 6 * D], F32)
    hT = dram("hT", [D, S], BF16)
    zsT = dram("zsT", [DI, NOWN], F32)
    xbcpre = dram("xbcpre", [NCONV, S], F32)
    dtraw = dram("dtraw", [256, S], F32)
    qT = dram("qT", [D, NOWN], F32)
    kT = dram("kT", [1024, S], F32)
    vT = dram("vT", [1024, S], BF16)
    gatesT = dram("gatesT", [2 * D, NOWN], F32)
    xs_tok = dram("xs_tok", [S, DI], BF16)
    B_tok = dram("B_tok", [S, 1024], BF16)
    BTd = dram("BTd", [1024, S], BF16)
    CTd = dram("CTd", [1024, S], BF16)
    dt_tok = dram("dt_tok", [S, 256], F32)
    a_tok = dram("a_tok", [S, 256], F32)
    ysum = dram("ysum", [NOWN, DI], F32)
    yssmT = dram("yssmT", [DI, NOWN], BF16)
    qTn = dram("qTn", [D, NOWN], BF16)
    kTn = dram("kTn", [1024, S], BF16)
    v_tok = dram("v_tok", [S, 1024], BF16)
    yattnT = dram("yattnT", [D, NOWN], BF16)
    t1T = dram("t1T", [D, NOWN], F32)
    mixT = dram("mixT", [D, NOWN], BF16)
    mixed = dram("mixed", [NOWN, D], F32)
    x1d = dram("x1d", [NOWN, D], F32)
    h2T = dram("h2T", [D, NOWN], BF16)
    upre = dram("upre", [2 * FFN, NOWN], F32)
    actT = dram("actT", [FFN, NOUT], BF16)
    fd = dram("fd", [NOUT, D], F32)
    wdb = dram("wdb", [FFN, D], BF16)

    with ExitStack() as es0:
        S_ = Sched(nc, es0)
        uid = [0]

        def sb(es, name, shape, dt):
            uid[0] += 1
            return es.enter_context(nc.sbuf_tensor("%s_%d" % (name, uid[0]), shape, dt))

        def ps(es, name, shape, dt):
            uid[0] += 1
            return es.enter_context(nc.psum_tensor("%s_%d" % (name, uid[0]), shape, dt))
        C32 = sb(es0, "C32", [128, 8, 128], F32)
        C16 = sb(es0, "C16", [128, 8, 128], BF16)
        modP = sb(es0, "modP", [128, 192], F32)
        bC32, bC16, bmodP = Buf(), Buf(), Buf()
        S_.dma("sp", C32[:], cst[:, :, :], writes=[bC32])
        S_.dma("pool", C16[:], cst[:, :, :], writes=[bC16])
        IDf, ONEf, TRIf, SUf, TRIb, SUb, ROPf = [C32[:, i, :] for i in range(7)]
        IDb, ONEb = C16[:, 0, :], C16[:, 1, :]
        TRI = (TRIf, TRIb)
        SU = (SUf, SUb)

        with ExitStack() as es:
            ct = sb(es, "ct", [128, KT], F32)
            condT = sb(es, "condT", [128, KT], BF16)
            wb = [sb(es, "wada%d" % i, [128, KT, 512], BF16) for i in range(3)]
            brow = [sb(es, "brow%d" % i, [1, 512], F32) for i in range(2)]
            mrow = [sb(es, "mrow%d" % i, [1, 512], F32) for i in range(2)]
            pm = [ps(es, "pm%d" % i, [1, 512], F32) for i in range(2)]
            pcol = ps(es, "pcol", [128, 192], F32)
            b_ct, b_cond, b_pcol = Buf(), Buf(), Buf()
            b_brow, b_mrow = [Buf(), Buf()], [Buf(), Buf()]
            b_wb = [Buf(), Buf(), Buf()]
            b_pm = [Buf(), Buf()]
            S_.dma("sp", ct[:], c_d[:, :], writes=[b_ct])
            S_.op("act", lambda h: h.activation(out=condT[:], in_=ct[:], func=AF.Silu), reads=[b_ct], writes=[b_cond])
            wv = w_ada.rearrange("(k p) c -> p k c", p=128)
            for cb in range(48):
                i = cb % 2
                iw = cb % 3
                S_.dma("pool", wb[iw][:], wv[:, :, cb * 512:(cb + 1) * 512], writes=[b_wb[iw]])
                S_.dma("sp", brow[i][:], b_ada[:, cb * 512:(cb + 1) * 512], writes=[b_brow[i]])
                for kt in range(KT):
                    S_.op("pe", lambda h, kt=kt, i=i, iw=iw: h.matmul(pm[i][:], lhsT=condT[:, kt:kt + 1], rhs=wb[iw][:, kt, :],
                                                                 start=(kt == 0), stop=(kt == KT - 1)),
                          reads=[b_cond, b_wb[iw]], writes=[b_pm[i]], sig=(kt == KT - 1))
                S_.op("dve", lambda h, i=i: h.tensor_tensor(out=mrow[i][:], in0=pm[i][:], in1=brow[i][:], op=ALU.add),
                      reads=[b_pm[i], b_brow[i]], writes=[b_mrow[i]])
                S_.dma("sp", modD[:, cb * 512:(cb + 1) * 512], mrow[i][:], reads=[b_mrow[i]])
                for q in range(4):
                    j = cb * 4 + q
                    S_.op("pe", lambda h, j=j, q=q, i=i: h.matmul(pcol[:, j:j + 1], lhsT=mrow[i][0:1, q * 128:(q + 1) * 128],
                                                                    rhs=ONEf[0:1, 0:1], start=True, stop=True),
                          reads=[b_mrow[i], bC32], writes=[b_pcol], sig=(q == 3))
            S_.op("dve", lambda h: h.tensor_copy(out=modP[:], in_=pcol[:]), reads=[b_pcol], writes=[bmodP])
            S_.op("dve", lambda h: h.tensor_scalar(out=modP[:, 32:64], in0=modP[:, 32:64], scalar1=1.0, scalar2=None, op0=ALU.add),
                  reads=[bmodP], writes=[bmodP])
            S_.op("dve", lambda h: h.tensor_scalar(out=modP[:, 128:160], in0=modP[:, 128:160], scalar1=1.0, scalar2=None, op0=ALU.add),
                  reads=[bmodP], writes=[bmodP])
            S_.barrier()
        if stop_after <= 0:
            return finish(nc, S_, out_d)

        def ln_stats(es_tiles, xt, b_xt, st, mv, rstd, b_st):
            for j in range(8):
                S_.op("dve", lambda h, j=j: h.bn_stats(out=st[:, j, :], in_=xt[:, j * 512:(j + 1) * 512]),
                      reads=[b_xt], writes=[b_st])
            S_.op("dve", lambda h: h.bn_aggr(out=mv[:], in_=st[:].rearrange("p a b -> p (a b)")), reads=[b_st], writes=[b_st])
            S_.op("dve", lambda h: h.tensor_scalar(out=rstd[:], in0=mv[:, 1:2], scalar1=LN_EPS, scalar2=None, op0=ALU.add),
                  reads=[b_st], writes=[b_st])
            S_.op("act", lambda h: h.activation(out=rstd[:], in_=rstd[:], func=AF.Sqrt), reads=[b_st], writes=[b_st])
            S_.op("dve", lambda h: h.reciprocal(out=rstd[:], in_=rstd[:]), reads=[b_st], writes=[b_st])

        def modT(xn, b_xn, hblk, b_hblk, sub, pT, b_pT, sc0, sh0):
            for g in range(8):
                i = g % 2
                for q in range(4):
                    kt = g * 4 + q
                    S_.op("pe", lambda h, kt=kt, q=q, i=i: h.transpose(out=pT[i][:, q * 128:(q + 1) * 128],
                                                                       in_=xn[:, kt * 128:(kt + 1) * 128], identity=IDf),
                          reads=[b_xn, bC32], writes=[b_pT[i]], sig=(q == 3))
                for q in range(4):
                    kt = g * 4 + q
                    S_.op("act", lambda h, kt=kt, q=q, i=i: h.activation(
                        out=hblk[:, kt, sub * 128:(sub + 1) * 128], in_=pT[i][:, q * 128:(q + 1) * 128], func=AF.Identity,
                        scale=modP[:, sc0 + kt:sc0 + kt + 1], bias=modP[:, sh0 + kt:sh0 + kt + 1]),
                        reads=[b_pT[i], bmodP], writes=[b_hblk])

        with ExitStack() as es:
            xt = [sb(es, "xt%d" % i, [128, D], F32) for i in range(2)]
            xn = [sb(es, "xn%d" % i, [128, D], F32) for i in range(2)]
            st = sb(es, "st", [128, 8, 6], F32)
            mv = sb(es, "mv", [128, 2], F32)
            rstd = sb(es, "rstd", [128, 1], F32)
            hblk = [sb(es, "hblk%d" % i, [128, KT, 512], BF16) for i in range(2)]
            pT = [ps(es, "pT%d" % i, [128, 512], F32) for i in range(2)]
            b_xt, b_xn, b_hblk, b_pT = [Buf(), Buf()], [Buf(), Buf()], [Buf(), Buf()], [Buf(), Buf()]
            b_st = Buf()
            hTv = hT.rearrange("(k p) t -> p k t", p=128)
            for blk in range(8):
                hb = blk % 2
                for sub in range(4):
                    tt = blk * 4 + sub
                    i = tt % 2
                    S_.dma("sp", xt[i][:], x_d[tt * 128:(tt + 1) * 128, :], writes=[b_xt[i]])
                    ln_stats(es, xt[i], b_xt[i], st, mv, rstd, b_st)
                    S_.op("dve", lambda h, i=i: h.tensor_scalar(out=xn[i][:], in0=xt[i][:], scalar1=mv[:, 0:1], scalar2=rstd[:, 0:1],
                                                                 op0=ALU.subtract, op1=ALU.mult),
                          reads=[b_xt[i], b_st], writes=[b_xn[i]])
                    modT(xn[i], b_xn[i], hblk[hb], b_hblk[hb], sub, pT, b_pT, 32, 0)
                S_.dma("sp", hTv[:, :, blk * 512:(blk + 1) * 512], hblk[hb][:], reads=[b_hblk[hb]])
            S_.barrier()
        if stop_after <= 1:
            return finish(nc, S_, out_d)

        def mm_A(name, act_d, kt_n, tok0, ntok, W_d, col0, ncols, mk_evac, cw=256):
            with ExitStack() as es:
                S_.store_q = "sp"
                evac = mk_evac(es)
                A = sb(es, name + "_A", [128, kt_n, ntok], BF16)
                Wt = [sb(es, name + "_W%d" % i, [128, kt_n, cw], BF16) for i in range(2)]
                pp = [ps(es, name + "_p%d" % i, [128, 512], F32) for i in range(6)]
                b_A, b_W, b_pp = Buf(), [Buf(), Buf()], [Buf() for _ in range(6)]
                av = act_d.rearrange("(k p) t -> p k t", p=128)
                for k0 in range(0, kt_n, 8):
                    S_.dma("sp", A[:, k0:k0 + 8, :], av[:, k0:k0 + 8, tok0:tok0 + ntok], writes=[b_A])
                wv = W_d.rearrange("(k p) c -> p k c", p=128)
                tbs = tok_blocks(0, ntok)
                cnt = 0
                for wi, c0 in enumerate(range(col0, col0 + ncols, cw)):
                    i = wi % 2
                    S_.dma("pool", Wt[i][:], wv[:, :, c0:c0 + cw], writes=[b_W[i]])
                    for cc in range(cw // 128):
                        for (t0, n) in tbs:
                            pi = cnt % 6
                            cnt += 1
                            for kt in range(kt_n):
                                S_.op("pe", lambda h, kt=kt, i=i, cc=cc, t0=t0, n=n, pi=pi: h.matmul(
                                    pp[pi][:, 0:n], lhsT=Wt[i][:, kt, cc * 128:(cc + 1) * 128], rhs=A[:, kt, t0:t0 + n],
                                    start=(kt == 0), stop=(kt == kt_n - 1)),
                                    reads=[b_A, b_W[i]], writes=[b_pp[pi]], sig=(kt == kt_n - 1))
                            evac(es, pp[pi][:, 0:n], c0 + cc * 128, tok0 + t0, n, b_pp[pi], cnt)
                S_.barrier()
                S_.store_q = "pool"

        class Stage:
            def __init__(self, es, name, dt, n=4, w=512):
                self.t = [sb(es, "%s_st%d" % (name, i), [128, w], dt) for i in range(n)]
                self.b = [Buf() for _ in range(n)]
                self.i = 0

            def nxt(self):
                j = self.i % len(self.t)
                self.i += 1
                return self.t[j], self.b[j]

        def p2(tok0, ntok, full):
            def mk(es_):
                stg = {"f": Stage(es_, "p2f", F32), "h": Stage(es_, "p2h", BF16)}

                def evac(es, p, c, t, n, b_p, idx):
                    return evac_(stg, es, p, c, t, n, b_p, idx)
                return evac

            def evac_(stg, es, p, c, t, n, b_p, idx):
                eng = "act" if idx % 2 == 0 else "dve"
                if c < XBC0:
                    tl, bt = stg["f"].nxt()
                    S_.op("act", lambda h: h.activation(out=tl[:, 0:n], in_=p, func=AF.Silu), reads=[b_p], writes=[bt])
                    S_.dma("sp", zsT[c:c + 128, t:t + n], tl[:, 0:n], reads=[bt])
                    return
                if c >= V0:
                    tl, bt = stg["h"].nxt()
                    dst = vT[c - V0:c - V0 + 128, t:t + n]
                else:
                    tl, bt = stg["f"].nxt()
                    if c < DT0:
                        dst = xbcpre[c - XBC0:c - XBC0 + 128, t:t + n]
                    elif c < Q0:
                        dst = dtraw[c - DT0:c - DT0 + 128, t:t + n]
                    elif c < K0:
                        dst = qT[c - Q0:c - Q0 + 128, t:t + n]
                    else:
                        dst = kT[c - K0:c - K0 + 128, t:t + n]
                if eng == "act":
                    S_.op("act", lambda h: h.activation(out=tl[:, 0:n], in_=p, func=AF.Copy), reads=[b_p], writes=[bt])
                else:
                    S_.op("dve", lambda h: h.tensor_copy(out=tl[:, 0:n], in_=p), reads=[b_p], writes=[bt])
                S_.dma("sp", dst, tl[:, 0:n], reads=[bt])
            if full:
                mm_A("p2a", hT, KT, tok0, ntok, w_in, 0, IN_COLS, mk)
            else:
                mm_A("p2b", hT, KT, tok0, ntok, w_in, XBC0, 8192 + 1024, mk)
                mm_A("p2c", hT, KT, tok0, ntok, w_in, DT0, 256, mk)
                mm_A("p2d", hT, KT, tok0, ntok, w_in, K0, 2048, mk)

        p2(0, NOWN, True)
        p2(NOWN, S - NOWN, False)

        def p2g():
            stg = {}
            with ExitStack() as esb:
                bg = sb(esb, "bgate", [128, 64], F32)
                b_bg = Buf()
                S_.dma("sp", bg[:], b_gate[:, :], writes=[b_bg])

                def mk(es_):
                    st_ = Stage(es_, "pgf", F32)

                    def evac(es, p, c, t, n, b_p, idx):
                        tl, bt = st_.nxt()
                        S_.op("act", lambda h: h.activation(out=tl[:, 0:n], in_=p, func=AF.Sigmoid, bias=bg[:, c // 128:c // 128 + 1]),
                              reads=[b_p, b_bg], writes=[bt])
                        S_.dma("sp", gatesT[c:c + 128, t:t + n], tl[:, 0:n], reads=[bt])
                    return evac
                mm_A("pg", hT, KT, 0, NOWN, w_gate, 0, 2 * D, mk)
        p2g()
        if stop_after <= 2:
            return finish(nc, S_, out_d)

        def p3():
            with ExitStack() as es:
                xin = [sb(es, "cxin%d" % i, [128, S + 2], F32) for i in range(2)]
                acc = [sb(es, "cacc%d" % i, [128, S], F32) for i in range(2)]
                ob = [sb(es, "cob%d" % i, [128, S], BF16) for i in range(2)]
                cw = sb(es, "ccw", [128, 80, 3], F32)
                cbs = sb(es, "ccb", [128, 80], F32)
                stT = [sb(es, "cstT%d" % i, [128, 32, 128], BF16) for i in range(2)]
                pT = [ps(es, "cpT%d" % i, [128, 1024], BF16) for i in range(2)]
                pF = [ps(es, "cpF%d" % i, [128, 512], F32) for i in range(2)]
                tt = sb(es, "ctt", [128, S], F32)
                dtv = sb(es, "cdtv", [128, S], F32)
                stF = sb(es, "cstF", [128, 32, 128], F32)
                al = sb(es, "cal", [128, 2], F32)
                db = sb(es, "cdb", [128, 2], F32)
                b_xin, b_ob, b_stT, b_pT, b_pF = [Buf(), Buf()], [Buf(), Buf()], [Buf(), Buf()], [Buf(), Buf()], [Buf(), Buf()]
                b_cw, b_tt, b_dtv, b_stF, b_al = Buf(), Buf(), Buf(), Buf(), Buf()
                b_acc = [Buf(), Buf()]
                S_.dma("sp", cw[:], scw[:, :, :], writes=[b_cw])
                S_.dma("sp", cbs[:], scb[:, :], writes=[b_cw])
                S_.dma("sp", al[:], alog[:, :], writes=[b_al])
                S_.dma("sp", db[:], dtb[:, :], writes=[b_al])
                for i in range(2):
                    S_.op("dve", lambda h, i=i: h.memset(xin[i][:, 0:1], 0.0), writes=[b_xin[i]])
                    S_.op("dve", lambda h, i=i: h.memset(xin[i][:, S + 1:S + 2], 0.0), writes=[b_xin[i]])
                xsv = xs_tok.rearrange("(t p) c -> p t c", p=128)
                btv = B_tok.rearrange("(t p) c -> p t c", p=128)
                for blk in range(80):
                    i = blk % 2
                    nld = S if blk < 72 else NOWN
                    S_.dma("sp", xin[i][:, 1:nld + 1], xbcpre[blk * 128:(blk + 1) * 128, 0:nld], writes=[b_xin[i]])
                    S_.op("act", lambda h, i=i, blk=blk: h.activation(out=acc[i][:], in_=xin[i][:, 1:S + 1], func=AF.Identity,
                                                                        scale=cw[:, blk, 1:2], bias=cbs[:, blk:blk + 1]),
                          reads=[b_xin[i], b_cw], writes=[b_acc[i]])
                    S_.op("dve", lambda h, i=i, blk=blk: h.scalar_tensor_tensor(out=acc[i][:], in0=xin[i][:, 0:S], scalar=cw[:, blk, 0:1],
                                                                                  in1=acc[i][:], op0=ALU.mult, op1=ALU.add),
                          reads=[b_xin[i], b_cw], writes=[b_acc[i]])
                    S_.op("dve", lambda h, i=i, blk=blk: h.scalar_tensor_tensor(out=acc[i][:], in0=xin[i][:, 2:S + 2], scalar=cw[:, blk, 2:3],
                                                                                  in1=acc[i][:], op0=ALU.mult, op1=ALU.add),
                          reads=[b_xin[i], b_cw], writes=[b_acc[i]])
                    S_.op("act", lambda h, i=i: h.activation(out=ob[i][:], in_=acc[i][:], func=AF.Silu), reads=[b_acc[i]], writes=[b_ob[i]])
                    if blk < 72:
                        for g in range(4):
                            j = g % 2
                            for q in range(8):
                                tl = g * 8 + q
                                S_.op("pe", lambda h, tl=tl, q=q, j=j, i=i: h.transpose(out=pT[j][:, q * 128:(q + 1) * 128],
                                                                                         in_=ob[i][:, tl * 128:(tl + 1) * 128], identity=IDb),
                                      reads=[b_ob[i], bC16], writes=[b_pT[j]], sig=(q == 7))
                            if g % 2 == 0:
                                S_.op("act", lambda h, g=g, j=j, i=i: h.activation(out=stT[i][:, g * 8:(g + 1) * 8, :].rearrange("p a b -> p (a b)"),
                                                                                     in_=pT[j][:], func=AF.Copy),
                                      reads=[b_pT[j]], writes=[b_stT[i]])
                            else:
                                S_.op("dve", lambda h, g=g, j=j, i=i: h.tensor_copy(out=stT[i][:, g * 8:(g + 1) * 8, :].rearrange("p a b -> p (a b)"),
                                                                                      in_=pT[j][:]),
                                      reads=[b_pT[j]], writes=[b_stT[i]])
                        if blk < 64:
                            S_.dma("sp", xsv[:, :, blk * 128:(blk + 1) * 128], stT[i][:], reads=[b_stT[i]])
                        else:
                            S_.dma("sp", btv[:, :, (blk - 64) * 128:(blk - 63) * 128], stT[i][:], reads=[b_stT[i]])
                            S_.dma("sp", BTd[(blk - 64) * 128:(blk - 63) * 128, :], ob[i][:], reads=[b_ob[i]])
                    else:
                        S_.dma("sp", CTd[(blk - 72) * 128:(blk - 71) * 128, 0:NOWN], ob[i][:, 0:NOWN], reads=[b_ob[i]])
                S_.op("act", lambda h: h.activation(out=al[:], in_=al[:], func=AF.Exp), reads=[b_al], writes=[b_al])
                S_.op("dve", lambda h: h.tensor_scalar(out=al[:], in0=al[:], scalar1=-1.0, scalar2=None, op0=ALU.mult), reads=[b_al], writes=[b_al])
                dtv_ = dt_tok.rearrange("(t p) c -> p t c", p=128)
                atv_ = a_tok.rearrange("(t p) c -> p t c", p=128)
                for d in range(2):
                    i = d
                    S_.dma("sp", xin[i][:, 1:S + 1], dtraw[d * 128:(d + 1) * 128, :], writes=[b_xin[i]])
                    S_.op("act", lambda h, i=i, d=d: h.activation(out=acc[i][:], in_=xin[i][:, 1:S + 1], func=AF.Identity, bias=db[:, d:d + 1]),
                          reads=[b_xin[i], b_al], writes=[b_acc[i]])
                    S_.op("act", lambda h, i=i: h.activation(out=tt[:], in_=acc[i][:], func=AF.Abs), reads=[b_acc[i]], writes=[b_tt])
                    S_.op("act", lambda h: h.activation(out=tt[:], in_=tt[:], func=AF.Exp, scale=-1.0), reads=[b_tt], writes=[b_tt])
                    S_.op("act", lambda h: h.activation(out=tt[:], in_=tt[:], func=AF.Ln, bias=1.0), reads=[b_tt], writes=[b_tt])
                    S_.op("dve", lambda h, i=i: h.scalar_tensor_tensor(out=dtv[:], in0=acc[i][:], scalar=0.0, in1=tt[:], op0=ALU.max, op1=ALU.add),
                          reads=[b_acc[i], b_tt], writes=[b_dtv])
                    S_.op("dve", lambda h, d=d: h.tensor_scalar(out=tt[:], in0=dtv[:], scalar1=al[:, d:d + 1], scalar2=None, op0=ALU.mult),
                          reads=[b_dtv, b_al], writes=[b_tt])
                    for (src, b_src, dstv) in ((dtv, b_dtv, dtv_), (tt, b_tt, atv_)):
                        for g in range(8):
                            j = g % 2
                            for q in range(4):
                                tl = g * 4 + q
                                S_.op("pe", lambda h, tl=tl, q=q, j=j, src=src: h.transpose(out=pF[j][:, q * 128:(q + 1) * 128],
                                                                                             in_=src[:, tl * 128:(tl + 1) * 128], identity=IDf),
                                      reads=[b_src, bC32], writes=[b_pF[j]], sig=(q == 3))
                            S_.op("dve", lambda h, g=g, j=j: h.tensor_copy(out=stF[:, g * 4:(g + 1) * 4, :].rearrange("p a b -> p (a b)"), in_=pF[j][:]),
                                  reads=[b_pF[j]], writes=[b_stF])
                        S_.dma("sp", dstv[:, :, d * 128:(d + 1) * 128], stF[:], reads=[b_stF])
                S_.barrier()
        p3()
        if stop_after <= 3:
            return finish(nc, S_, out_d)

        def p4():
            with ExitStack() as es:
                H = sb(es, "sH", [128, 8, 1024], F32)
                Hb = sb(es, "sHb", [128, 8, 1024], BF16)
                xs = [sb(es, "sxs%d" % i, [128, DI], BF16) for i in range(2)]
                xdt = [sb(es, "sxdt%d" % i, [128, DI], BF16) for i in range(2)]
                xw = [sb(es, "sxw%d" % i, [128, DI], BF16) for i in range(2)]
                bt = [sb(es, "sbt%d" % i, [128, 1024], BF16) for i in range(2)]
                BTc = [sb(es, "sBT%d" % i, [128, 8, 128], BF16) for i in range(2)]
                CTc = [sb(es, "sCT%d" % i, [128, 8, 128], BF16) for i in range(2)]
                dts = [sb(es, "sdt%d" % i, [128, 128], F32) for i in range(2)]
                as_ = [sb(es, "sa%d" % i, [128, 128], F32) for i in range(2)]
                toend = [sb(es, "stoend%d" % i, [128, 128], F32) for i in range(2)]
                eL = [sb(es, "seL%d" % i, [128, 128], F32) for i in range(2)]
                ecum = [sb(es, "secum%d" % i, [128, 128], F32) for i in range(2)]
                wts = [sb(es, "swts%d" % i, [128, 128], F32) for i in range(2)]
                Dbc = sb(es, "sDbc", [128, 128], F32)
                segr = [sb(es, "ssegr%d" % i, [128, 8, 128], F32) for i in range(3)]
                E = [sb(es, "sE%d" % i, [128, 8, 128], BF16) for i in range(2)]
                MT = [sb(es, "sMT%d" % i, [128, 8, 128], BF16) for i in range(2)]
                CBm = [sb(es, "sCBm%d" % i, [128, 128], F32) for i in range(2)]
                tmpy = [sb(es, "stmpy%d" % i, [128, 512], F32) for i in range(1)] * 2
                tmph = [sb(es, "stmph%d" % i, [128, 512], F32) for i in range(1)] * 2
                xD = [sb(es, "sxD%d" % i, [128, 512], BF16) for i in range(3)]
                yst = [sb(es, "syst%d" % i, [128, 512], F32) for i in range(2)]
                yld = [sb(es, "syld%d" % i, [128, 512], F32) for i in range(3)]
                pseg = [ps(es, "spseg%d" % i, [128, 1024], F32) for i in range(2)]
                pyi = [ps(es, "spyi%d" % i, [128, 512], F32) for i in range(2)]
                pyo = ps(es, "spyo", [128, 512], F32)
                psu = ps(es, "spsu", [128, 512], F32)
                (b_xs, b_xdt, b_xw, b_bt, b_BC, b_da, b_sm, b_segr, b_E, b_MT, b_CBm, b_tmpy, b_tmph, b_xD,
                 b_yst, b_yld, b_pseg, b_pyi) = ([Buf(), Buf(), Buf()] for _ in range(18))
                b_D, b_pyo, b_psu = Buf(), Buf(), Buf()
                b_tmpy = [b_tmpy[0]] * 2
                b_tmph = [b_tmph[0]] * 2
                b_H = [[Buf(), Buf()] for _ in range(8)]
                b_Hb = [[Buf(), Buf()] for _ in range(8)]
                S_.store_q = "pool"
                S_.dma("sp", Dbc[:], ssmd[0:1, :].partition_broadcast(128).rearrange("p a b -> p (a b)"), writes=[b_D])
                btv = BTd.rearrange("(g n) t -> n g t", n=128)
                ctv = CTd.rearrange("(g n) t -> n g t", n=128)
                v3 = lambda ap: ap.rearrange("p (h q) -> p h q", q=64)
                citer = [0]

                def prologue(d, c, i, full, last):
                    r0 = c * 128
                    S_.dma("sp", xs[i][:], xs_tok[r0:r0 + 128, :], writes=[b_xs[i]])
                    S_.dma("sp", bt[i][:], B_tok[r0:r0 + 128, :], writes=[b_bt[i]])
                    S_.dma("sp", dts[i][:], dt_tok[r0:r0 + 128, d * 128:(d + 1) * 128], writes=[b_da[i]])
                    S_.dma("sp", as_[i][:], a_tok[r0:r0 + 128, d * 128:(d + 1) * 128], writes=[b_da[i]])
                    if full:
                        S_.dma("sp", BTc[i][:], btv[:, :, r0:r0 + 128], writes=[b_BC[i]])
                        S_.dma("sp", CTc[i][:], ctv[:, :, r0:r0 + 128], writes=[b_BC[i]])
                    for q, cm in enumerate((SU[d], ONEf, TRI[d])):
                        S_.op("pe", lambda h, q=q, cm=cm: h.matmul(psu[:, 128 + q * 128:256 + q * 128], lhsT=cm, rhs=as_[i][:], start=True, stop=True),
                              reads=[b_da[i], bC32], writes=[b_psu], sig=(q == 2))
                    for q, dst in enumerate((toend[i], eL[i], ecum[i])):
                        S_.op("act", lambda h, q=q, dst=dst: h.activation(out=dst[:], in_=psu[:, 128 + q * 128:256 + q * 128], func=AF.Exp),
                              reads=[b_psu], writes=[b_sm[i]])
                    S_.op("dve", lambda h: h.tensor_tensor(out=wts[i][:], in0=toend[i][:], in1=dts[i][:], op=ALU.mult),
                          reads=[b_sm[i], b_da[i]], writes=[b_sm[i]])
                    if full:
                        S_.op("dve", lambda h: h.tensor_tensor(out=v3(xdt[i][:]), in0=v3(xs[i][:]),
                                                                in1=dts[i][:].unsqueeze(2).broadcast_to([128, 128, 64]), op=ALU.mult),
                              reads=[b_xs[i], b_da[i]], writes=[b_xdt[i]])
                    if not last:
                        S_.op("dve", lambda h: h.tensor_tensor(out=v3(xw[i][:]), in0=v3(xs[i][:]),
                                                                in1=wts[i][:].unsqueeze(2).broadcast_to([128, 128, 64]), op=ALU.mult),
                              reads=[b_xs[i], b_sm[i]], writes=[b_xw[i]])

                def stage_a0(u, un):
                    d, c, i, g, hg, k, full, last, first = u
                    if not full:
                        return
                    k3 = un % 3
                    if hg == 0:
                        S_.op("pe", lambda h: h.matmul(psu[:, 0:128], lhsT=BTc[i][:, g, :], rhs=CTc[i][:, g, :], start=True, stop=True),
                              reads=[b_BC[i]], writes=[b_psu])
                        S_.op("dve", lambda h: h.tensor_tensor(out=CBm[g % 2][:], in0=psu[:, 0:128], in1=TRI[d], op=ALU.mult),
                              reads=[b_psu, bC32], writes=[b_CBm[g % 2]])
                    h0 = g * 16 + hg * 8
                    if d == 1:
                        S_.dma("sp", yld[k3][:], ysum[c * 128:(c + 1) * 128, g * 1024 + hg * 512:g * 1024 + (hg + 1) * 512], writes=[b_yld[k3]])
                    else:
                        S_.op("dve", lambda h: h.tensor_tensor(out=v3(xD[k3][:]), in0=v3(xs[i][:, g * 1024 + hg * 512:g * 1024 + (hg + 1) * 512]),
                                                               in1=Dbc[:, h0:h0 + 8].unsqueeze(2).broadcast_to([128, 8, 64]), op=ALU.mult),
                              reads=[b_xs[i], b_D], writes=[b_xD[k3]])
                    for j in range(8):
                        S_.op("act", lambda h, j=j: h.activation(out=segr[k3][:, j, :], in_=TRI[d], func=AF.Copy, scale=as_[i][:, h0 + j:h0 + j + 1]),
                              reads=[b_da[i], bC32], writes=[b_segr[k3]])

                def stage_a1(u, un):
                    d, c, i, g, hg, k, full, last, first = u
                    if not full:
                        return
                    k3 = un % 3
                    sflat = segr[k3][:].rearrange("p a b -> p (a b)")
                    eflat = E[k][:].rearrange("p a b -> p (a b)")
                    for q in range(2):
                        S_.op("pe", lambda h, q=q: h.matmul(pseg[k][:, q * 512:(q + 1) * 512], lhsT=SU[d], rhs=sflat[:, q * 512:(q + 1) * 512],
                                                            start=True, stop=True),
                              reads=[b_segr[k3], bC32], writes=[b_pseg[k]], sig=(q == 1))
                    for q in range(2):
                        S_.op("act", lambda h, q=q: h.activation(out=eflat[:, q * 512:(q + 1) * 512], in_=pseg[k][:, q * 512:(q + 1) * 512], func=AF.Exp),
                              reads=[b_pseg[k]], writes=[b_E[k]])

                def stage_b(u, un):
                    d, c, i, g, hg, k, full, last, first = u
                    k3 = un % 3
                    r0 = c * 128
                    h0 = g * 16 + hg * 8
                    c0 = g * 1024 + hg * 512
                    if full:
                        S_.op("dve", lambda h: h.tensor_tensor(out=MT[k][:], in0=E[k][:], in1=CBm[g % 2][:].unsqueeze(1).broadcast_to([128, 8, 128]), op=ALU.mult),
                              reads=[b_E[k], b_CBm[g % 2]], writes=[b_MT[k]])
                        if d == 0:
                            S_.op("pe", lambda h: h.matmul(pyi[k][:, :], lhsT=IDb, rhs=xD[k3][:], start=True, stop=False),
                                  reads=[b_xD[k3], bC16], writes=[b_pyi[k]], sig=False)
                        else:
                            S_.op("pe", lambda h: h.matmul(pyi[k][:, :], lhsT=IDf, rhs=yld[k3][:], start=True, stop=False),
                                  reads=[b_yld[k3], bC32], writes=[b_pyi[k]], sig=False)
                        for j in range(8):
                            S_.op("pe", lambda h, j=j: h.matmul(pyi[k][:, j * 64:(j + 1) * 64], lhsT=MT[k][:, j, :], rhs=xdt[i][:, (h0 + j) * 64:(h0 + j + 1) * 64],
                                                                start=False, stop=(j == 7)),
                                  reads=[b_MT[k], b_xdt[i]], writes=[b_pyi[k]], sig=(j == 7))
                        S_.op("pe", lambda h: h.matmul(pyo[:, :], lhsT=CTc[i][:, g, :], rhs=Hb[:, g, hg * 512:(hg + 1) * 512], start=True, stop=True),
                              reads=[b_BC[i], b_Hb[g][hg]], writes=[b_pyo])
                        S_.op("dve", lambda h: h.tensor_tensor(out=v3(tmpy[k][:]), in0=v3(pyo[:, :]),
                                                               in1=ecum[i][:, h0:h0 + 8].unsqueeze(2).broadcast_to([128, 8, 64]), op=ALU.mult),
                              reads=[b_pyo, b_sm[i]], writes=[b_tmpy[k]])
                        S_.op("dve", lambda h: h.tensor_tensor(out=yst[k][:], in0=tmpy[k][:], in1=pyi[k][:], op=ALU.add),
                              reads=[b_tmpy[k], b_pyi[k]], writes=[b_yst[k]])
                        S_.dma("sp", ysum[r0:r0 + 128, c0:c0 + 512], yst[k][:], reads=[b_yst[k]])
                    if not last:
                        S_.op("pe", lambda h: h.matmul(psu[:, :], lhsT=bt[i][:, g * 128:(g + 1) * 128], rhs=xw[i][:, c0:c0 + 512], start=True, stop=True),
                              reads=[b_bt[i], b_xw[i]], writes=[b_psu])
                        S_.op("dve", lambda h: h.tensor_tensor(out=v3(tmph[k][:]), in0=v3(H[:, g, hg * 512:(hg + 1) * 512]),
                                                               in1=eL[i][:, h0:h0 + 8].unsqueeze(2).broadcast_to([128, 8, 64]), op=ALU.mult),
                              reads=[b_H[g][hg], b_sm[i]], writes=[b_tmph[k]])
                        S_.op("dve", lambda h: h.tensor_tensor(out=H[:, g, hg * 512:(hg + 1) * 512], in0=tmph[k][:], in1=psu[:, :], op=ALU.add),
                              reads=[b_tmph[k], b_psu], writes=[b_H[g][hg]])
                        S_.op("act", lambda h: h.activation(out=Hb[:, g, hg * 512:(hg + 1) * 512], in_=H[:, g, hg * 512:(hg + 1) * 512], func=AF.Copy),
                              reads=[b_H[g][hg]], writes=[b_Hb[g][hg]])

                for d in range(2):
                    for g in range(8):
                        for hg in range(2):
                            S_.op("dve", lambda h, g=g, hg=hg: h.memset(H[:, g, hg * 512:(hg + 1) * 512], 0.0), writes=[b_H[g][hg]])
                            S_.op("dve", lambda h, g=g, hg=hg: h.memset(Hb[:, g, hg * 512:(hg + 1) * 512], 0.0), writes=[b_Hb[g][hg]])
                    chunks = list(range(17)) if d == 0 else list(range(31, -1, -1))
                    units = []
                    cinfo = []
                    for c in chunks:
                        i = citer[0] % 2
                        citer[0] += 1
                        cinfo.append((d, c, i, c <= 16, c == chunks[-1]))
                        for g in range(8):
                            for hg in range(2):
                                units.append((d, c, i, g, hg, len(units) % 2, c <= 16, c == chunks[-1], g == 0 and hg == 0))
                    prologue(*cinfo[0])
                    prologue(*cinfo[1])
                    stage_a0(units[0], 0)
                    stage_a0(units[1], 1)
                    stage_a1(units[0], 0)
                    for ui, u in enumerate(units):
                        if ui + 2 < len(units):
                            stage_a0(units[ui + 2], ui + 2)
                        if ui + 1 < len(units):
                            stage_a1(units[ui + 1], ui + 1)
                        stage_b(u, ui)
                        if ui % 16 == 15:
                            ci = ui // 16
                            if ci + 2 < len(cinfo):
                                prologue(*cinfo[ci + 2])
                    S_.barrier()
                S_.store_q = "pool"
        p4()
        if stop_after <= 4:
            return finish(nc, S_, out_d)

        def p5():
            with ExitStack() as es:
                nw = sb(es, "g5nw", [128, 64], F32)
                ys = [sb(es, "g5ys%d" % i, [128, 4, 1024], F32) for i in range(2)]
                zs = [sb(es, "g5zs%d" % i, [128, 8, 512], F32) for i in range(2)]
                G = [sb(es, "g5G%d" % i, [128, 8, 512], F32) for i in range(2)]
                sq = [sb(es, "g5sq%d" % i, [128, 512], F32) for i in range(2)]
                rr = [sb(es, "g5rr%d" % i, [128, 512], F32) for i in range(2)]
                ob = [sb(es, "g5o%d" % i, [128, 512], BF16) for i in range(2)]
                pY = [ps(es, "g5pY%d" % i, [128, 512], F32) for i in range(2)]
                pSS = [ps(es, "g5pSS%d" % i, [128, 512], F32) for i in range(2)]
                b_nw = Buf()
                b_G, b_rr, b_pSS = ([Buf(), Buf()] for _ in range(3))
                b_ys, b_zs, b_sq, b_ob, b_pY = ([Buf(), Buf()] for _ in range(5))
                S_.dma("sp", nw[:], snw[:, :], writes=[b_nw])
                ysv = ysum.rearrange("(t p) c -> p t c", p=128)
                zsv = zsT.rearrange("(c p) t -> p c t", p=128)
                it = 0
                for (t0, n) in tok_blocks(0, NOWN):
                    nt = n // 128
                    for g in range(8):
                        i = it % 2
                        it += 1
                        S_.dma("sp", ys[i][:, 0:nt, :], ysv[:, t0 // 128:t0 // 128 + nt, g * 1024:(g + 1) * 1024], writes=[b_ys[i]])
                        S_.dma("sp", zs[i][:, :, 0:n], zsv[:, g * 8:(g + 1) * 8, t0:t0 + n], writes=[b_zs[i]])
                        for ct in range(8):
                            j = ct % 2
                            for q in range(nt):
                                S_.op("pe", lambda h, q=q, ct=ct, j=j, i=i: h.transpose(out=pY[j][:, q * 128:(q + 1) * 128],
                                                                                         in_=ys[i][:, q, ct * 128:(ct + 1) * 128], identity=IDf),
                                      reads=[b_ys[i], bC32], writes=[b_pY[j]], sig=(q == nt - 1))
                            S_.op("dve", lambda h, ct=ct, j=j, i=i, n=n: h.tensor_tensor(out=G[i][:, ct, 0:n], in0=pY[j][:, 0:n], in1=zs[i][:, ct, 0:n], op=ALU.mult),
                                  reads=[b_pY[j], b_zs[i]], writes=[b_G[i]])
                            S_.op("act", lambda h, ct=ct, j=j, n=n, i=i: h.activation(out=sq[j][:, 0:n], in_=G[i][:, ct, 0:n], func=AF.Square),
                                  reads=[b_G[i]], writes=[b_sq[j]])
                            S_.op("pe", lambda h, ct=ct, j=j, n=n, i=i: h.matmul(pSS[i][:, 0:n], lhsT=ONEf, rhs=sq[j][:, 0:n], start=(ct == 0), stop=(ct == 7)),
                                  reads=[b_sq[j], bC32], writes=[b_pSS[i]], sig=True)
                        S_.op("dve", lambda h, n=n, i=i: h.tensor_scalar(out=rr[i][:, 0:n], in0=pSS[i][:, 0:n], scalar1=1.0 / 1024.0, scalar2=RMS_EPS,
                                                                     op0=ALU.mult, op1=ALU.add), reads=[b_pSS[i]], writes=[b_rr[i]])
                        S_.op("act", lambda h, n=n, i=i: h.activation(out=rr[i][:, 0:n], in_=rr[i][:, 0:n], func=AF.Sqrt), reads=[b_rr[i]], writes=[b_rr[i]])
                        S_.op("dve", lambda h, n=n, i=i: h.reciprocal(out=rr[i][:, 0:n], in_=rr[i][:, 0:n]), reads=[b_rr[i]], writes=[b_rr[i]])
                        for ct in range(8):
                            j = ct % 2
                            cg = g * 8 + ct
                            S_.op("dve", lambda h, ct=ct, j=j, cg=cg, n=n, i=i: h.scalar_tensor_tensor(out=ob[j][:, 0:n], in0=G[i][:, ct, 0:n], scalar=nw[:, cg:cg + 1],
                                                                                                 in1=rr[i][:, 0:n], op0=ALU.mult, op1=ALU.mult),
                                  reads=[b_G[i], b_rr[i], b_nw], writes=[b_ob[j]])
                            S_.dma("sp", yssmT[cg * 128:(cg + 1) * 128, t0:t0 + n], ob[j][:, 0:n], reads=[b_ob[j]])
                S_.barrier()
        p5()
        if stop_after <= 5:
            return finish(nc, S_, out_d)

        def p6():
            with ExitStack() as es:
                gq = sb(es, "gq", [128, 1], F32)
                gk = sb(es, "gk", [128, 1], F32)
                xq = [sb(es, "a6x%d" % i, [128, 512], F32) for i in range(2)]
                cs = [sb(es, "a6c%d" % i, [128, 512], F32) for i in range(2)]
                sn = [sb(es, "a6s%d" % i, [128, 512], F32) for i in range(2)]
                sq = [sb(es, "a6sq%d" % i, [128, 512], F32) for i in range(2)]
                rr = [sb(es, "a6r%d" % i, [128, 512], F32) for i in range(2)]
                xn = [sb(es, "a6xn%d" % i, [128, 512], F32) for i in range(2)]
                t1 = [sb(es, "a6t1%d" % i, [128, 512], F32) for i in range(2)]
                t2 = [sb(es, "a6t2%d" % i, [128, 512], F32) for i in range(2)]
                ob = [sb(es, "a6o%d" % i, [128, 512], BF16) for i in range(2)]
                pS = [ps(es, "a6pS%d" % i, [128, 512], F32) for i in range(2)]
                pR = [ps(es, "a6pR%d" % i, [128, 512], F32) for i in range(2)]
                vrow = [sb(es, "a6v%d" % i, [128, S], BF16) for i in range(2)]
                vst = [sb(es, "a6vs%d" % i, [128, 32, 128], BF16) for i in range(2)]
                pT = [ps(es, "a6pT%d" % i, [128, 1024], BF16) for i in range(2)]
                b_g = Buf()
                (b_xq, b_cs, b_ob, b_vrow, b_vst, b_pT, b_sq, b_rr, b_xn, b_t1, b_t2, b_pS, b_pR) = ([Buf(), Buf()] for _ in range(13))
                S_.dma("sp", gq[:], qnw[:, :], writes=[b_g])
                S_.dma("sp", gk[:], knw[:, :], writes=[b_g])
                it = 0
                for (src, dst, nh, ntok, g) in ((kT, kTn, 8, S, gk), (qT, qTn, 32, NOWN, gq)):
                    for hd in range(nh):
                        for (t0, n) in tok_blocks(0, ntok):
                            i = it % 2
                            it += 1
                            S_.dma("sp", xq[i][:, 0:n], src[hd * 128:(hd + 1) * 128, t0:t0 + n], writes=[b_xq[i]])
                            S_.dma("sp", cs[i][:, 0:n], cos_d[:, t0:t0 + n], writes=[b_cs[i]])
                            S_.dma("sp", sn[i][:, 0:n], sin_d[:, t0:t0 + n], writes=[b_cs[i]])
                            S_.op("act", lambda h, i=i, n=n: h.activation(out=sq[i][:, 0:n], in_=xq[i][:, 0:n], func=AF.Square),
                                  reads=[b_xq[i]], writes=[b_sq[i]])
                            S_.op("pe", lambda h, i=i, n=n: h.matmul(pS[i][:, 0:n], lhsT=ONEf, rhs=sq[i][:, 0:n], start=True, stop=True),
                                  reads=[b_sq[i], bC32], writes=[b_pS[i]])
                            S_.op("dve", lambda h, i=i, n=n: h.tensor_scalar(out=rr[i][:, 0:n], in0=pS[i][:, 0:n], scalar1=1.0 / 128.0, scalar2=RMS_EPS,
                                                                              op0=ALU.mult, op1=ALU.add), reads=[b_pS[i]], writes=[b_rr[i]])
                            S_.op("act", lambda h, i=i, n=n: h.activation(out=rr[i][:, 0:n], in_=rr[i][:, 0:n], func=AF.Sqrt), reads=[b_rr[i]], writes=[b_rr[i]])
                            S_.op("dve", lambda h, i=i, n=n: h.reciprocal(out=rr[i][:, 0:n], in_=rr[i][:, 0:n]), reads=[b_rr[i]], writes=[b_rr[i]])
                            S_.op("dve", lambda h, i=i, n=n, g=g: h.scalar_tensor_tensor(out=xn[i][:, 0:n], in0=xq[i][:, 0:n], scalar=g[:, 0:1],
                                                                                         in1=rr[i][:, 0:n], op0=ALU.mult, op1=ALU.mult),
                                  reads=[b_xq[i], b_rr[i], b_g], writes=[b_xn[i]])
                            S_.op("pe", lambda h, i=i, n=n: h.matmul(pR[i][:, 0:n], lhsT=ROPf, rhs=xn[i][:, 0:n], start=True, stop=True),
                                  reads=[b_xn[i], bC32], writes=[b_pR[i]])
                            S_.op("dve", lambda h, i=i, n=n: h.tensor_tensor(out=t1[i][:, 0:n], in0=xn[i][:, 0:n], in1=cs[i][:, 0:n], op=ALU.mult),
                                  reads=[b_xn[i], b_cs[i]], writes=[b_t1[i]])
                            S_.op("dve", lambda h, i=i, n=n: h.tensor_tensor(out=t2[i][:, 0:n], in0=pR[i][:, 0:n], in1=sn[i][:, 0:n], op=ALU.mult),
                                  reads=[b_pR[i], b_cs[i]], writes=[b_t2[i]])
                            S_.op("dve", lambda h, i=i, n=n: h.tensor_tensor(out=ob[i][:, 0:n], in0=t1[i][:, 0:n], in1=t2[i][:, 0:n], op=ALU.add),
                                  reads=[b_t1[i], b_t2[i]], writes=[b_ob[i]])
                            S_.dma("sp", dst[hd * 128:(hd + 1) * 128, t0:t0 + n], ob[i][:, 0:n], reads=[b_ob[i]])
                vtv = v_tok.rearrange("(t p) c -> p t c", p=128)
                for hd in range(8):
                    i = hd % 2
                    S_.dma("sp", vrow[i][:], vT[hd * 128:(hd + 1) * 128, :], writes=[b_vrow[i]])
                    for g in range(4):
                        j = g % 2
                        for q in range(8):
                            tl = g * 8 + q
                            S_.op("pe", lambda h, tl=tl, q=q, j=j, i=i: h.transpose(out=pT[j][:, q * 128:(q + 1) * 128],
                                                                                     in_=vrow[i][:, tl * 128:(tl + 1) * 128], identity=IDb),
                                  reads=[b_vrow[i], bC16], writes=[b_pT[j]], sig=(q == 7))
                        S_.op("dve", lambda h, g=g, j=j, i=i: h.tensor_copy(out=vst[i][:, g * 8:(g + 1) * 8, :].rearrange("p a b -> p (a b)"), in_=pT[j][:]),
                              reads=[b_pT[j]], writes=[b_vst[i]])
                    S_.dma("sp", vtv[:, :, hd * 128:(hd + 1) * 128], vst[i][:], reads=[b_vst[i]])
                S_.barrier()
        p6()
        if stop_after <= 6:
            return finish(nc, S_, out_d)

        def p7():
            scale = 128.0 ** -0.5
            with ExitStack() as es:
                ksb = [sb(es, "a7k%d" % i, [128, S], BF16) for i in range(2)]
                vsb = [sb(es, "a7v%d" % i, [128, 32, 128], BF16) for i in range(2)]
                qsb = [sb(es, "a7q%d" % i, [128, 512], BF16) for i in range(2)]
                pt = [sb(es, "a7p%d" % i, [128, 2, 512], BF16) for i in range(3)]
                rec = sb(es, "a7rec", [128, 512], F32)
                lacc = sb(es, "a7lacc", [128, 512], F32)
                osb = [sb(es, "a7o%d" % i, [128, 512], BF16) for i in range(2)]
                pS = [ps(es, "a7pS%d" % i, [128, 2, 512], F32) for i in range(2)]
                pO = [ps(es, "a7pO%d" % i, [128, 512], F32) for i in range(2)]
                pL = [ps(es, "a7pL%d" % i, [128, 512], F32) for i in range(2)]
                b_k, b_v, b_q, b_o, b_pO, b_pL, b_pS = ([Buf(), Buf()] for _ in range(7))
                b_pt = [Buf() for _ in range(3)]
                b_rec, b_lacc = Buf(), Buf()
                vtv = v_tok.rearrange("(t p) c -> p t c", p=128)
                it = 0
                sc = 0
                for kvh in range(8):
                    ki = kvh % 2
                    S_.dma("sp", ksb[ki][:], kTn[kvh * 128:(kvh + 1) * 128, :], writes=[b_k[ki]])
                    S_.dma("sp", vsb[ki][:], vtv[:, :, kvh * 128:(kvh + 1) * 128], writes=[b_v[ki]])
                    for qh in range(4):
                        hd = kvh * 4 + qh
                        for (t0, n) in tok_blocks(0, NOWN):
                            i = it % 2
                            it += 1
                            S_.dma("sp", qsb[i][:, 0:n], qTn[hd * 128:(hd + 1) * 128, t0:t0 + n], writes=[b_q[i]])
                            if it <= 86:
                                S_.dma("pool", wdb[(it - 1) * 128:it * 128, :], w_down[(it - 1) * 128:it * 128, :])

                            def smm(pi, i=i, n=n, ki=ki):
                                js, jp = (sc + pi) % 2, (sc + pi) % 3
                                for e in range(2):
                                    kt = 2 * pi + e
                                    S_.op("pe", lambda h, e=e, kt=kt: h.matmul(pS[js][:, e, 0:n], lhsT=ksb[ki][:, kt * 128:(kt + 1) * 128], rhs=qsb[i][:, 0:n],
                                                                               start=True, stop=True), reads=[b_k[ki], b_q[i]], writes=[b_pS[js]], sig=(e == 1))
                                S_.op("act", lambda h: h.activation(out=pt[jp][:, :, 0:n], in_=pS[js][:, :, 0:n], func=AF.Exp, scale=scale),
                                      reads=[b_pS[js]], writes=[b_pt[jp]])

                            def pv(pi, i=i, n=n, ki=ki):
                                jp = (sc + pi) % 3
                                for e in range(2):
                                    kt = 2 * pi + e
                                    S_.op("pe", lambda h, e=e, kt=kt: h.matmul(pO[i][:, 0:n], lhsT=vsb[ki][:, kt, :], rhs=pt[jp][:, e, 0:n],
                                                                               start=(kt == 0), stop=(kt == 31)), reads=[b_v[ki], b_pt[jp]], writes=[b_pO[i]], sig=(kt == 31))
                                S_.op("pe", lambda h: h.matmul(pL[i][:, 0:n], lhsT=ONEb, rhs=pt[jp][:, 0, 0:n], start=(pi == 0), stop=False),
                                      reads=[bC16, b_pt[jp]], writes=[b_pL[i]], sig=True)
                                if pi == 0:
                                    S_.op("dve", lambda h: h.tensor_copy(out=lacc[:, 0:n], in_=pt[jp][:, 1, 0:n]), reads=[b_pt[jp]], writes=[b_lacc])
                                else:
                                    S_.op("dve", lambda h: h.tensor_tensor(out=lacc[:, 0:n], in0=lacc[:, 0:n], in1=pt[jp][:, 1, 0:n], op=ALU.add),
                                          reads=[b_pt[jp]], writes=[b_lacc])
                            smm(0)
                            smm(1)
                            for pi in range(16):
                                pv(pi)
                                if pi + 2 < 16:
                                    smm(pi + 2)
                            sc += 16
                            S_.op("pe", lambda h, i=i, n=n: h.matmul(pL[i][:, 0:n], lhsT=ONEf, rhs=lacc[:, 0:n], start=False, stop=True),
                                  reads=[bC32, b_lacc], writes=[b_pL[i]])
                            S_.op("dve", lambda h, i=i, n=n: h.reciprocal(out=rec[:, 0:n], in_=pL[i][:, 0:n]), reads=[b_pL[i]], writes=[b_rec])
                            S_.op("dve", lambda h, i=i, n=n: h.tensor_tensor(out=osb[i][:, 0:n], in0=pO[i][:, 0:n], in1=rec[:, 0:n], op=ALU.mult),
                                  reads=[b_pO[i], b_rec], writes=[b_o[i]])
                            S_.dma("sp", yattnT[hd * 128:(hd + 1) * 128, t0:t0 + n], osb[i][:, 0:n], reads=[b_o[i]])
                S_.barrier()
        p7()
        if stop_after <= 7:
            return finish(nc, S_, out_d)

        def p8():
            def mk1(es_):
                gs, ts = Stage(es_, "p8g", F32), Stage(es_, "p8t", F32)

                def evac(es, p, c, t, n, b_p, idx):
                    gl, bg_ = gs.nxt()
                    tl, bt_ = ts.nxt()
                    S_.dma("sp", gl[:, 0:n], gatesT[c:c + 128, t:t + n], writes=[bg_])
                    S_.op("dve", lambda h: h.tensor_tensor(out=tl[:, 0:n], in0=p, in1=gl[:, 0:n], op=ALU.mult), reads=[b_p, bg_], writes=[bt_])
                    S_.dma("sp", t1T[c:c + 128, t:t + n], tl[:, 0:n], reads=[bt_])
                return evac
            mm_A("p8a", yssmT, 64, 0, 1088, w_sp, 0, D, mk1, cw=128)
            mm_A("p8b", yssmT, 64, 1088, 1088, w_sp, 0, D, mk1, cw=128)

            def mk2(es_):
                gs, ts, ms, os_ = Stage(es_, "p8g2", F32), Stage(es_, "p8t2", F32), Stage(es_, "p8m", F32), Stage(es_, "p8o", BF16)

                def evac(es, p, c, t, n, b_p, idx):
                    gl, bg_ = gs.nxt()
                    tl, bt_ = ts.nxt()
                    ml, bm_ = ms.nxt()
                    ol, bo_ = os_.nxt()
                    S_.dma("sp", gl[:, 0:n], gatesT[D + c:D + c + 128, t:t + n], writes=[bg_])
                    S_.dma("sp", tl[:, 0:n], t1T[c:c + 128, t:t + n], writes=[bt_])
                    S_.op("dve", lambda h: h.tensor_tensor(out=ml[:, 0:n], in0=p, in1=gl[:, 0:n], op=ALU.mult), reads=[b_p, bg_], writes=[bm_])
                    S_.op("dve", lambda h: h.tensor_tensor(out=ol[:, 0:n], in0=ml[:, 0:n], in1=tl[:, 0:n], op=ALU.add), reads=[bm_, bt_], writes=[bo_])
                    S_.dma("sp", mixT[c:c + 128, t:t + n], ol[:, 0:n], reads=[bo_])
                return evac
            mm_A("p8c", yattnT, KT, 0, NOWN, w_ap, 0, D, mk2)
        p8()
        if stop_after <= 8:
            return finish(nc, S_, out_d)

        def mm_B(name, act_d, kt_n, tok0, ntok, W_d, ncols, dst_d, dst_t0, cw=256):
            with ExitStack() as es:
                S_.store_q = "sp"
                A = sb(es, name + "_A", [128, kt_n, ntok], BF16)
                Wt = [sb(es, name + "_W%d" % i, [128, kt_n, cw], BF16) for i in range(2)]
                pp = [ps(es, name + "_p%d" % i, [128, 512], F32) for i in range(6)]
                stg = Stage(es, name + "_s", F32, n=4, w=cw)
                b_A, b_W, b_pp = Buf(), [Buf(), Buf()], [Buf() for _ in range(6)]
                av = act_d.rearrange("(k p) t -> p k t", p=128)
                step = 8 if kt_n % 8 == 0 else 2
                for k0 in range(0, kt_n, step):
                    S_.dma("sp", A[:, k0:k0 + step, :], av[:, k0:k0 + step, tok0:tok0 + ntok], writes=[b_A])
                wv = W_d.rearrange("(k p) c -> p k c", p=128)
                cnt = 0
                for wi, c0 in enumerate(range(0, ncols, cw)):
                    i = wi % 2
                    S_.dma("pool", Wt[i][:], wv[:, :, c0:c0 + cw], writes=[b_W[i]])
                    for tt in range(ntok // 128):
                        pi = cnt % 6
                        cnt += 1
                        for kt in range(kt_n):
                            S_.op("pe", lambda h, kt=kt, i=i, tt=tt, pi=pi: h.matmul(
                                pp[pi][:, 0:cw], lhsT=A[:, kt, tt * 128:(tt + 1) * 128], rhs=Wt[i][:, kt, :],
                                start=(kt == 0), stop=(kt == kt_n - 1)),
                                reads=[b_A, b_W[i]], writes=[b_pp[pi]], sig=(kt == kt_n - 1))
                        tl, bt_ = stg.nxt()
                        if cnt % 2 == 0:
                            S_.op("act", lambda h, pi=pi, tl=tl: h.activation(out=tl[:, 0:cw], in_=pp[pi][:, 0:cw], func=AF.Copy), reads=[b_pp[pi]], writes=[bt_])
                        else:
                            S_.op("dve", lambda h, pi=pi, tl=tl: h.tensor_copy(out=tl[:, 0:cw], in_=pp[pi][:, 0:cw]), reads=[b_pp[pi]], writes=[bt_])
                        r0 = dst_t0 + tt * 128
                        S_.dma("sp", dst_d[r0:r0 + 128, c0:c0 + cw], tl[:, 0:cw], reads=[bt_])
                S_.barrier()
                S_.store_q = "pool"

        mm_B("p9", mixT, KT, 0, NOWN, w_out, D, mixed, 0)

        def ln_affine_pass(name, a_d, res_d, gcol0, lng_d, lnb_d, ntiles, dst_d, do_h2):
            with ExitStack() as es:
                gbc = sb(es, name + "gbc", [128, D], F32)
                lg = sb(es, name + "lg", [128, D], F32)
                lb = sb(es, name + "lb", [128, D], F32)
                at = sb(es, name + "at", [128, D], F32)
                rt = [sb(es, name + "rt%d" % i, [128, D], F32) for i in range(2)]
                st = sb(es, name + "st", [128, 8, 6], F32)
                mv = sb(es, name + "mv", [128, 2], F32)
                rstd = sb(es, name + "rstd", [128, 1], F32)
                b_bc, b_at, b_st = Buf(), Buf(), Buf()
                b_rt = [Buf(), Buf()]
                bc = lambda ap: ap.partition_broadcast(128).rearrange("p a b -> p (a b)")
                S_.dma("sp", gbc[:], bc(modD[0:1, gcol0:gcol0 + D]), writes=[b_bc])
                S_.dma("sp", lg[:], bc(lng_d[0:1, :]), writes=[b_bc])
                S_.dma("sp", lb[:], bc(lnb_d[0:1, :]), writes=[b_bc])
                if do_h2:
                    xn = sb(es, name + "xn", [128, D], F32)
                    hblk = sb(es, name + "hblk", [128, KT, 512], BF16)
                    pT = [ps(es, name + "pT%d" % i, [128, 512], F32) for i in range(2)]
                    b_xn, b_hblk = Buf(), Buf()
                    b_pT = [Buf(), Buf()]
                    h2v = h2T.rearrange("(k p) t -> p k t", p=128)
                for tt in range(ntiles):
                    i = tt % 2
                    r0 = tt * 128
                    S_.dma("sp", at[:], a_d[r0:r0 + 128, :], writes=[b_at])
                    S_.dma("sp", rt[i][:], res_d[r0:r0 + 128, :], writes=[b_rt[i]])
                    S_.op("dve", lambda h: h.tensor_tensor(out=at[:], in0=at[:], in1=gbc[:], op=ALU.mult), reads=[b_bc], writes=[b_at])
                    S_.op("dve", lambda h, i=i: h.scalar_tensor_tensor(out=rt[i][:], in0=rt[i][:], scalar=float(ALPHA), in1=at[:], op0=ALU.mult, op1=ALU.add),
                          reads=[b_at], writes=[b_rt[i]])
                    ln_stats(es, rt[i], b_rt[i], st, mv, rstd, b_st)
                    S_.op("dve", lambda h, i=i: h.tensor_scalar(out=rt[i][:], in0=rt[i][:], scalar1=mv[:, 0:1], scalar2=rstd[:, 0:1],
                                                                 op0=ALU.subtract, op1=ALU.mult), reads=[b_st], writes=[b_rt[i]])
                    S_.op("dve", lambda h, i=i: h.tensor_tensor(out=rt[i][:], in0=rt[i][:], in1=lg[:], op=ALU.mult), reads=[b_bc], writes=[b_rt[i]])
                    S_.op("dve", lambda h, i=i: h.tensor_tensor(out=rt[i][:], in0=rt[i][:], in1=lb[:], op=ALU.add), reads=[b_bc], writes=[b_rt[i]])
                    S_.dma("sp", dst_d[r0:r0 + 128, :], rt[i][:], reads=[b_rt[i]])
                    if do_h2:
                        ln_stats(es, rt[i], b_rt[i], st, mv, rstd, b_st)
                        S_.op("dve", lambda h, i=i: h.tensor_scalar(out=xn[:], in0=rt[i][:], scalar1=mv[:, 0:1], scalar2=rstd[:, 0:1],
                                                                     op0=ALU.subtract, op1=ALU.mult), reads=[b_rt[i], b_st], writes=[b_xn])
                        sub = tt % 4
                        modT(xn, b_xn, hblk, b_hblk, sub, pT, b_pT, 128, 96)
                        if sub == 3 or tt == ntiles - 1:
                            t0 = (tt // 4) * 512
                            w = (sub + 1) * 128
                            S_.dma("sp", h2v[:, :, t0:t0 + w], hblk[:, :, 0:w], reads=[b_hblk])
                S_.barrier()
        ln_affine_pass("l1", mixed, x_d, 2 * D, ln1g, ln1b, NOWN // 128, x1d, True)
        if stop_after <= 9:
            return finish(nc, S_, out_d)

        def p10():
            def mk(es_):
                st_ = Stage(es_, "p10s", F32)

                def evac(es, p, c, t, n, b_p, idx):
                    tl, bt_ = st_.nxt()
                    if idx % 2 == 0:
                        S_.op("act", lambda h: h.activation(out=tl[:, 0:n], in_=p, func=AF.Copy), reads=[b_p], writes=[bt_])
                    else:
                        S_.op("dve", lambda h: h.tensor_copy(out=tl[:, 0:n], in_=p), reads=[b_p], writes=[bt_])
                    S_.dma("sp", upre[c:c + 128, t:t + n], tl[:, 0:n], reads=[bt_])
                return evac
            mm_A("p10", h2T, KT, 0, NOWN, w_up, 0, 2 * FFN, mk)
        p10()

        def p11():
            NW = NOUT + 2
            with ExitStack() as es:
                xin = [sb(es, "fxin%d" % i, [128, NW], F32) for i in range(4)]
                acc = [sb(es, "facc%d" % i, [128, NOUT], F32) for i in range(2)]
                sa = sb(es, "fsa", [128, NOUT], F32)
                ob = [sb(es, "fob%d" % i, [128, NOUT], BF16) for i in range(2)]
                cw = sb(es, "fcw", [128, 172, 3], F32)
                cbs = sb(es, "fcb", [128, 172], F32)
                b_xin = [Buf() for _ in range(4)]
                b_acc, b_ob = [Buf(), Buf()], [Buf(), Buf()]
                b_cw, b_sa = Buf(), Buf()
                S_.dma("sp", cw[:], fcw[:, :, :], writes=[b_cw])
                S_.dma("sp", cbs[:], fcb[:, :], writes=[b_cw])
                for i in range(4):
                    S_.op("dve", lambda h, i=i: h.memset(xin[i][:, 0:1], 0.0), writes=[b_xin[i]])
                for blk in range(86):
                    for half in range(2):
                        ci = half * 86 + blk
                        i = (blk % 2) * 2 + half
                        S_.dma("sp", xin[i][:, 1:NW], upre[ci * 128:(ci + 1) * 128, 0:NOUT + 1], writes=[b_xin[i]])
                        S_.op("act", lambda h, i=i, ci=ci, half=half: h.activation(out=acc[half][:], in_=xin[i][:, 1:NOUT + 1], func=AF.Identity,
                                                                                scale=cw[:, ci, 1:2], bias=cbs[:, ci:ci + 1]),
                              reads=[b_xin[i], b_cw], writes=[b_acc[half]])
                        S_.op("dve", lambda h, i=i, ci=ci, half=half: h.scalar_tensor_tensor(out=acc[half][:], in0=xin[i][:, 0:NOUT], scalar=cw[:, ci, 0:1],
                                                                                          in1=acc[half][:], op0=ALU.mult, op1=ALU.add),
                              reads=[b_xin[i], b_cw], writes=[b_acc[half]])
                        S_.op("dve", lambda h, i=i, ci=ci, half=half: h.scalar_tensor_tensor(out=acc[half][:], in0=xin[i][:, 2:NOUT + 2], scalar=cw[:, ci, 2:3],
                                                                                          in1=acc[half][:], op0=ALU.mult, op1=ALU.add),
                              reads=[b_xin[i], b_cw], writes=[b_acc[half]])
                    j = blk % 2
                    S_.op("act", lambda h: h.activation(out=sa[:], in_=acc[0][:], func=AF.Silu), reads=[b_acc[0]], writes=[b_sa])
                    S_.op("dve", lambda h, j=j: h.tensor_tensor(out=ob[j][:], in0=sa[:], in1=acc[1][:], op=ALU.mult), reads=[b_sa, b_acc[1]], writes=[b_ob[j]])
                    S_.dma("sp", actT[blk * 128:(blk + 1) * 128, :], ob[j][:], reads=[b_ob[j]])
                S_.barrier()
        p11()
        if stop_after <= 11:
            return finish(nc, S_, out_d)

        for sbk in range(4):
            mm_B("p12_%d" % sbk, actT, 86, sbk * 512, 512, wdb, D, fd, sbk * 512)
        ln_affine_pass("l2", fd, x1d, 5 * D, ln2g, ln2b, NOUT // 128, out_d, False)

        return finish(nc, S_, out_d)


def finish(nc, S_, out_d):
    S_.barrier()
    return nc


def _consts():
    c = np.zeros((128, 8, 128), np.float32)
    i = np.arange(128)
    c[:, 0, :] = np.eye(128, dtype=np.float32)
    c[:, 1, :] = 1.0
    c[:, 2, :] = (i[:, None] <= i[None, :])
    c[:, 3, :] = (i[:, None] > i[None, :])
    c[:, 4, :] = (i[:, None] >= i[None, :])
    c[:, 5, :] = (i[:, None] < i[None, :])
    P = np.zeros((128, 128), np.float32)
    for base in (0, 64):
        for j in range(32):
            P[base + j, base + 32 + j] = -1.0
            P[base + 32 + j, base + j] = 1.0
    c[:, 6, :] = P.T
    return c


def _rope_tables(flip):
    t = np.arange(S)
    if flip:
        t = t[::-1]
    row = (t // 64).astype(np.float32)
    colp = (t % 64).astype(np.float32)
    inv = (1.0 / (np.float32(10000.0) ** (np.arange(32, dtype=np.float32) / np.float32(32)))).astype(np.float32)
    cos = np.zeros((128, S), np.float32)
    sin = np.zeros((128, S), np.float32)
    for d in range(128):
        pos = row if d < 64 else colp
        ang = (pos * inv[d % 32]).astype(np.float32)
        cos[d] = np.cos(ang)
        sin[d] = np.sin(ang)
    return cos, sin


def _pp(v, nb):
    return np.ascontiguousarray(np.asarray(v, np.float32).reshape(nb, 128).T)


def make_in_maps(inputs, cores):
    f = lambda k: np.asarray(inputs[k], np.float32)
    w_in0 = np.ascontiguousarray(f("w_in")[0])
    w_in1 = w_in0.copy()
    w_in1[:, DT0:DT0 + 128] = w_in0[:, DT0 + 128:DT0 + 256]
    w_in1[:, DT0 + 128:DT0 + 256] = w_in0[:, DT0:DT0 + 128]
    consts = _consts()
    ropes = [_rope_tables(0), _rope_tables(1)]
    maps = []
    for core in cores:
        b, hf = core // 2, core % 2
        xl = f("x")[b]
        if hf:
            xl = xl[::-1]
        taps = [2, 1, 0] if hf else [0, 1, 2]
        dirs = [1, 0] if hf else [0, 1]
        scw = f("ssm_conv_w")[0][taps]
        fcw = f("ffn_conv_w")[0][taps]
        m = {
            "x": np.ascontiguousarray(xl),
            "c": _pp(f("c")[b], 32),
            "w_ada": f("w_ada")[0], "b_ada": f("b_ada")[0].reshape(1, -1),
            "w_in": w_in1 if hf else w_in0,
            "ssm_conv_w": np.ascontiguousarray(scw.reshape(3, 80, 128).transpose(2, 1, 0)),
            "ssm_conv_b": _pp(f("ssm_conv_b")[0], 80),
            "ssm_a_log": np.ascontiguousarray(f("ssm_a_log")[0][dirs].T),
            "ssm_dt_bias": np.ascontiguousarray(f("ssm_dt_bias")[0][dirs].T),
            "ssm_d": f("ssm_d")[0].reshape(1, 128),
            "ssm_norm_w": _pp(f("ssm_norm_w")[0], 64),
            "q_norm_w": f("q_norm_w")[0].reshape(128, 1), "k_norm_w": f("k_norm_w")[0].reshape(128, 1),
            "w_ssm_proj": f("w_ssm_proj")[0], "w_attn_proj": f("w_attn_proj")[0],
            "w_gate": f("w_gate")[0], "b_gate": _pp(f("b_gate")[0], 64),
            "w_out": f("w_out")[0], "ln1_g": f("ln1_g")[0].reshape(1, -1), "ln1_b": f("ln1_b")[0].reshape(1, -1),
            "w_up": f("w_up")[0],
            "ffn_conv_w": np.ascontiguousarray(fcw.reshape(3, 172, 128).transpose(2, 1, 0)),
            "ffn_conv_b": _pp(f("ffn_conv_b")[0], 172),
            "w_down": f("w_down")[0], "ln2_g": f("ln2_g")[0].reshape(1, -1), "ln2_b": f("ln2_b")[0].reshape(1, -1),
            "rope_cos": ropes[hf][0], "rope_sin": ropes[hf][1],
            "consts": consts,
        }
        maps.append(m)
    return maps


def kernel(**inputs):
    nc = build_nc()
    cores = list(range(8))
    maps = make_in_maps(inputs, cores)
    res = run_bass_kernel_spmd(nc, maps, core_ids=cores)
    out = np.zeros((4, S, D), np.float32)
    for core in cores:
        b, hf = core // 2, core % 2
        o = np.asarray(res.results[core]["out"])
        if hf:
            out[b, NOUT:] = o[::-1]
        else:
            out[b, :NOUT] = o
    return out
```

```python
import math
from contextlib import ExitStack
import numpy as np
import concourse.bass as bass
import concourse.mybir as mybir
from concourse.bass_utils import run_bass_kernel_spmd

F32 = mybir.dt.float32
BF16 = mybir.dt.bfloat16
ALU = mybir.AluOpType
AF = mybir.ActivationFunctionType

D = 4096
S = 4096
NOWN = 2176
NOUT = 2048
KT = 32
DI = 8192
NCONV = 10240
FFN = 11008
IN_COLS = 24832
Z0, XBC0, DT0, Q0, K0, V0 = 0, 8192, 18432, 18688, 22784, 23808
ALPHA = 2.0 ** 0.25
LN_EPS = 1e-5
RMS_EPS = 1e-6


class Buf:
    __slots__ = ("w", "r")

    def __init__(self):
        self.w = None
        self.r = {}


class Sched:
    def __init__(self, nc, es, ndma=20):
        self.nc = nc
        self.engs = {}
        for name, h in (("pe", nc.tensor), ("act", nc.scalar), ("dve", nc.vector),
                        ("pool", nc.gpsimd), ("sp", nc.sync)):
            sem = es.enter_context(nc.semaphore("s_" + name))
            self.engs[name] = {"name": name, "h": h, "sem": sem, "cnt": 0, "seen": {}}
        self.dsems = {q: [[es.enter_context(nc.semaphore("d%s%d" % (q, i)), ), 0] for i in range(n)]
                      for q, n in (("sp", 12), ("pool", 10), ("act", 6))}
        self.di = {"sp": 0, "pool": 0, "act": 0}
        self.store_q = "pool"

    def _wait(self, e, ev):
        sem, val, src = ev
        if src == "pe" and e["name"] == "pe":
            return
        k = id(sem)
        if e["seen"].get(k, 0) >= val:
            return
        e["h"].wait_ge(sem, val)
        e["seen"][k] = val

    def _deps(self, e, reads, writes):
        for b in reads:
            if b.w is not None:
                self._wait(e, b.w)
        for b in writes:
            if b.w is not None:
                self._wait(e, b.w)
            for ev in b.r.values():
                self._wait(e, ev)

    def _record(self, ev, reads, writes):
        for b in reads:
            b.r[ev[2]] = ev
        for b in writes:
            b.w = ev
            b.r = {}

    def op(self, en, fn, reads=(), writes=(), sig=True):
        e = self.engs[en]
        self._deps(e, reads, writes)
        ins = fn(e["h"])
        if sig:
            e["cnt"] += 1
            ins.then_inc(e["sem"], 1)
            ev = (e["sem"], e["cnt"], en)
        else:
            ev = (e["sem"], e["cnt"] + 1, en)
        self._record(ev, reads, writes)

    def dma(self, en, out, in_, reads=(), writes=(), **kw):
        if en == "sp" and str(out.space) == "DRAM":
            en = self.store_q
        e = self.engs[en]
        self._deps(e, reads, writes)
        pool_ = self.dsems[en]
        idx = self.di[en]
        slot = pool_[idx]
        self.di[en] = (idx + 1) % len(pool_)
        key = "dq%s%d" % (en, idx)
        if slot[1] > 0:
            self._wait(e, (slot[0], slot[1], key))
        ins = e["h"].dma_start(out=out, in_=in_, **kw)
        slot[1] += 16
        ins.then_inc(slot[0], 16)
        self._record((slot[0], slot[1], key), reads, writes)

    def barrier(self):
        for e in self.engs.values():
            for o in self.engs.values():
                if o is not e and o["cnt"] > 0:
                    self._wait(e, (o["sem"], o["cnt"], o["name"] + "_b"))
            for q, pool_ in self.dsems.items():
                for i, sl in enumerate(pool_):
                    if sl[1] > 0:
                        self._wait(e, (sl[0], sl[1], "dq%s%d" % (q, i)))


def tok_blocks(n0, n, bs=512):
    out = []
    t = n0
    while t < n0 + n:
        m = min(bs, n0 + n - t)
        out.append((t, m))
        t += m
    return out


class K:
    def __init__(self, nc, dev_out):
        self.nc = nc
        self.dev_out = set(dev_out)
        self.flip = 0

    def dram(self, name, shape, dt, inp=False, out=False):
        if inp:
            return self.nc.dram_tensor(name, shape, dt, kind="ExternalInput").ap()
        if out or name in self.dev_out:
            return self.nc.dram_tensor(name, shape, dt, kind="ExternalOutput").ap()
        return self.nc.dram_tensor(name, shape, dt).ap()


def build_nc(dev_out=(), stop_after=99):
    nc = bass.Bass("TRN2", target_bir_lowering=False)
    kb = K(nc, dev_out)
    dram = kb.dram
    x_d = dram("x", [S, D], F32, inp=True)
    c_d = dram("c", [128, KT], F32, inp=True)
    w_ada = dram("w_ada", [D, 6 * D], F32, inp=True)
    b_ada = dram("b_ada", [1, 6 * D], F32, inp=True)
    w_in = dram("w_in", [D, IN_COLS], F32, inp=True)
    scw = dram("ssm_conv_w", [128, 80, 3], F32, inp=True)
    scb = dram("ssm_conv_b", [128, 80], F32, inp=True)
    alog = dram("ssm_a_log", [128, 2], F32, inp=True)
    dtb = dram("ssm_dt_bias", [128, 2], F32, inp=True)
    ssmd = dram("ssm_d", [1, 128], F32, inp=True)
    snw = dram("ssm_norm_w", [128, 64], F32, inp=True)
    qnw = dram("q_norm_w", [128, 1], F32, inp=True)
    knw = dram("k_norm_w", [128, 1], F32, inp=True)
    w_sp = dram("w_ssm_proj", [DI, D], F32, inp=True)
    w_ap = dram("w_attn_proj", [D, D], F32, inp=True)
    w_gate = dram("w_gate", [D, 2 * D], F32, inp=True)
    b_gate = dram("b_gate", [128, 64], F32, inp=True)
    w_out = dram("w_out", [D, D], F32, inp=True)
    ln1g = dram("ln1_g", [1, D], F32, inp=True)
    ln1b = dram("ln1_b", [1, D], F32, inp=True)
    w_up = dram("w_up", [D, 2 * FFN], F32, inp=True)
    fcw = dram("ffn_conv_w", [128, 172, 3], F32, inp=True)
    fcb = dram("ffn_conv_b", [128, 172], F32, inp=True)
    w_down = dram("w_down", [FFN, D], F32, inp=True)
    ln2g = dram("ln2_g", [1, D], F32, inp=True)
    ln2b = dram("ln2_b", [1, D], F32, inp=True)
    cos_d = dram("rope_cos", [128, S], F32, inp=True)
    sin_d = dram("rope_sin", [128, S], F32, inp=True)
    cst = dram("consts", [128, 8, 128], F32, inp=True)
    out_d = dram("out", [NOUT, D], F32, out=True)
    modD = dram("modD", [1, 6 * D], F32)
    hT = dram("hT", [D, S], BF16)
    zsT = dram("zsT", [DI, NOWN], F32)
    xbcpre = dram("xbcpre", [NCONV, S], F32)
    dtraw = dram("dtraw", [256, S], F32)
    qT = dram("qT", [D, NOWN], F32)
    kT = dram("kT", [1024, S], F32)
    vT = dram("vT", [1024, S], BF16)
    gatesT = dram("gatesT", [2 * D, NOWN], F32)
    xs_tok = dram("xs_tok", [S, DI], BF16)
    B_tok = dram("B_tok", [S, 1024], BF16)
    BTd = dram("BTd", [1024, S], BF16)
    CTd = dram("CTd", [1024, S], BF16)
    dt_tok = dram("dt_tok", [S, 256], F32)
    a_tok = dram("a_tok", [S, 256], F32)
    ysum = dram("ysum", [NOWN, DI], F32)
    yssmT = dram("yssmT", [DI, NOWN], BF16)
    qTn = dram("qTn", [D, NOWN], BF16)
    kTn = dram("kTn", [1024, S], BF16)
    v_tok = dram("v_tok", [S, 1024], BF16)
    yattnT = dram("yattnT", [D, NOWN], BF16)
    t1T = dram("t1T", [D, NOWN], F32)
    mixT = dram("mixT", [D, NOWN], BF16)
    mixed = dram("mixed", [NOWN, D], F32)
    x1d = dram("x1d", [NOWN, D], F32)
    h2T = dram("h2T", [D, NOWN], BF16)
    upre = dram("upre", [2 * FFN, NOWN], F32)
    actT = dram("actT", [FFN, NOUT], BF16)
    fd = dram("fd", [NOUT, D], F32)
    wdb = dram("wdb", [FFN, D], BF16)

    with ExitStack() as es0:
        S_ = Sched(nc, es0)
        uid = [0]

        def sb(es, name, shape, dt):
            uid[0] += 1
            return es.enter_context(nc.sbuf_tensor("%s_%d" % (name, uid[0]), shape, dt))

        def ps(es, name, shape, dt):
            uid[0] += 1
            return es.enter_context(nc.psum_tensor("%s_%d" % (name, uid[0]), shape, dt))
        C32 = sb(es0, "C32", [128, 8, 128], F32)
        C16 = sb(es0, "C16", [128, 8, 128], BF16)
        modP = sb(es0, "modP", [128, 192], F32)
        bC32, bC16, bmodP = Buf(), Buf(), Buf()
        S_.dma("sp", C32[:], cst[:, :, :], writes=[bC32])
        S_.dma("pool", C16[:], cst[:, :, :], writes=[bC16])
        IDf, ONEf, TRIf, SUf, TRIb, SUb, ROPf = [C32[:, i, :] for i in range(7)]
        IDb, ONEb = C16[:, 0, :], C16[:, 1, :]
        TRI = (TRIf, TRIb)
        SU = (SUf, SUb)

        with ExitStack() as es:
            ct = sb(es, "ct", [128, KT], F32)
            condT = sb(es, "condT", [128, KT], BF16)
            wb = [sb(es, "wada%d" % i, [128, KT, 512], BF16) for i in range(3)]
            brow = [sb(es, "brow%d" % i, [1, 512], F32) for i in range(2)]
            mrow = [sb(es, "mrow%d" % i, [1, 512], F32) for i in range(2)]
            pm = [ps(es, "pm%d" % i, [1, 512], F32) for i in range(2)]
            pcol = ps(es, "pcol", [128, 192], F32)
            b_ct, b_cond, b_pcol = Buf(), Buf(), Buf()
            b_brow, b_mrow = [Buf(), Buf()], [Buf(), Buf()]
            b_wb = [Buf(), Buf(), Buf()]
            b_pm = [Buf(), Buf()]
            S_.dma("sp", ct[:], c_d[:, :], writes=[b_ct])
            S_.op("act", lambda h: h.activation(out=condT[:], in_=ct[:], func=AF.Silu), reads=[b_ct], writes=[b_cond])
            wv = w_ada.rearrange("(k p) c -> p k c", p=128)
            for cb in range(48):
                i = cb % 2
                iw = cb % 3
                S_.dma("pool", wb[iw][:], wv[:, :, cb * 512:(cb + 1) * 512], writes=[b_wb[iw]])
                S_.dma("sp", brow[i][:], b_ada[:, cb * 512:(cb + 1) * 512], writes=[b_brow[i]])
                for kt in range(KT):
                    S_.op("pe", lambda h, kt=kt, i=i, iw=iw: h.matmul(pm[i][:], lhsT=condT[:, kt:kt + 1], rhs=wb[iw][:, kt, :],
                                                                 start=(kt == 0), stop=(kt == KT - 1)),
                          reads=[b_cond, b_wb[iw]], writes=[b_pm[i]], sig=(kt == KT - 1))
                S_.op("dve", lambda h, i=i: h.tensor_tensor(out=mrow[i][:], in0=pm[i][:], in1=brow[i][:], op=ALU.add),
                      reads=[b_pm[i], b_brow[i]], writes=[b_mrow[i]])
                S_.dma("sp", modD[:, cb * 512:(cb + 1) * 512], mrow[i][:], reads=[b_mrow[i]])
                for q in range(4):
                    j = cb * 4 + q
                    S_.op("pe", lambda h, j=j, q=q, i=i: h.matmul(pcol[:, j:j + 1], lhsT=mrow[i][0:1, q * 128:(q + 1) * 128],
                                                                    rhs=ONEf[0:1, 0:1], start=True, stop=True),
                          reads=[b_mrow[i], bC32], writes=[b_pcol], sig=(q == 3))
            S_.op("dve", lambda h: h.tensor_copy(out=modP[:], in_=pcol[:]), reads=[b_pcol], writes=[bmodP])
            S_.op("dve", lambda h: h.tensor_scalar(out=modP[:, 32:64], in0=modP[:, 32:64], scalar1=1.0, scalar2=None, op0=ALU.add),
                  reads=[bmodP], writes=[bmodP])
            S_.op("dve", lambda h: h.tensor_scalar(out=modP[:, 128:160], in0=modP[:, 128:160], scalar1=1.0, scalar2=None, op0=ALU.add),
                  reads=[bmodP], writes=[bmodP])
            S_.barrier()
        if stop_after <= 0:
            return finish(nc, S_, out_d)

        def ln_stats(es_tiles, xt, b_xt, st, mv, rstd, b_st):
            for j in range(8):
                S_.op("dve", lambda h, j=j: h.bn_stats(out=st[:, j, :], in_=xt[:, j * 512:(j + 1) * 512]),
                      reads=[b_xt], writes=[b_st])
            S_.op("dve", lambda h: h.bn_aggr(out=mv[:], in_=st[:].rearrange("p a b -> p (a b)")), reads=[b_st], writes=[b_st])
            S_.op("dve", lambda h: h.tensor_scalar(out=rstd[:], in0=mv[:, 1:2], scalar1=LN_EPS, scalar2=None, op0=ALU.add),
                  reads=[b_st], writes=[b_st])
            S_.op("act", lambda h: h.activation(out=rstd[:], in_=rstd[:], func=AF.Sqrt), reads=[b_st], writes=[b_st])
            S_.op("dve", lambda h: h.reciprocal(out=rstd[:], in_=rstd[:]), reads=[b_st], writes=[b_st])

        def modT(xn, b_xn, hblk, b_hblk, sub, pT, b_pT, sc0, sh0):
            for g in range(8):
                i = g % 2
                for q in range(4):
                    kt = g * 4 + q
                    S_.op("pe", lambda h, kt=kt, q=q, i=i: h.transpose(out=pT[i][:, q * 128:(q + 1) * 128],
                                                                       in_=xn[:, kt * 128:(kt + 1) * 128], identity=IDf),
                          reads=[b_xn, bC32], writes=[b_pT[i]], sig=(q == 3))
                for q in range(4):
                    kt = g * 4 + q
                    S_.op("act", lambda h, kt=kt, q=q, i=i: h.activation(
                        out=hblk[:, kt, sub * 128:(sub + 1) * 128], in_=pT[i][:, q * 128:(q + 1) * 128], func=AF.Identity,
                        scale=modP[:, sc0 + kt:sc0 + kt + 1], bias=modP[:, sh0 + kt:sh0 + kt + 1]),
                        reads=[b_pT[i], bmodP], writes=[b_hblk])

        with ExitStack() as es:
            xt = [sb(es, "xt%d" % i, [128, D], F32) for i in range(2)]
            xn = [sb(es, "xn%d" % i, [128, D], F32) for i in range(2)]
            st = sb(es, "st", [128, 8, 6], F32)
            mv = sb(es, "mv", [128, 2], F32)
            rstd = sb(es, "rstd", [128, 1], F32)
            hblk = [sb(es, "hblk%d" % i, [128, KT, 512], BF16) for i in range(2)]
            pT = [ps(es, "pT%d" % i, [128, 512], F32) for i in range(2)]
            b_xt, b_xn, b_hblk, b_pT = [Buf(), Buf()], [Buf(), Buf()], [Buf(), Buf()], [Buf(), Buf()]
            b_st = Buf()
            hTv = hT.rearrange("(k p) t -> p k t", p=128)
            for blk in range(8):
                hb = blk % 2
                for sub in range(4):
                    tt = blk * 4 + sub
                    i = tt % 2
                    S_.dma("sp", xt[i][:], x_d[tt * 128:(tt + 1) * 128, :], writes=[b_xt[i]])
                    ln_stats(es, xt[i], b_xt[i], st, mv, rstd, b_st)
                    S_.op("dve", lambda h, i=i: h.tensor_scalar(out=xn[i][:], in0=xt[i][:], scalar1=mv[:, 0:1], scalar2=rstd[:, 0:1],
                                                                 op0=ALU.subtract, op1=ALU.mult),
                          reads=[b_xt[i], b_st], writes=[b_xn[i]])
                    modT(xn[i], b_xn[i], hblk[hb], b_hblk[hb], sub, pT, b_pT, 32, 0)
                S_.dma("sp", hTv[:, :, blk * 512:(blk + 1) * 512], hblk[hb][:], reads=[b_hblk[hb]])
            S_.barrier()
        if stop_after <= 1:
            return finish(nc, S_, out_d)

        def mm_A(name, act_d, kt_n, tok0, ntok, W_d, col0, ncols, mk_evac, cw=256):
            mm_A_multi(name, act_d, kt_n, tok0, ntok, [(W_d, col0, ncols, mk_evac)], cw=cw)

        def mm_A_multi(name, act_d, kt_n, tok0, ntok, segs, cw=256):
            with ExitStack() as es:
                S_.store_q = "sp"
                evacs = {}
                for (_, _, _, mk_) in segs:
                    if mk_ not in evacs:
                        evacs[mk_] = mk_(es)
                A = sb(es, name + "_A", [128, kt_n, ntok], BF16)
                Wt = [sb(es, name + "_W%d" % i, [128, kt_n, cw], BF16) for i in range(2)]
                pp = [ps(es, name + "_p%d" % i, [128, 512], F32) for i in range(6)]
                b_A, b_W, b_pp = Buf(), [Buf(), Buf()], [Buf() for _ in range(6)]
                av = act_d.rearrange("(k p) t -> p k t", p=128)
                for k0 in range(0, kt_n, 8):
                    S_.dma("sp", A[:, k0:k0 + 8, :], av[:, k0:k0 + 8, tok0:tok0 + ntok], writes=[b_A])
                tbs = tok_blocks(0, ntok)
                cnt = 0
                wlist = []
                for (W_d, col0, ncols, mk_) in segs:
                    wv_ = W_d.rearrange("(k p) c -> p k c", p=128)
                    for c0_ in range(col0, col0 + ncols, cw):
                        wlist.append((wv_, c0_, evacs[mk_]))
                for wi, (wv, c0, evac) in enumerate(wlist):
                    i = wi % 2
                    S_.dma("pool", Wt[i][:], wv[:, :, c0:c0 + cw], writes=[b_W[i]])
                    for cc in range(cw // 128):
                        for (t0, n) in tbs:
                            pi = cnt % 6
                            cnt += 1
                            for kt in range(kt_n):
                                S_.op("pe", lambda h, kt=kt, i=i, cc=cc, t0=t0, n=n, pi=pi: h.matmul(
                                    pp[pi][:, 0:n], lhsT=Wt[i][:, kt, cc * 128:(cc + 1) * 128], rhs=A[:, kt, t0:t0 + n],
                                    start=(kt == 0), stop=(kt == kt_n - 1)),
                                    reads=[b_A, b_W[i]], writes=[b_pp[pi]], sig=(kt == kt_n - 1))
                            evac(es, pp[pi][:, 0:n], c0 + cc * 128, tok0 + t0, n, b_pp[pi], cnt)
                S_.barrier()
                S_.store_q = "pool"

        class Stage:
            def __init__(self, es, name, dt, n=4, w=512):
                self.t = [sb(es, "%s_st%d" % (name, i), [128, w], dt) for i in range(n)]
                self.b = [Buf() for _ in range(n)]
                self.i = 0

            def nxt(self):
                j = self.i % len(self.t)
                self.i += 1
                return self.t[j], self.b[j]

        def p2(tok0, ntok, full):
            def mk(es_):
                stg = {"f": Stage(es_, "p2f", F32), "h": Stage(es_, "p2h", BF16)}

                def evac(es, p, c, t, n, b_p, idx):
                    return evac_(stg, es, p, c, t, n, b_p, idx)
                return evac

            def evac_(stg, es, p, c, t, n, b_p, idx):
                eng = "act" if idx % 2 == 0 else "dve"
                if c < XBC0:
                    tl, bt = stg["f"].nxt()
                    S_.op("act", lambda h: h.activation(out=tl[:, 0:n], in_=p, func=AF.Silu), reads=[b_p], writes=[bt])
                    S_.dma("sp", zsT[c:c + 128, t:t + n], tl[:, 0:n], reads=[bt])
                    return
                if c >= V0:
                    tl, bt = stg["h"].nxt()
                    dst = vT[c - V0:c - V0 + 128, t:t + n]
                else:
                    tl, bt = stg["f"].nxt()
                    if c < DT0:
                        dst = xbcpre[c - XBC0:c - XBC0 + 128, t:t + n]
                    elif c < Q0:
                        dst = dtraw[c - DT0:c - DT0 + 128, t:t + n]
                    elif c < K0:
                        dst = qT[c - Q0:c - Q0 + 128, t:t + n]
                    else:
                        dst = kT[c - K0:c - K0 + 128, t:t + n]
                if eng == "act":
                    S_.op("act", lambda h: h.activation(out=tl[:, 0:n], in_=p, func=AF.Copy), reads=[b_p], writes=[bt])
                else:
                    S_.op("dve", lambda h: h.tensor_copy(out=tl[:, 0:n], in_=p), reads=[b_p], writes=[bt])
                S_.dma("sp", dst, tl[:, 0:n], reads=[bt])
            if full:
                mm_A_multi("p2a", hT, KT, tok0, ntok, [(w_in, 0, IN_COLS, mk), (w_gate, 0, 2 * D, mk_gate)])
            else:
                mm_A_multi("p2b", hT, KT, tok0, ntok, [(w_in, XBC0, 8192 + 1024, mk),
                                                       (w_in, DT0, 256, mk),
                                                       (w_in, K0, 2048, mk)])

        def mk_gate(es_):
            st_ = Stage(es_, "pgf", F32)
            bg = sb(es_, "bgate", [128, 64], F32)
            b_bg = Buf()
            S_.dma("sp", bg[:], b_gate[:, :], writes=[b_bg])

            def evac(es, p, c, t, n, b_p, idx):
                tl, bt = st_.nxt()
                S_.op("act", lambda h: h.activation(out=tl[:, 0:n], in_=p, func=AF.Sigmoid, bias=bg[:, c // 128:c // 128 + 1]),
                      reads=[b_p, b_bg], writes=[bt])
                S_.dma("sp", gatesT[c:c + 128, t:t + n], tl[:, 0:n], reads=[bt])
            return evac

        p2(0, NOWN, True)
        p2(NOWN, S - NOWN, False)
        if stop_after <= 2:
            return finish(nc, S_, out_d)

        def p3():
            with ExitStack() as es:
                xin = [sb(es, "cxin%d" % i, [128, S + 2], F32) for i in range(2)]
                acc = [sb(es, "cacc%d" % i, [128, S], F32) for i in range(2)]
                ob = [sb(es, "cob%d" % i, [128, S], BF16) for i in range(2)]
                cw = sb(es, "ccw", [128, 80, 3], F32)
                cbs = sb(es, "ccb", [128, 80], F32)
                stT = [sb(es, "cstT%d" % i, [128, 32, 128], BF16) for i in range(2)]
                pT = [ps(es, "cpT%d" % i, [128, 1024], BF16) for i in range(2)]
                pF = [ps(es, "cpF%d" % i, [128, 512], F32) for i in range(2)]
                tt = sb(es, "ctt", [128, S], F32)
                dtv = sb(es, "cdtv", [128, S], F32)
                stF = sb(es, "cstF", [128, 32, 128], F32)
                al = sb(es, "cal", [128, 2], F32)
                db = sb(es, "cdb", [128, 2], F32)
                b_xin, b_ob, b_stT, b_pT, b_pF = [Buf(), Buf()], [Buf(), Buf()], [Buf(), Buf()], [Buf(), Buf()], [Buf(), Buf()]
                b_cw, b_tt, b_dtv, b_stF, b_al = Buf(), Buf(), Buf(), Buf(), Buf()
                b_acc = [Buf(), Buf()]
                S_.dma("sp", cw[:], scw[:, :, :], writes=[b_cw])
                S_.dma("sp", cbs[:], scb[:, :], writes=[b_cw])
                S_.dma("sp", al[:], alog[:, :], writes=[b_al])
                S_.dma("sp", db[:], dtb[:, :], writes=[b_al])
                for i in range(2):
                    S_.op("dve", lambda h, i=i: h.memset(xin[i][:, 0:1], 0.0), writes=[b_xin[i]])
                    S_.op("dve", lambda h, i=i: h.memset(xin[i][:, S + 1:S + 2], 0.0), writes=[b_xin[i]])
                xsv = xs_tok.rearrange("(t p) c -> p t c", p=128)
                btv = B_tok.rearrange("(t p) c -> p t c", p=128)
                def c_front(blk):
                        i = blk % 2
                        nld = S if blk < 72 else NOWN
                        S_.dma("sp", xin[i][:, 1:nld + 1], xbcpre[blk * 128:(blk + 1) * 128, 0:nld], writes=[b_xin[i]])
                        S_.op("act", lambda h, i=i, blk=blk: h.activation(out=acc[i][:], in_=xin[i][:, 1:S + 1], func=AF.Identity,
                                                                            scale=cw[:, blk, 1:2], bias=cbs[:, blk:blk + 1]),
                              reads=[b_xin[i], b_cw], writes=[b_acc[i]])

                def c_rest(blk):
                        i = blk % 2
                        S_.op("dve", lambda h, i=i, blk=blk: h.scalar_tensor_tensor(out=acc[i][:], in0=xin[i][:, 0:S], scalar=cw[:, blk, 0:1],
                                                                                      in1=acc[i][:], op0=ALU.mult, op1=ALU.add),
                              reads=[b_xin[i], b_cw], writes=[b_acc[i]])
                        S_.op("dve", lambda h, i=i, blk=blk: h.scalar_tensor_tensor(out=acc[i][:], in0=xin[i][:, 2:S + 2], scalar=cw[:, blk, 2:3],
                                                                                      in1=acc[i][:], op0=ALU.mult, op1=ALU.add),
                              reads=[b_xin[i], b_cw], writes=[b_acc[i]])
                        S_.op("act", lambda h, i=i: h.activation(out=ob[i][:], in_=acc[i][:], func=AF.Silu), reads=[b_acc[i]], writes=[b_ob[i]])
                        if blk < 72:
                            for g in range(4):
                                j = g % 2
                                for q in range(8):
                                    tl = g * 8 + q
                                    S_.op("pe", lambda h, tl=tl, q=q, j=j, i=i: h.transpose(out=pT[j][:, q * 128:(q + 1) * 128],
                                                                                             in_=ob[i][:, tl * 128:(tl + 1) * 128], identity=IDb),
                                          reads=[b_ob[i], bC16], writes=[b_pT[j]], sig=(q == 7))
                                if g % 2 == 0:
                                    S_.op("act", lambda h, g=g, j=j, i=i: h.activation(out=stT[i][:, g * 8:(g + 1) * 8, :].rearrange("p a b -> p (a b)"),
                                                                                         in_=pT[j][:], func=AF.Copy),
                                          reads=[b_pT[j]], writes=[b_stT[i]])
                                else:
                                    S_.op("dve", lambda h, g=g, j=j, i=i: h.tensor_copy(out=stT[i][:, g * 8:(g + 1) * 8, :].rearrange("p a b -> p (a b)"),
                                                                                          in_=pT[j][:]),
                                          reads=[b_pT[j]], writes=[b_stT[i]])
                            if blk < 64:
                                S_.dma("sp", xsv[:, :, blk * 128:(blk + 1) * 128], stT[i][:], reads=[b_stT[i]])
                            else:
                                S_.dma("sp", btv[:, :, (blk - 64) * 128:(blk - 63) * 128], stT[i][:], reads=[b_stT[i]])
                                S_.dma("sp", BTd[(blk - 64) * 128:(blk - 63) * 128, :], ob[i][:], reads=[b_ob[i]])
                        else:
                            S_.dma("sp", CTd[(blk - 72) * 128:(blk - 71) * 128, 0:NOWN], ob[i][:, 0:NOWN], reads=[b_ob[i]])
                c_front(0)
                for blk in range(80):
                    if blk + 1 < 80:
                        c_front(blk + 1)
                    c_rest(blk)
                S_.op("act", lambda h: h.activation(out=al[:], in_=al[:], func=AF.Exp), reads=[b_al], writes=[b_al])
                S_.op("dve", lambda h: h.tensor_scalar(out=al[:], in0=al[:], scalar1=-1.0, scalar2=None, op0=ALU.mult), reads=[b_al], writes=[b_al])
                dtv_ = dt_tok.rearrange("(t p) c -> p t c", p=128)
                atv_ = a_tok.rearrange("(t p) c -> p t c", p=128)
                for d in range(2):
                    i = d
                    S_.dma("sp", xin[i][:, 1:S + 1], dtraw[d * 128:(d + 1) * 128, :], writes=[b_xin[i]])
                    S_.op("act", lambda h, i=i, d=d: h.activation(out=acc[i][:], in_=xin[i][:, 1:S + 1], func=AF.Identity, bias=db[:, d:d + 1]),
                          reads=[b_xin[i], b_al], writes=[b_acc[i]])
                    S_.op("act", lambda h, i=i: h.activation(out=tt[:], in_=acc[i][:], func=AF.Abs), reads=[b_acc[i]], writes=[b_tt])
                    S_.op("act", lambda h: h.activation(out=tt[:], in_=tt[:], func=AF.Exp, scale=-1.0), reads=[b_tt], writes=[b_tt])
                    S_.op("act", lambda h: h.activation(out=tt[:], in_=tt[:], func=AF.Ln, bias=1.0), reads=[b_tt], writes=[b_tt])
                    S_.op("dve", lambda h, i=i: h.scalar_tensor_tensor(out=dtv[:], in0=acc[i][:], scalar=0.0, in1=tt[:], op0=ALU.max, op1=ALU.add),
                          reads=[b_acc[i], b_tt], writes=[b_dtv])
                    S_.op("dve", lambda h, d=d: h.tensor_scalar(out=tt[:], in0=dtv[:], scalar1=al[:, d:d + 1], scalar2=None, op0=ALU.mult),
                          reads=[b_dtv, b_al], writes=[b_tt])
                    for (src, b_src, dstv) in ((dtv, b_dtv, dtv_), (tt, b_tt, atv_)):
                        for g in range(8):
                            j = g % 2
                            for q in range(4):
                                tl = g * 4 + q
                                S_.op("pe", lambda h, tl=tl, q=q, j=j, src=src: h.transpose(out=pF[j][:, q * 128:(q + 1) * 128],
                                                                                             in_=src[:, tl * 128:(tl + 1) * 128], identity=IDf),
                                      reads=[b_src, bC32], writes=[b_pF[j]], sig=(q == 3))
                            S_.op("dve", lambda h, g=g, j=j: h.tensor_copy(out=stF[:, g * 4:(g + 1) * 4, :].rearrange("p a b -> p (a b)"), in_=pF[j][:]),
                                  reads=[b_pF[j]], writes=[b_stF])
                        S_.dma("sp", dstv[:, :, d * 128:(d + 1) * 128], stF[:], reads=[b_stF])
                S_.barrier()
        p3()
        if stop_after <= 3:
            return finish(nc, S_, out_d)

        def p4():
            with ExitStack() as es:
                H = sb(es, "sH", [128, 8, 1024], F32)
                Hb = sb(es, "sHb", [128, 8, 1024], BF16)
                xs = [sb(es, "sxs%d" % i, [128, DI], BF16) for i in range(2)]
                xdt = [sb(es, "sxdt%d" % i, [128, DI], BF16) for i in range(2)]
                xw = [sb(es, "sxw%d" % i, [128, DI], BF16) for i in range(2)]
                bt = [sb(es, "sbt%d" % i, [128, 1024], BF16) for i in range(2)]
                BTc = [sb(es, "sBT%d" % i, [128, 8, 128], BF16) for i in range(2)]
                CTc = [sb(es, "sCT%d" % i, [128, 8, 128], BF16) for i in range(2)]
                dts = [sb(es, "sdt%d" % i, [128, 128], F32) for i in range(2)]
                as_ = [sb(es, "sa%d" % i, [128, 128], F32) for i in range(2)]
                toend = [sb(es, "stoend%d" % i, [128, 128], F32) for i in range(2)]
                eL = [sb(es, "seL%d" % i, [128, 128], F32) for i in range(2)]
                ecum = [sb(es, "secum%d" % i, [128, 128], F32) for i in range(2)]
                wts = [sb(es, "swts%d" % i, [128, 128], F32) for i in range(2)]
                Dbc = sb(es, "sDbc", [128, 128], F32)
                segr = [sb(es, "ssegr%d" % i, [128, 8, 128], F32) for i in range(3)]
                E = [sb(es, "sE%d" % i, [128, 8, 128], BF16) for i in range(2)]
                MT = [sb(es, "sMT%d" % i, [128, 8, 128], BF16) for i in range(2)]
                CBm = [sb(es, "sCBm%d" % i, [128, 128], F32) for i in range(2)]
                tmpy = [sb(es, "stmpy%d" % i, [128, 512], F32) for i in range(1)] * 2
                tmph = [sb(es, "stmph%d" % i, [128, 512], F32) for i in range(1)] * 2
                xD = [sb(es, "sxD%d" % i, [128, 512], BF16) for i in range(3)]
                yst = [sb(es, "syst%d" % i, [128, 512], F32) for i in range(2)]
                yld = [sb(es, "syld%d" % i, [128, 512], F32) for i in range(3)]
                pseg = [ps(es, "spseg%d" % i, [128, 1024], F32) for i in range(2)]
                pyi = [ps(es, "spyi%d" % i, [128, 512], F32) for i in range(2)]
                pyo = ps(es, "spyo", [128, 512], F32)
                psu = ps(es, "spsu", [128, 512], F32)
                (b_xs, b_xdt, b_xw, b_bt, b_BC, b_da, b_sm, b_segr, b_E, b_MT, b_CBm, b_tmpy, b_tmph, b_xD,
                 b_yst, b_yld, b_pseg, b_pyi) = ([Buf(), Buf(), Buf()] for _ in range(18))
                b_D, b_pyo, b_psu = Buf(), Buf(), Buf()
                b_tmpy = [b_tmpy[0]] * 2
                b_tmph = [b_tmph[0]] * 2
                b_H = [[Buf(), Buf()] for _ in range(8)]
                b_Hb = [[Buf(), Buf()] for _ in range(8)]
                S_.store_q = "pool"
                S_.dma("sp", Dbc[:], ssmd[0:1, :].partition_broadcast(128).rearrange("p a b -> p (a b)"), writes=[b_D])
                btv = BTd.rearrange("(g n) t -> n g t", n=128)
                ctv = CTd.rearrange("(g n) t -> n g t", n=128)
                v3 = lambda ap: ap.rearrange("p (h q) -> p h q", q=64)
                citer = [0]

                def prologue(d, c, i, full, last):
                    r0 = c * 128
                    S_.dma("sp", xs[i][:], xs_tok[r0:r0 + 128, :], writes=[b_xs[i]])
                    S_.dma("sp", bt[i][:], B_tok[r0:r0 + 128, :], writes=[b_bt[i]])
                    S_.dma("sp", dts[i][:], dt_tok[r0:r0 + 128, d * 128:(d + 1) * 128], writes=[b_da[i]])
                    S_.dma("sp", as_[i][:], a_tok[r0:r0 + 128, d * 128:(d + 1) * 128], writes=[b_da[i]])
                    if full:
                        S_.dma("sp", BTc[i][:], btv[:, :, r0:r0 + 128], writes=[b_BC[i]])
                        S_.dma("sp", CTc[i][:], ctv[:, :, r0:r0 + 128], writes=[b_BC[i]])
                    for q, cm in enumerate((SU[d], ONEf, TRI[d])):
                        S_.op("pe", lambda h, q=q, cm=cm: h.matmul(psu[:, 128 + q * 128:256 + q * 128], lhsT=cm, rhs=as_[i][:], start=True, stop=True),
                              reads=[b_da[i], bC32], writes=[b_psu], sig=(q == 2))
                    for q, dst in enumerate((toend[i], eL[i], ecum[i])):
                        S_.op("act", lambda h, q=q, dst=dst: h.activation(out=dst[:], in_=psu[:, 128 + q * 128:256 + q * 128], func=AF.Exp),
                              reads=[b_psu], writes=[b_sm[i]])
                    S_.op("dve", lambda h: h.tensor_tensor(out=wts[i][:], in0=toend[i][:], in1=dts[i][:], op=ALU.mult),
                          reads=[b_sm[i], b_da[i]], writes=[b_sm[i]])
                    if full:
                        S_.op("dve", lambda h: h.tensor_tensor(out=v3(xdt[i][:]), in0=v3(xs[i][:]),
                                                                in1=dts[i][:].unsqueeze(2).broadcast_to([128, 128, 64]), op=ALU.mult),
                              reads=[b_xs[i], b_da[i]], writes=[b_xdt[i]])
                    if not last:
                        S_.op("dve", lambda h: h.tensor_tensor(out=v3(xw[i][:]), in0=v3(xs[i][:]),
                                                                in1=wts[i][:].unsqueeze(2).broadcast_to([128, 128, 64]), op=ALU.mult),
                              reads=[b_xs[i], b_sm[i]], writes=[b_xw[i]])

                def stage_a0(u, un):
                    d, c, i, g, hg, k, full, last, first = u
                    if not full:
                        return
                    k3 = un % 3
                    if hg == 0:
                        S_.op("pe", lambda h: h.matmul(psu[:, 0:128], lhsT=BTc[i][:, g, :], rhs=CTc[i][:, g, :], start=True, stop=True),
                              reads=[b_BC[i]], writes=[b_psu])
                        S_.op("dve", lambda h: h.tensor_tensor(out=CBm[g % 2][:], in0=psu[:, 0:128], in1=TRI[d], op=ALU.mult),
                              reads=[b_psu, bC32], writes=[b_CBm[g % 2]])
                    h0 = g * 16 + hg * 8
                    if d == 1:
                        S_.dma("sp", yld[k3][:], ysum[c * 128:(c + 1) * 128, g * 1024 + hg * 512:g * 1024 + (hg + 1) * 512], writes=[b_yld[k3]])
                    else:
                        S_.op("dve", lambda h: h.tensor_tensor(out=v3(xD[k3][:]), in0=v3(xs[i][:, g * 1024 + hg * 512:g * 1024 + (hg + 1) * 512]),
                                                               in1=Dbc[:, h0:h0 + 8].unsqueeze(2).broadcast_to([128, 8, 64]), op=ALU.mult),
                              reads=[b_xs[i], b_D], writes=[b_xD[k3]])
                    for j in range(8):
                        S_.op("act", lambda h, j=j: h.activation(out=segr[k3][:, j, :], in_=TRI[d], func=AF.Copy, scale=as_[i][:, h0 + j:h0 + j + 1]),
                              reads=[b_da[i], bC32], writes=[b_segr[k3]])

                def stage_a1(u, un):
                    d, c, i, g, hg, k, full, last, first = u
                    if not full:
                        return
                    k3 = un % 3
                    sflat = segr[k3][:].rearrange("p a b -> p (a b)")
                    eflat = E[k][:].rearrange("p a b -> p (a b)")
                    for q in range(2):
                        S_.op("pe", lambda h, q=q: h.matmul(pseg[k][:, q * 512:(q + 1) * 512], lhsT=SU[d], rhs=sflat[:, q * 512:(q + 1) * 512],
                                                            start=True, stop=True),
                              reads=[b_segr[k3], bC32], writes=[b_pseg[k]], sig=(q == 1))
                    for q in range(2):
                        S_.op("act", lambda h, q=q: h.activation(out=eflat[:, q * 512:(q + 1) * 512], in_=pseg[k][:, q * 512:(q + 1) * 512], func=AF.Exp),
                              reads=[b_pseg[k]], writes=[b_E[k]])

                def stage_b(u, un):
                    d, c, i, g, hg, k, full, last, first = u
                    k3 = un % 3
                    r0 = c * 128
                    h0 = g * 16 + hg * 8
                    c0 = g * 1024 + hg * 512
                    if full:
                        S_.op("dve", lambda h: h.tensor_tensor(out=MT[k][:], in0=E[k][:], in1=CBm[g % 2][:].unsqueeze(1).broadcast_to([128, 8, 128]), op=ALU.mult),
                              reads=[b_E[k], b_CBm[g % 2]], writes=[b_MT[k]])
                        if d == 0:
                            S_.op("pe", lambda h: h.matmul(pyi[k][:, :], lhsT=IDb, rhs=xD[k3][:], start=True, stop=False),
                                  reads=[b_xD[k3], bC16], writes=[b_pyi[k]], sig=False)
                        else:
                            S_.op("pe", lambda h: h.matmul(pyi[k][:, :], lhsT=IDf, rhs=yld[k3][:], start=True, stop=False),
                                  reads=[b_yld[k3], bC32], writes=[b_pyi[k]], sig=False)
                        for j in range(8):
                            S_.op("pe", lambda h, j=j: h.matmul(pyi[k][:, j * 64:(j + 1) * 64], lhsT=MT[k][:, j, :], rhs=xdt[i][:, (h0 + j) * 64:(h0 + j + 1) * 64],
                                                                start=False, stop=(j == 7)),
                                  reads=[b_MT[k], b_xdt[i]], writes=[b_pyi[k]], sig=(j == 7))
                        S_.op("pe", lambda h: h.matmul(pyo[:, :], lhsT=CTc[i][:, g, :], rhs=Hb[:, g, hg * 512:(hg + 1) * 512], start=True, stop=True),
                              reads=[b_BC[i], b_Hb[g][hg]], writes=[b_pyo])
                        S_.op("dve", lambda h: h.tensor_tensor(out=v3(tmpy[k][:]), in0=v3(pyo[:, :]),
                                                               in1=ecum[i][:, h0:h0 + 8].unsqueeze(2).broadcast_to([128, 8, 64]), op=ALU.mult),
                              reads=[b_pyo, b_sm[i]], writes=[b_tmpy[k]])
                        S_.op("dve", lambda h: h.tensor_tensor(out=yst[k][:], in0=tmpy[k][:], in1=pyi[k][:], op=ALU.add),
                              reads=[b_tmpy[k], b_pyi[k]], writes=[b_yst[k]])
                        S_.dma("sp", ysum[r0:r0 + 128, c0:c0 + 512], yst[k][:], reads=[b_yst[k]])
                    if not last:
                        S_.op("pe", lambda h: h.matmul(psu[:, :], lhsT=bt[i][:, g * 128:(g + 1) * 128], rhs=xw[i][:, c0:c0 + 512], start=True, stop=True),
                              reads=[b_bt[i], b_xw[i]], writes=[b_psu])
                        S_.op("dve", lambda h: h.tensor_tensor(out=v3(tmph[k][:]), in0=v3(H[:, g, hg * 512:(hg + 1) * 512]),
                                                               in1=eL[i][:, h0:h0 + 8].unsqueeze(2).broadcast_to([128, 8, 64]), op=ALU.mult),
                              reads=[b_H[g][hg], b_sm[i]], writes=[b_tmph[k]])
                        S_.op("dve", lambda h: h.tensor_tensor(out=H[:, g, hg * 512:(hg + 1) * 512], in0=tmph[k][:], in1=psu[:, :], op=ALU.add),
                              reads=[b_tmph[k], b_psu], writes=[b_H[g][hg]])
                        S_.op("act", lambda h: h.activation(out=Hb[:, g, hg * 512:(hg + 1) * 512], in_=H[:, g, hg * 512:(hg + 1) * 512], func=AF.Copy),
                              reads=[b_H[g][hg]], writes=[b_Hb[g][hg]])

                for d in range(2):
                    for g in range(8):
                        for hg in range(2):
                            S_.op("dve", lambda h, g=g, hg=hg: h.memset(H[:, g, hg * 512:(hg + 1) * 512], 0.0), writes=[b_H[g][hg]])
                            S_.op("dve", lambda h, g=g, hg=hg: h.memset(Hb[:, g, hg * 512:(hg + 1) * 512], 0.0), writes=[b_Hb[g][hg]])
                    chunks = list(range(17)) if d == 0 else list(range(31, -1, -1))
                    units = []
                    cinfo = []
                    for c in chunks:
                        i = citer[0] % 2
                        citer[0] += 1
                        cinfo.append((d, c, i, c <= 16, c == chunks[-1]))
                        for g in range(8):
                            for hg in range(2):
                                units.append((d, c, i, g, hg, len(units) % 2, c <= 16, c == chunks[-1], g == 0 and hg == 0))
                    prologue(*cinfo[0])
                    prologue(*cinfo[1])
                    stage_a0(units[0], 0)
                    stage_a0(units[1], 1)
                    stage_a1(units[0], 0)
                    for ui, u in enumerate(units):
                        if ui + 2 < len(units):
                            stage_a0(units[ui + 2], ui + 2)
                        if ui + 1 < len(units):
                            stage_a1(units[ui + 1], ui + 1)
                        stage_b(u, ui)
                        if ui % 16 == 15:
                            ci = ui // 16
                            if ci + 2 < len(cinfo):
                                prologue(*cinfo[ci + 2])
                    S_.barrier()
                S_.store_q = "pool"
        p4()
        if stop_after <= 4:
            return finish(nc, S_, out_d)

        def p5():
            with ExitStack() as es:
                nw = sb(es, "g5nw", [128, 64], F32)
                ys = [sb(es, "g5ys%d" % i, [128, 4, 1024], F32) for i in range(2)]
                zs = [sb(es, "g5zs%d" % i, [128, 8, 512], F32) for i in range(2)]
                G = [sb(es, "g5G%d" % i, [128, 8, 512], F32) for i in range(2)]
                sq = [sb(es, "g5sq%d" % i, [128, 512], F32) for i in range(2)]
                rr = [sb(es, "g5rr%d" % i, [128, 512], F32) for i in range(2)]
                ob = [sb(es, "g5o%d" % i, [128, 512], BF16) for i in range(2)]
                pY = [ps(es, "g5pY%d" % i, [128, 512], F32) for i in range(2)]
                pSS = [ps(es, "g5pSS%d" % i, [128, 512], F32) for i in range(2)]
                b_nw = Buf()
                b_G, b_rr, b_pSS = ([Buf(), Buf()] for _ in range(3))
                b_ys, b_zs, b_sq, b_ob, b_pY = ([Buf(), Buf()] for _ in range(5))
                S_.dma("sp", nw[:], snw[:, :], writes=[b_nw])
                ysv = ysum.rearrange("(t p) c -> p t c", p=128)
                zsv = zsT.rearrange("(c p) t -> p c t", p=128)
                it = 0
                for (t0, n) in tok_blocks(0, NOWN):
                    nt = n // 128
                    for g in range(8):
                        i = it % 2
                        it += 1
                        S_.dma("sp", ys[i][:, 0:nt, :], ysv[:, t0 // 128:t0 // 128 + nt, g * 1024:(g + 1) * 1024], writes=[b_ys[i]])
                        S_.dma("sp", zs[i][:, :, 0:n], zsv[:, g * 8:(g + 1) * 8, t0:t0 + n], writes=[b_zs[i]])
                        def ss_mm(ct, i=i, n=n):
                            j = ct % 2
                            S_.op("pe", lambda h: h.matmul(pSS[i][:, 0:n], lhsT=ONEf, rhs=sq[j][:, 0:n], start=(ct == 0), stop=(ct == 7)),
                                  reads=[b_sq[j], bC32], writes=[b_pSS[i]], sig=True)
                        for ct in range(8):
                            j = ct % 2
                            for q in range(nt):
                                S_.op("pe", lambda h, q=q, ct=ct, j=j, i=i: h.transpose(out=pY[j][:, q * 128:(q + 1) * 128],
                                                                                         in_=ys[i][:, q, ct * 128:(ct + 1) * 128], identity=IDf),
                                      reads=[b_ys[i], bC32], writes=[b_pY[j]], sig=(q == nt - 1))
                            S_.op("dve", lambda h, ct=ct, j=j, i=i, n=n: h.tensor_tensor(out=G[i][:, ct, 0:n], in0=pY[j][:, 0:n], in1=zs[i][:, ct, 0:n], op=ALU.mult),
                                  reads=[b_pY[j], b_zs[i]], writes=[b_G[i]])
                            if ct >= 1:
                                ss_mm(ct - 1)
                            S_.op("act", lambda h, ct=ct, j=j, n=n, i=i: h.activation(out=sq[j][:, 0:n], in_=G[i][:, ct, 0:n], func=AF.Square),
                                  reads=[b_G[i]], writes=[b_sq[j]])
                        ss_mm(7)
                        S_.op("dve", lambda h, n=n, i=i: h.tensor_scalar(out=rr[i][:, 0:n], in0=pSS[i][:, 0:n], scalar1=1.0 / 1024.0, scalar2=RMS_EPS,
                                                                     op0=ALU.mult, op1=ALU.add), reads=[b_pSS[i]], writes=[b_rr[i]])
                        S_.op("act", lambda h, n=n, i=i: h.activation(out=rr[i][:, 0:n], in_=rr[i][:, 0:n], func=AF.Sqrt), reads=[b_rr[i]], writes=[b_rr[i]])
                        S_.op("dve", lambda h, n=n, i=i: h.reciprocal(out=rr[i][:, 0:n], in_=rr[i][:, 0:n]), reads=[b_rr[i]], writes=[b_rr[i]])
                        for ct in range(8):
                            j = ct % 2
                            cg = g * 8 + ct
                            S_.op("dve", lambda h, ct=ct, j=j, cg=cg, n=n, i=i: h.scalar_tensor_tensor(out=ob[j][:, 0:n], in0=G[i][:, ct, 0:n], scalar=nw[:, cg:cg + 1],
                                                                                                 in1=rr[i][:, 0:n], op0=ALU.mult, op1=ALU.mult),
                                  reads=[b_G[i], b_rr[i], b_nw], writes=[b_ob[j]])
                            S_.dma("sp", yssmT[cg * 128:(cg + 1) * 128, t0:t0 + n], ob[j][:, 0:n], reads=[b_ob[j]])
                S_.barrier()
        p5()
        if stop_after <= 5:
            return finish(nc, S_, out_d)

        def p6():
            with ExitStack() as es:
                gq = sb(es, "gq", [128, 1], F32)
                gk = sb(es, "gk", [128, 1], F32)
                xq = [sb(es, "a6x%d" % i, [128, 512], F32) for i in range(2)]
                cs = [sb(es, "a6c%d" % i, [128, 512], F32) for i in range(2)]
                sn = [sb(es, "a6s%d" % i, [128, 512], F32) for i in range(2)]
                sq = [sb(es, "a6sq%d" % i, [128, 512], F32) for i in range(2)]
                rr = [sb(es, "a6r%d" % i, [128, 512], F32) for i in range(2)]
                xn = [sb(es, "a6xn%d" % i, [128, 512], F32) for i in range(2)]
                t1 = [sb(es, "a6t1%d" % i, [128, 512], F32) for i in range(2)]
                t2 = [sb(es, "a6t2%d" % i, [128, 512], F32) for i in range(2)]
                ob = [sb(es, "a6o%d" % i, [128, 512], BF16) for i in range(2)]
                pS = [ps(es, "a6pS%d" % i, [128, 512], F32) for i in range(2)]
                pR = [ps(es, "a6pR%d" % i, [128, 512], F32) for i in range(2)]
                vrow = [sb(es, "a6v%d" % i, [128, S], BF16) for i in range(2)]
                vst = [sb(es, "a6vs%d" % i, [128, 32, 128], BF16) for i in range(2)]
                pT = [ps(es, "a6pT%d" % i, [128, 1024], BF16) for i in range(2)]
                b_g = Buf()
                (b_xq, b_cs, b_ob, b_vrow, b_vst, b_pT, b_sq, b_rr, b_xn, b_t1, b_t2, b_pS, b_pR) = ([Buf(), Buf()] for _ in range(13))
                S_.dma("sp", gq[:], qnw[:, :], writes=[b_g])
                S_.dma("sp", gk[:], knw[:, :], writes=[b_g])
                it = 0
                for (src, dst, nh, ntok, g) in ((kT, kTn, 8, S, gk), (qT, qTn, 32, NOWN, gq)):
                    for hd in range(nh):
                        for (t0, n) in tok_blocks(0, ntok):
                            i = it % 2
                            it += 1
                            S_.dma("sp", xq[i][:, 0:n], src[hd * 128:(hd + 1) * 128, t0:t0 + n], writes=[b_xq[i]])
                            S_.dma("sp", cs[i][:, 0:n], cos_d[:, t0:t0 + n], writes=[b_cs[i]])
                            S_.dma("sp", sn[i][:, 0:n], sin_d[:, t0:t0 + n], writes=[b_cs[i]])
                            S_.op("act", lambda h, i=i, n=n: h.activation(out=sq[i][:, 0:n], in_=xq[i][:, 0:n], func=AF.Square),
                                  reads=[b_xq[i]], writes=[b_sq[i]])
                            S_.op("pe", lambda h, i=i, n=n: h.matmul(pS[i][:, 0:n], lhsT=ONEf, rhs=sq[i][:, 0:n], start=True, stop=True),
                                  reads=[b_sq[i], bC32], writes=[b_pS[i]])
                            S_.op("dve", lambda h, i=i, n=n: h.tensor_scalar(out=rr[i][:, 0:n], in0=pS[i][:, 0:n], scalar1=1.0 / 128.0, scalar2=RMS_EPS,
                                                                              op0=ALU.mult, op1=ALU.add), reads=[b_pS[i]], writes=[b_rr[i]])
                            S_.op("act", lambda h, i=i, n=n: h.activation(out=rr[i][:, 0:n], in_=rr[i][:, 0:n], func=AF.Sqrt), reads=[b_rr[i]], writes=[b_rr[i]])
                            S_.op("dve", lambda h, i=i, n=n: h.reciprocal(out=rr[i][:, 0:n], in_=rr[i][:, 0:n]), reads=[b_rr[i]], writes=[b_rr[i]])
                            S_.op("dve", lambda h, i=i, n=n, g=g: h.scalar_tensor_tensor(out=xn[i][:, 0:n], in0=xq[i][:, 0:n], scalar=g[:, 0:1],
                                                                                         in1=rr[i][:, 0:n], op0=ALU.mult, op1=ALU.mult),
                                  reads=[b_xq[i], b_rr[i], b_g], writes=[b_xn[i]])
                            S_.op("pe", lambda h, i=i, n=n: h.matmul(pR[i][:, 0:n], lhsT=ROPf, rhs=xn[i][:, 0:n], start=True, stop=True),
                                  reads=[b_xn[i], bC32], writes=[b_pR[i]])
                            S_.op("dve", lambda h, i=i, n=n: h.tensor_tensor(out=t1[i][:, 0:n], in0=xn[i][:, 0:n], in1=cs[i][:, 0:n], op=ALU.mult),
                                  reads=[b_xn[i], b_cs[i]], writes=[b_t1[i]])
                            S_.op("dve", lambda h, i=i, n=n: h.tensor_tensor(out=t2[i][:, 0:n], in0=pR[i][:, 0:n], in1=sn[i][:, 0:n], op=ALU.mult),
                                  reads=[b_pR[i], b_cs[i]], writes=[b_t2[i]])
                            S_.op("dve", lambda h, i=i, n=n: h.tensor_tensor(out=ob[i][:, 0:n], in0=t1[i][:, 0:n], in1=t2[i][:, 0:n], op=ALU.add),
                                  reads=[b_t1[i], b_t2[i]], writes=[b_ob[i]])
                            S_.dma("sp", dst[hd * 128:(hd + 1) * 128, t0:t0 + n], ob[i][:, 0:n], reads=[b_ob[i]])
                vtv = v_tok.rearrange("(t p) c -> p t c", p=128)
                for hd in range(8):
                    i = hd % 2
                    S_.dma("sp", vrow[i][:], vT[hd * 128:(hd + 1) * 128, :], writes=[b_vrow[i]])
                    for g in range(4):
                        j = g % 2
                        for q in range(8):
                            tl = g * 8 + q
                            S_.op("pe", lambda h, tl=tl, q=q, j=j, i=i: h.transpose(out=pT[j][:, q * 128:(q + 1) * 128],
                                                                                     in_=vrow[i][:, tl * 128:(tl + 1) * 128], identity=IDb),
                                  reads=[b_vrow[i], bC16], writes=[b_pT[j]], sig=(q == 7))
                        S_.op("dve", lambda h, g=g, j=j, i=i: h.tensor_copy(out=vst[i][:, g * 8:(g + 1) * 8, :].rearrange("p a b -> p (a b)"), in_=pT[j][:]),
                              reads=[b_pT[j]], writes=[b_vst[i]])
                    S_.dma("sp", vtv[:, :, hd * 128:(hd + 1) * 128], vst[i][:], reads=[b_vst[i]])
                S_.barrier()
        p6()
        if stop_after <= 6:
            return finish(nc, S_, out_d)

        def p7():
            scale = 128.0 ** -0.5
            with ExitStack() as es:
                ksb = [sb(es, "a7k%d" % i, [128, S], BF16) for i in range(2)]
                vsb = [sb(es, "a7v%d" % i, [128, 32, 128], BF16) for i in range(2)]
                qsb = [sb(es, "a7q%d" % i, [128, 512], BF16) for i in range(2)]
                pt = [sb(es, "a7p%d" % i, [128, 2, 512], BF16) for i in range(3)]
                rec = sb(es, "a7rec", [128, 512], F32)
                lacc = sb(es, "a7lacc", [128, 512], F32)
                osb = [sb(es, "a7o%d" % i, [128, 512], BF16) for i in range(2)]
                pS = [ps(es, "a7pS%d" % i, [128, 2, 512], F32) for i in range(2)]
                pO = [ps(es, "a7pO%d" % i, [128, 512], F32) for i in range(2)]
                pL = [ps(es, "a7pL%d" % i, [128, 512], F32) for i in range(2)]
                b_k, b_v, b_q, b_o, b_pO, b_pL, b_pS = ([Buf(), Buf()] for _ in range(7))
                b_pt = [Buf() for _ in range(3)]
                b_rec, b_lacc = Buf(), Buf()
                vtv = v_tok.rearrange("(t p) c -> p t c", p=128)
                it = 0
                sc = 0
                for kvh in range(8):
                    ki = kvh % 2
                    S_.dma("sp", ksb[ki][:], kTn[kvh * 128:(kvh + 1) * 128, :], writes=[b_k[ki]])
                    S_.dma("sp", vsb[ki][:], vtv[:, :, kvh * 128:(kvh + 1) * 128], writes=[b_v[ki]])
                    for qh in range(4):
                        hd = kvh * 4 + qh
                        for (t0, n) in tok_blocks(0, NOWN):
                            i = it % 2
                            it += 1
                            S_.dma("sp", qsb[i][:, 0:n], qTn[hd * 128:(hd + 1) * 128, t0:t0 + n], writes=[b_q[i]])
                            if it <= 86:
                                S_.dma("pool", wdb[(it - 1) * 128:it * 128, :], w_down[(it - 1) * 128:it * 128, :])

                            def smm(pi, i=i, n=n, ki=ki):
                                js, jp = (sc + pi) % 2, (sc + pi) % 3
                                for e in range(2):
                                    kt = 2 * pi + e
                                    S_.op("pe", lambda h, e=e, kt=kt: h.matmul(pS[js][:, e, 0:n], lhsT=ksb[ki][:, kt * 128:(kt + 1) * 128], rhs=qsb[i][:, 0:n],
                                                                               start=True, stop=True), reads=[b_k[ki], b_q[i]], writes=[b_pS[js]], sig=(e == 1))
                                S_.op("act", lambda h: h.activation(out=pt[jp][:, :, 0:n], in_=pS[js][:, :, 0:n], func=AF.Exp, scale=scale),
                                      reads=[b_pS[js]], writes=[b_pt[jp]])

                            def pv(pi, i=i, n=n, ki=ki):
                                jp = (sc + pi) % 3
                                for e in range(2):
                                    kt = 2 * pi + e
                                    S_.op("pe", lambda h, e=e, kt=kt: h.matmul(pO[i][:, 0:n], lhsT=vsb[ki][:, kt, :], rhs=pt[jp][:, e, 0:n],
                                                                               start=(kt == 0), stop=(kt == 31)), reads=[b_v[ki], b_pt[jp]], writes=[b_pO[i]], sig=(kt == 31))
                                S_.op("pe", lambda h: h.matmul(pL[i][:, 0:n], lhsT=ONEb, rhs=pt[jp][:, 0, 0:n], start=(pi == 0), stop=False),
                                      reads=[bC16, b_pt[jp]], writes=[b_pL[i]], sig=True)
                                if pi == 0:
                                    S_.op("dve", lambda h: h.tensor_copy(out=lacc[:, 0:n], in_=pt[jp][:, 1, 0:n]), reads=[b_pt[jp]], writes=[b_lacc])
                                else:
                                    S_.op("dve", lambda h: h.tensor_tensor(out=lacc[:, 0:n], in0=lacc[:, 0:n], in1=pt[jp][:, 1, 0:n], op=ALU.add),
                                          reads=[b_pt[jp]], writes=[b_lacc])
                            smm(0)
                            smm(1)
                            for pi in range(16):
                                pv(pi)
                                if pi + 2 < 16:
                                    smm(pi + 2)
                            sc += 16
                            S_.op("pe", lambda h, i=i, n=n: h.matmul(pL[i][:, 0:n], lhsT=ONEf, rhs=lacc[:, 0:n], start=False, stop=True),
                                  reads=[bC32, b_lacc], writes=[b_pL[i]])
                            S_.op("dve", lambda h, i=i, n=n: h.reciprocal(out=rec[:, 0:n], in_=pL[i][:, 0:n]), reads=[b_pL[i]], writes=[b_rec])
                            S_.op("dve", lambda h, i=i, n=n: h.tensor_tensor(out=osb[i][:, 0:n], in0=pO[i][:, 0:n], in1=rec[:, 0:n], op=ALU.mult),
                                  reads=[b_pO[i], b_rec], writes=[b_o[i]])
                            S_.dma("sp", yattnT[hd * 128:(hd + 1) * 128, t0:t0 + n], osb[i][:, 0:n], reads=[b_o[i]])
                S_.barrier()
        p7()
        if stop_after <= 7:
            return finish(nc, S_, out_d)

        def p8():
            def mk1(es_):
                gs, ts = Stage(es_, "p8g", F32), Stage(es_, "p8t", F32)

                def evac(es, p, c, t, n, b_p, idx):
                    gl, bg_ = gs.nxt()
                    tl, bt_ = ts.nxt()
                    S_.dma("sp", gl[:, 0:n], gatesT[c:c + 128, t:t + n], writes=[bg_])
                    S_.op("dve", lambda h: h.tensor_tensor(out=tl[:, 0:n], in0=p, in1=gl[:, 0:n], op=ALU.mult), reads=[b_p, bg_], writes=[bt_])
                    S_.dma("sp", t1T[c:c + 128, t:t + n], tl[:, 0:n], reads=[bt_])
                return evac
            mm_A("p8a", yssmT, 64, 0, 1088, w_sp, 0, D, mk1, cw=128)
            mm_A("p8b", yssmT, 64, 1088, 1088, w_sp, 0, D, mk1, cw=128)

            def mk2(es_):
                gs, ts, ms, os_ = Stage(es_, "p8g2", F32), Stage(es_, "p8t2", F32), Stage(es_, "p8m", F32), Stage(es_, "p8o", BF16)

                def evac(es, p, c, t, n, b_p, idx):
                    gl, bg_ = gs.nxt()
                    tl, bt_ = ts.nxt()
                    ml, bm_ = ms.nxt()
                    ol, bo_ = os_.nxt()
                    S_.dma("sp", gl[:, 0:n], gatesT[D + c:D + c + 128, t:t + n], writes=[bg_])
                    S_.dma("sp", tl[:, 0:n], t1T[c:c + 128, t:t + n], writes=[bt_])
                    S_.op("dve", lambda h: h.tensor_tensor(out=ml[:, 0:n], in0=p, in1=gl[:, 0:n], op=ALU.mult), reads=[b_p, bg_], writes=[bm_])
                    S_.op("dve", lambda h: h.tensor_tensor(out=ol[:, 0:n], in0=ml[:, 0:n], in1=tl[:, 0:n], op=ALU.add), reads=[bm_, bt_], writes=[bo_])
                    S_.dma("sp", mixT[c:c + 128, t:t + n], ol[:, 0:n], reads=[bo_])
                return evac
            mm_A("p8c", yattnT, KT, 0, NOWN, w_ap, 0, D, mk2)
        p8()
        if stop_after <= 8:
            return finish(nc, S_, out_d)

        def mm_B(name, act_d, kt_n, tok0, ntok, W_d, ncols, dst_d, dst_t0, cw=256):
            with ExitStack() as es:
                S_.store_q = "sp"
                A = sb(es, name + "_A", [128, kt_n, ntok], BF16)
                Wt = [sb(es, name + "_W%d" % i, [128, kt_n, cw], BF16) for i in range(2)]
                pp = [ps(es, name + "_p%d" % i, [128, 512], F32) for i in range(6)]
                stg = Stage(es, name + "_s", F32, n=4, w=cw)
                b_A, b_W, b_pp = Buf(), [Buf(), Buf()], [Buf() for _ in range(6)]
                av = act_d.rearrange("(k p) t -> p k t", p=128)
                step = 8 if kt_n % 8 == 0 else 2
                for k0 in range(0, kt_n, step):
                    S_.dma("sp", A[:, k0:k0 + step, :], av[:, k0:k0 + step, tok0:tok0 + ntok], writes=[b_A])
                wv = W_d.rearrange("(k p) c -> p k c", p=128)
                cnt = 0
                for wi, c0 in enumerate(range(0, ncols, cw)):
                    i = wi % 2
                    S_.dma("pool", Wt[i][:], wv[:, :, c0:c0 + cw], writes=[b_W[i]])
                    for tt in range(ntok // 128):
                        pi = cnt % 6
                        cnt += 1
                        for kt in range(kt_n):
                            S_.op("pe", lambda h, kt=kt, i=i, tt=tt, pi=pi: h.matmul(
                                pp[pi][:, 0:cw], lhsT=A[:, kt, tt * 128:(tt + 1) * 128], rhs=Wt[i][:, kt, :],
                                start=(kt == 0), stop=(kt == kt_n - 1)),
                                reads=[b_A, b_W[i]], writes=[b_pp[pi]], sig=(kt == kt_n - 1))
                        tl, bt_ = stg.nxt()
                        if cnt % 2 == 0:
                            S_.op("act", lambda h, pi=pi, tl=tl: h.activation(out=tl[:, 0:cw], in_=pp[pi][:, 0:cw], func=AF.Copy), reads=[b_pp[pi]], writes=[bt_])
                        else:
                            S_.op("dve", lambda h, pi=pi, tl=tl: h.tensor_copy(out=tl[:, 0:cw], in_=pp[pi][:, 0:cw]), reads=[b_pp[pi]], writes=[bt_])
                        r0 = dst_t0 + tt * 128
                        S_.dma("sp", dst_d[r0:r0 + 128, c0:c0 + cw], tl[:, 0:cw], reads=[bt_])
                S_.barrier()
                S_.store_q = "pool"

        mm_B("p9", mixT, KT, 0, NOWN, w_out, D, mixed, 0)

        def ln_affine_pass(name, a_d, res_d, gcol0, lng_d, lnb_d, ntiles, dst_d, do_h2):
            with ExitStack() as es:
                gbc = sb(es, name + "gbc", [128, D], F32)
                lg = sb(es, name + "lg", [128, D], F32)
                lb = sb(es, name + "lb", [128, D], F32)
                at = sb(es, name + "at", [128, D], F32)
                rt = [sb(es, name + "rt%d" % i, [128, D], F32) for i in range(2)]
                st = sb(es, name + "st", [128, 8, 6], F32)
                mv = sb(es, name + "mv", [128, 2], F32)
                rstd = sb(es, name + "rstd", [128, 1], F32)
                b_bc, b_at, b_st = Buf(), Buf(), Buf()
                b_rt = [Buf(), Buf()]
                bc = lambda ap: ap.partition_broadcast(128).rearrange("p a b -> p (a b)")
                S_.dma("sp", gbc[:], bc(modD[0:1, gcol0:gcol0 + D]), writes=[b_bc])
                S_.dma("sp", lg[:], bc(lng_d[0:1, :]), writes=[b_bc])
                S_.dma("sp", lb[:], bc(lnb_d[0:1, :]), writes=[b_bc])
                if do_h2:
                    xn = sb(es, name + "xn", [128, D], F32)
                    hblk = sb(es, name + "hblk", [128, KT, 512], BF16)
                    pT = [ps(es, name + "pT%d" % i, [128, 512], F32) for i in range(2)]
                    b_xn, b_hblk = Buf(), Buf()
                    b_pT = [Buf(), Buf()]
                    h2v = h2T.rearrange("(k p) t -> p k t", p=128)
                for tt in range(ntiles):
                    i = tt % 2
                    r0 = tt * 128
                    S_.dma("sp", at[:], a_d[r0:r0 + 128, :], writes=[b_at])
                    S_.dma("sp", rt[i][:], res_d[r0:r0 + 128, :], writes=[b_rt[i]])
                    S_.op("dve", lambda h: h.tensor_tensor(out=at[:], in0=at[:], in1=gbc[:], op=ALU.mult), reads=[b_bc], writes=[b_at])
                    S_.op("dve", lambda h, i=i: h.scalar_tensor_tensor(out=rt[i][:], in0=rt[i][:], scalar=float(ALPHA), in1=at[:], op0=ALU.mult, op1=ALU.add),
                          reads=[b_at], writes=[b_rt[i]])
                    ln_stats(es, rt[i], b_rt[i], st, mv, rstd, b_st)
                    S_.op("dve", lambda h, i=i: h.tensor_scalar(out=rt[i][:], in0=rt[i][:], scalar1=mv[:, 0:1], scalar2=rstd[:, 0:1],
                                                                 op0=ALU.subtract, op1=ALU.mult), reads=[b_st], writes=[b_rt[i]])
                    S_.op("dve", lambda h, i=i: h.tensor_tensor(out=rt[i][:], in0=rt[i][:], in1=lg[:], op=ALU.mult), reads=[b_bc], writes=[b_rt[i]])
                    S_.op("dve", lambda h, i=i: h.tensor_tensor(out=rt[i][:], in0=rt[i][:], in1=lb[:], op=ALU.add), reads=[b_bc], writes=[b_rt[i]])
                    S_.dma("sp", dst_d[r0:r0 + 128, :], rt[i][:], reads=[b_rt[i]])
                    if do_h2:
                        ln_stats(es, rt[i], b_rt[i], st, mv, rstd, b_st)
                        S_.op("dve", lambda h, i=i: h.tensor_scalar(out=xn[:], in0=rt[i][:], scalar1=mv[:, 0:1], scalar2=rstd[:, 0:1],
                                                                     op0=ALU.subtract, op1=ALU.mult), reads=[b_rt[i], b_st], writes=[b_xn])
                        sub = tt % 4
                        modT(xn, b_xn, hblk, b_hblk, sub, pT, b_pT, 128, 96)
                        if sub == 3 or tt == ntiles - 1:
                            t0 = (tt // 4) * 512
                            w = (sub + 1) * 128
                            S_.dma("sp", h2v[:, :, t0:t0 + w], hblk[:, :, 0:w], reads=[b_hblk])
                S_.barrier()
        ln_affine_pass("l1", mixed, x_d, 2 * D, ln1g, ln1b, NOWN // 128, x1d, True)
        if stop_after <= 9:
            return finish(nc, S_, out_d)

        def p10():
            def mk(es_):
                st_ = Stage(es_, "p10s", F32)

                def evac(es, p, c, t, n, b_p, idx):
                    tl, bt_ = st_.nxt()
                    if idx % 2 == 0:
                        S_.op("act", lambda h: h.activation(out=tl[:, 0:n], in_=p, func=AF.Copy), reads=[b_p], writes=[bt_])
                    else:
                        S_.op("dve", lambda h: h.tensor_copy(out=tl[:, 0:n], in_=p), reads=[b_p], writes=[bt_])
                    S_.dma("sp", upre[c:c + 128, t:t + n], tl[:, 0:n], reads=[bt_])
                return evac
            mm_A("p10", h2T, KT, 0, NOWN, w_up, 0, 2 * FFN, mk)
        p10()

        def p11():
            NW = NOUT + 2
            with ExitStack() as es:
                xin = [sb(es, "fxin%d" % i, [128, NW], F32) for i in range(4)]
                acc = [sb(es, "facc%d" % i, [128, NOUT], F32) for i in range(4)]
                sa = sb(es, "fsa", [128, NOUT], F32)
                ob = [sb(es, "fob%d" % i, [128, NOUT], BF16) for i in range(2)]
                cw = sb(es, "fcw", [128, 172, 3], F32)
                cbs = sb(es, "fcb", [128, 172], F32)
                b_xin = [Buf() for _ in range(4)]
                b_acc = [Buf() for _ in range(4)]
                b_ob = [Buf(), Buf()]
                b_cw, b_sa = Buf(), Buf()
                S_.dma("sp", cw[:], fcw[:, :, :], writes=[b_cw])
                S_.dma("sp", cbs[:], fcb[:, :], writes=[b_cw])
                for i in range(4):
                    S_.op("dve", lambda h, i=i: h.memset(xin[i][:, 0:1], 0.0), writes=[b_xin[i]])

                def front(blk):
                    for half in range(2):
                        ci = half * 86 + blk
                        i = (blk % 2) * 2 + half
                        S_.dma("sp", xin[i][:, 1:NW], upre[ci * 128:(ci + 1) * 128, 0:NOUT + 1], writes=[b_xin[i]])
                        S_.op("act", lambda h, i=i, ci=ci: h.activation(out=acc[i][:], in_=xin[i][:, 1:NOUT + 1], func=AF.Identity,
                                                                        scale=cw[:, ci, 1:2], bias=cbs[:, ci:ci + 1]),
                              reads=[b_xin[i], b_cw], writes=[b_acc[i]])

                def back(blk):
                    for half in range(2):
                        ci = half * 86 + blk
                        i = (blk % 2) * 2 + half
                        S_.op("dve", lambda h, i=i, ci=ci: h.scalar_tensor_tensor(out=acc[i][:], in0=xin[i][:, 0:NOUT], scalar=cw[:, ci, 0:1],
                                                                                   in1=acc[i][:], op0=ALU.mult, op1=ALU.add),
                              reads=[b_xin[i], b_cw], writes=[b_acc[i]])
                        S_.op("dve", lambda h, i=i, ci=ci: h.scalar_tensor_tensor(out=acc[i][:], in0=xin[i][:, 2:NOUT + 2], scalar=cw[:, ci, 2:3],
                                                                                   in1=acc[i][:], op0=ALU.mult, op1=ALU.add),
                              reads=[b_xin[i], b_cw], writes=[b_acc[i]])
                    j = blk % 2
                    ia, ib = (blk % 2) * 2, (blk % 2) * 2 + 1
                    S_.op("act", lambda h: h.activation(out=sa[:], in_=acc[ia][:], func=AF.Silu), reads=[b_acc[ia]], writes=[b_sa])
                    S_.op("dve", lambda h: h.tensor_tensor(out=ob[j][:], in0=sa[:], in1=acc[ib][:], op=ALU.mult), reads=[b_sa, b_acc[ib]], writes=[b_ob[j]])
                    S_.dma("sp", actT[blk * 128:(blk + 1) * 128, :], ob[j][:], reads=[b_ob[j]])
                front(0)
                for blk in range(86):
                    if blk + 1 < 86:
                        front(blk + 1)
                    back(blk)
                S_.barrier()
        p11()
        if stop_after <= 11:
            return finish(nc, S_, out_d)

        for sbk in range(4):
            mm_B("p12_%d" % sbk, actT, 86, sbk * 512, 512, wdb, D, fd, sbk * 512)
        ln_affine_pass("l2", fd, x1d, 5 * D, ln2g, ln2b, NOUT // 128, out_d, False)

        return finish(nc, S_, out_d)


def finish(nc, S_, out_d):
    S_.barrier()
    return nc


def _consts():
    c = np.zeros((128, 8, 128), np.float32)
    i = np.arange(128)
    c[:, 0, :] = np.eye(128, dtype=np.float32)
    c[:, 1, :] = 1.0
    c[:, 2, :] = (i[:, None] <= i[None, :])
    c[:, 3, :] = (i[:, None] > i[None, :])
    c[:, 4, :] = (i[:, None] >= i[None, :])
    c[:, 5, :] = (i[:, None] < i[None, :])
    P = np.zeros((128, 128), np.float32)
    for base in (0, 64):
        for j in range(32):
            P[base + j, base + 32 + j] = -1.0
            P[base + 32 + j, base + j] = 1.0
    c[:, 6, :] = P.T
    return c


def _rope_tables(flip):
    t = np.arange(S)
    if flip:
        t = t[::-1]
    row = (t // 64).astype(np.float32)
    colp = (t % 64).astype(np.float32)
    inv = (1.0 / (np.float32(10000.0) ** (np.arange(32, dtype=np.float32) / np.float32(32)))).astype(np.float32)
    cos = np.zeros((128, S), np.float32)
    sin = np.zeros((128, S), np.float32)
    for d in range(128):
        pos = row if d < 64 else colp
        ang = (pos * inv[d % 32]).astype(np.float32)
        cos[d] = np.cos(ang)
        sin[d] = np.sin(ang)
    return cos, sin


def _pp(v, nb):
    return np.ascontiguousarray(np.asarray(v, np.float32).reshape(nb, 128).T)


def make_in_maps(inputs, cores):
    f = lambda k: np.asarray(inputs[k], np.float32)
    w_in0 = np.ascontiguousarray(f("w_in")[0])
    w_in1 = w_in0.copy()
    w_in1[:, DT0:DT0 + 128] = w_in0[:, DT0 + 128:DT0 + 256]
    w_in1[:, DT0 + 128:DT0 + 256] = w_in0[:, DT0:DT0 + 128]
    consts = _consts()
    ropes = [_rope_tables(0), _rope_tables(1)]
    maps = []
    for core in cores:
        b, hf = core // 2, core % 2
        xl = f("x")[b]
        if hf:
            xl = xl[::-1]
        taps = [2, 1, 0] if hf else [0, 1, 2]
        dirs = [1, 0] if hf else [0, 1]
        scw = f("ssm_conv_w")[0][taps]
        fcw = f("ffn_conv_w")[0][taps]
        m = {
            "x": np.ascontiguousarray(xl),
            "c": _pp(f("c")[b], 32),
            "w_ada": f("w_ada")[0], "b_ada": f("b_ada")[0].reshape(1, -1),
            "w_in": w_in1 if hf else w_in0,
            "ssm_conv_w": np.ascontiguousarray(scw.reshape(3, 80, 128).transpose(2, 1, 0)),
            "ssm_conv_b": _pp(f("ssm_conv_b")[0], 80),
            "ssm_a_log": np.ascontiguousarray(f("ssm_a_log")[0][dirs].T),
            "ssm_dt_bias": np.ascontiguousarray(f("ssm_dt_bias")[0][dirs].T),
            "ssm_d": f("ssm_d")[0].reshape(1, 128),
            "ssm_norm_w": _pp(f("ssm_norm_w")[0], 64),
            "q_norm_w": f("q_norm_w")[0].reshape(128, 1), "k_norm_w": f("k_norm_w")[0].reshape(128, 1),
            "w_ssm_proj": f("w_ssm_proj")[0], "w_attn_proj": f("w_attn_proj")[0],
            "w_gate": f("w_gate")[0], "b_gate": _pp(f("b_gate")[0], 64),
            "w_out": f("w_out")[0], "ln1_g": f("ln1_g")[0].reshape(1, -1), "ln1_b": f("ln1_b")[0].reshape(1, -1),
            "w_up": f("w_up")[0],
            "ffn_conv_w": np.ascontiguousarray(fcw.reshape(3, 172, 128).transpose(2, 1, 0)),
            "ffn_conv_b": _pp(f("ffn_conv_b")[0], 172),
            "w_down": f("w_down")[0], "ln2_g": f("ln2_g")[0].reshape(1, -1), "ln2_b": f("ln2_b")[0].reshape(1, -1),
            "rope_cos": ropes[hf][0], "rope_sin": ropes[hf][1],
            "consts": consts,
        }
        maps.append(m)
    return maps


def kernel(**inputs):
    nc = build_nc()
    cores = list(range(8))
    maps = make_in_maps(inputs, cores)
    res = run_bass_kernel_spmd(nc, maps, core_ids=cores)
    out = np.zeros((4, S, D), np.float32)
    for core in cores:
        b, hf = core // 2, core % 2
        o = np.asarray(res.results[core]["out"])
        if hf:
            out[b, NOUT:] = o[::-1]
        else:
            out[b, :NOUT] = o
    return out
```
